# Optimizing a Trainium2 kernel written in Bass

```python
import jax, jax.numpy as jnp
from jax import lax
import numpy as np

D_MODEL = 4096
BATCH = 4
SEQ = 2048
DEPTH = 4
DEC_BATCH = 8
DEC_SEQ = 1
PAST_LEN = 8192
PAGE_SIZE = 128

N_A_LAYERS = DEPTH // 2
N_B_LAYERS = DEPTH - N_A_LAYERS
D_RNN = D_MODEL
LRU_HEADS = 16
LRU_BLOCK = D_RNN // LRU_HEADS
CONV_W = 4
LRU_C = 8.0
HEAD_DIM = 128
N_SLOTS = 16
KV_HEADS = 4
GQA = N_SLOTS // KV_HEADS
WINDOWS = (128, 512, 2048)
DILATIONS = (1, 4, 16)
N_GROUPS = 3
BAND = 128
ATTN_W = N_SLOTS * HEAD_DIM
Q_W = N_GROUPS * ATTN_W
KV_W = N_GROUPS * 2 * KV_HEADS * HEAD_DIM
N_BUCKETS = 32
MAX_EXACT = 16
MAX_DIST = 2048
EPS = 1e-6

kernel_name = "yoco_rglru_dilated_swa_decoder_step"


def rmsnorm(x, g):
    xf = x.astype(jnp.float32)
    y = xf * lax.rsqrt(jnp.mean(xf * xf, axis=-1, keepdims=True) + EPS)
    return (y * g.astype(jnp.float32)).astype(x.dtype)


def rel_bucket(dist):
    dist = np.asarray(dist)
    d = np.maximum(dist, 1).astype(np.float32)
    large = MAX_EXACT + (np.log(d / MAX_EXACT) / np.log(MAX_DIST / MAX_EXACT)
                         * (N_BUCKETS - MAX_EXACT)).astype(np.int32)
    large = np.minimum(large, N_BUCKETS - 1)
    return np.where(dist < MAX_EXACT, dist, large).astype(np.int32)


def _lru_combine(left, right):
    a1, b1 = left
    a2, b2 = right
    return a1 * a2, a2 * b1 + b2


def rglru_mixer(u, conv_buf, h0, w_in, conv_w, conv_b, w_ga, b_ga, w_gx, b_gx, lam, w_out):
    n, t, _ = u.shape
    f32 = jnp.float32
    proj = u @ w_in
    xb, gate = proj[..., :D_RNN], proj[..., D_RNN:]
    xc = jnp.concatenate([conv_buf.astype(xb.dtype), xb], axis=1)
    xconv = conv_b
    for k in range(CONV_W):
        xconv = xconv + xc[:, k:k + t] * conv_w[k]
    new_conv = xc[:, xc.shape[1] - (CONV_W - 1):]
    xf = xconv.astype(f32)
    xh = xf.reshape(n, t, LRU_HEADS, LRU_BLOCK)
    r = jax.nn.sigmoid(jnp.einsum('nthi,hij->nthj', xh, w_ga.astype(f32)) + b_ga.astype(f32)).reshape(n, t, D_RNN)
    i = jax.nn.sigmoid(jnp.einsum('nthi,hij->nthj', xh, w_gx.astype(f32)) + b_gx.astype(f32)).reshape(n, t, D_RNN)
    log_a = -LRU_C * jax.nn.softplus(-lam.astype(f32)) * r
    a = jnp.exp(log_a)
    b = jnp.sqrt(-jnp.expm1(2.0 * log_a)) * (i * xf)
    b = b.at[:, 0].add(a[:, 0] * h0.astype(f32))
    _, h = lax.associative_scan(_lru_combine, (a, b), axis=1)
    y = (h.astype(u.dtype) * jax.nn.silu(gate)) @ w_out
    return y, new_conv, h[:, -1]


def dilated_prompt(q, k, v, r, table_g):
    n, t = q.shape[:2]
    f32 = jnp.float32
    win = r * BAND
    t_pad = -(-t // win) * win
    j = t_pad // r
    nb = j // BAND

    def to_band(a):
        h = a.shape[2]
        a = jnp.pad(a, ((0, 0), (0, t_pad - t), (0, 0), (0, 0))).reshape(n, j, r, h, HEAD_DIM)
        return a.transpose(0, 2, 1, 3, 4).reshape(n, r, nb, BAND, h, HEAD_DIM)

    def with_prev(a):
        prev = jnp.pad(a, ((0, 0), (0, 0), (1, 0), (0, 0), (0, 0), (0, 0)))[:, :, :-1]
        return jnp.concatenate([prev, a], axis=3)

    qb = to_band(q.astype(f32)).reshape(n, r, nb, BAND, KV_HEADS, GQA, HEAD_DIM)
    kb = with_prev(to_band(k.astype(f32)))
    vb = with_prev(to_band(v.astype(f32)))
    s = jnp.einsum('bcnqkgd,bcnpkd->bcnkgqp', qb, kb) * (HEAD_DIM ** -0.5)
    qq = np.arange(BAND)[:, None]
    kk = np.arange(2 * BAND)[None, :]
    diff = BAND + qq - kk
    in_band = (diff >= 0) & (diff <= BAND)
    bucket = rel_bucket(np.clip(diff, 0, BAND) * r)
    bias = table_g[bucket].transpose(2, 0, 1).reshape(KV_HEADS, GQA, BAND, 2 * BAND)
    mask = in_band[None] & ((np.arange(nb)[:, None, None] > 0) | (kk[None] >= BAND))
    s = jnp.where(mask[:, None, None], s + bias, -jnp.inf)
    m = jnp.max(s, axis=-1)
    p = jnp.exp(s - m[..., None])
    den = jnp.sum(p, axis=-1)
    o = jnp.einsum('bcnkgqp,bcnpkd->bcnqkgd', p, vb)
    o = o.reshape(n, r, j, N_SLOTS, HEAD_DIM).transpose(0, 2, 1, 3, 4).reshape(n, t_pad, N_SLOTS, HEAD_DIM)[:, :t]

    def stat(a):
        a = a.transpose(0, 1, 2, 5, 3, 4).reshape(n, r, j, N_SLOTS)
        return a.transpose(0, 2, 1, 3).reshape(n, t_pad, N_SLOTS)[:, :t]

    return o, stat(m), stat(den)


def dilated_sample(q, k_ext, v_ext, r, table_g):
    n, s_len = q.shape[:2]
    f32 = jnp.float32
    lb = k_ext.shape[1] - s_len
    steps = np.arange(BAND + 1)
    idx = lb + np.arange(s_len)[:, None] - steps[None, :] * r
    valid = idx >= 0
    idx = np.maximum(idx, 0)
    kg = k_ext[:, idx].astype(f32)
    vg = v_ext[:, idx].astype(f32)
    qh = q.astype(f32).reshape(n, s_len, KV_HEADS, GQA, HEAD_DIM)
    s = jnp.einsum('nskgd,nspkd->nkgsp', qh, kg) * (HEAD_DIM ** -0.5)
    bias = table_g[rel_bucket(steps * r)].T.reshape(KV_HEADS, GQA, 1, BAND + 1)
    s = jnp.where(valid[None, None, None], s + bias, -jnp.inf)
    m = jnp.max(s, axis=-1)
    p = jnp.exp(s - m[..., None])
    den = jnp.sum(p, axis=-1)
    o = jnp.einsum('nkgsp,nspkd->nskgd', p, vg).reshape(n, s_len, N_SLOTS, HEAD_DIM)
    m = m.transpose(0, 3, 1, 2).reshape(n, s_len, N_SLOTS)
    den = den.transpose(0, 3, 1, 2).reshape(n, s_len, N_SLOTS)
    return o, m, den


def combine_groups(parts):
    ms = jnp.stack([p[1] for p in parts])
    mx = jnp.max(ms, axis=0)
    e = jnp.exp(ms - mx[None])
    num = sum(e[g][..., None] * parts[g][0] for g in range(len(parts)))
    den = sum(e[g] * parts[g][2] for g in range(len(parts)))
    return num / den[..., None]


def trunk(x, conv_state, h_state, kv_bufs, a_pre_g, a_w_in, a_conv_w, a_conv_b, a_w_gate_a, a_b_gate_a,
          a_w_gate_x, a_b_gate_x, a_lambda, a_w_out, a_post_g, kv_norm_g, w_kv, rel_bias,
          b_pre_g, b_w_in, b_w_out, b_post_g):
    n, t, _ = x.shape
    new_conv, new_h, new_kv, kv_groups = [], [], [], []
    for layer in range(DEPTH):
        if layer < N_A_LAYERS:
            l = layer
            y, c, h = rglru_mixer(rmsnorm(x, a_pre_g[l]), conv_state[l], h_state[l], a_w_in[l], a_conv_w[l],
                                  a_conv_b[l], a_w_gate_a[l], a_b_gate_a[l], a_w_gate_x[l], a_b_gate_x[l],
                                  a_lambda[l], a_w_out[l])
            x = x + rmsnorm(y, a_post_g[l])
            new_conv.append(c)
            new_h.append(h)
            continue
        if layer == N_A_LAYERS:
            kv = (rmsnorm(x, kv_norm_g) @ w_kv).reshape(n, t, N_GROUPS, 2, KV_HEADS, HEAD_DIM)
            for g in range(N_GROUPS):
                kvg = kv[:, :, g]
                if kv_bufs is None:
                    keep = min(WINDOWS[g], t)
                    new_kv.append(kvg[:, t - keep:])
                    kv_groups.append(kvg)
                else:
                    lb = kv_bufs[g].shape[1]
                    ext = jnp.concatenate([kv_bufs[g].astype(kvg.dtype), kvg], axis=1)
                    new_kv.append(ext[:, ext.shape[1] - lb:])
                    kv_groups.append(ext)
        l = layer - N_A_LAYERS
        proj = rmsnorm(x, b_pre_g[l]) @ b_w_in[l]
        q = proj[..., :Q_W].reshape(n, t, N_GROUPS, N_SLOTS, HEAD_DIM)
        gate = proj[..., Q_W:]
        parts = []
        for g in range(N_GROUPS):
            table_g = rel_bias[:, g * N_SLOTS:(g + 1) * N_SLOTS].astype(jnp.float32)
            kvg = kv_groups[g]
            if kv_bufs is None:
                parts.append(dilated_prompt(q[:, :, g], kvg[:, :, 0], kvg[:, :, 1], DILATIONS[g], table_g))
            else:
                parts.append(dilated_sample(q[:, :, g], kvg[:, :, 0], kvg[:, :, 1], DILATIONS[g], table_g))
        o = combine_groups(parts)
        y = (o.reshape(n, t, ATTN_W).astype(x.dtype) * jax.nn.silu(gate)) @ b_w_out[l]
        x = x + rmsnorm(y, b_post_g[l])
    return x, jnp.stack(new_conv), jnp.stack(new_h), new_kv[0], new_kv[1], new_kv[2]


def setup_inputs(seed: int = 0) -> dict:
    key = jax.random.key(seed)
    ks = jax.random.split(key, 32)
    f32 = jnp.float32
    nrm = lambda k, shape, s=1.0: (jax.random.normal(k, shape, f32) * s)
    lb = [min(w, PAST_LEN) for w in WINDOWS]
    lam_u = jax.random.uniform(ks[10], (N_A_LAYERS, D_RNN), f32, minval=0.9, maxval=0.999)
    sig = lam_u ** (1.0 / LRU_C)
    return {
        "x_prompt": nrm(ks[0], (BATCH, SEQ, D_MODEL)),
        "x_sample": nrm(ks[1], (DEC_BATCH, DEC_SEQ, D_MODEL)),
        "state_conv": nrm(ks[2], (N_A_LAYERS, DEC_BATCH, CONV_W - 1, D_RNN)),
        "state_h": nrm(ks[3], (N_A_LAYERS, DEC_BATCH, D_RNN), 0.5),
        "state_kv_w128": nrm(ks[4], (DEC_BATCH, lb[0], 2, KV_HEADS, HEAD_DIM)),
        "state_kv_w512": nrm(ks[5], (DEC_BATCH, lb[1], 2, KV_HEADS, HEAD_DIM)),
        "state_kv_w2048": nrm(ks[6], (DEC_BATCH, lb[2], 2, KV_HEADS, HEAD_DIM)),
        "a_pre_g": 1.0 + nrm(ks[7], (N_A_LAYERS, D_MODEL), 0.01),
        "a_w_in": nrm(ks[8], (N_A_LAYERS, D_MODEL, 2 * D_RNN), D_MODEL ** -0.5),
        "a_conv_w": nrm(ks[9], (N_A_LAYERS, CONV_W, D_RNN), CONV_W ** -0.5),
        "a_conv_b": nrm(ks[11], (N_A_LAYERS, D_RNN), 0.01),
        "a_w_gate_a": nrm(ks[12], (N_A_LAYERS, LRU_HEADS, LRU_BLOCK, LRU_BLOCK), LRU_BLOCK ** -0.5),
        "a_b_gate_a": nrm(ks[13], (N_A_LAYERS, LRU_HEADS, LRU_BLOCK), 0.01),
        "a_w_gate_x": nrm(ks[14], (N_A_LAYERS, LRU_HEADS, LRU_BLOCK, LRU_BLOCK), LRU_BLOCK ** -0.5),
        "a_b_gate_x": nrm(ks[15], (N_A_LAYERS, LRU_HEADS, LRU_BLOCK), 0.01),
        "a_lambda": jnp.log(sig) - jnp.log1p(-sig),
        "a_w_out": nrm(ks[16], (N_A_LAYERS, D_RNN, D_MODEL), D_RNN ** -0.5),
        "a_post_g": 1.0 + nrm(ks[17], (N_A_LAYERS, D_MODEL), 0.01),
        "kv_norm_g": 1.0 + nrm(ks[18], (D_MODEL,), 0.01),
        "w_kv": nrm(ks[19], (D_MODEL, KV_W), D_MODEL ** -0.5),
        "rel_bias": nrm(ks[20], (N_BUCKETS, N_GROUPS * N_SLOTS), 0.2),
        "b_pre_g": 1.0 + nrm(ks[21], (N_B_LAYERS, D_MODEL), 0.01),
        "b_w_in": nrm(ks[22], (N_B_LAYERS, D_MODEL, Q_W + ATTN_W), D_MODEL ** -0.5),
        "b_w_out": nrm(ks[23], (N_B_LAYERS, ATTN_W, D_MODEL), ATTN_W ** -0.5),
        "b_post_g": 1.0 + nrm(ks[24], (N_B_LAYERS, D_MODEL), 0.01),
    }


def reference(x_prompt, x_sample, state_conv, state_h, state_kv_w128, state_kv_w512, state_kv_w2048,
              a_pre_g, a_w_in, a_conv_w, a_conv_b, a_w_gate_a, a_b_gate_a, a_w_gate_x, a_b_gate_x,
              a_lambda, a_w_out, a_post_g, kv_norm_g, w_kv, rel_bias, b_pre_g, b_w_in, b_w_out, b_post_g):
    nb_p = x_prompt.shape[0]
    conv0 = jnp.zeros((N_A_LAYERS, nb_p, CONV_W - 1, D_RNN), x_prompt.dtype)
    h0 = jnp.zeros((N_A_LAYERS, nb_p, D_RNN), jnp.float32)
    y_prompt, p_conv, p_h, p_kv128, p_kv512, p_kv2048 = trunk(
        x_prompt, conv0, h0, None, a_pre_g, a_w_in, a_conv_w, a_conv_b, a_w_gate_a, a_b_gate_a,
        a_w_gate_x, a_b_gate_x, a_lambda, a_w_out, a_post_g, kv_norm_g, w_kv, rel_bias,
        b_pre_g, b_w_in, b_w_out, b_post_g)
    y_sample, s_conv, s_h, s_kv128, s_kv512, s_kv2048 = trunk(
        x_sample, state_conv, state_h, (state_kv_w128, state_kv_w512, state_kv_w2048), a_pre_g, a_w_in,
        a_conv_w, a_conv_b, a_w_gate_a, a_b_gate_a, a_w_gate_x, a_b_gate_x, a_lambda, a_w_out, a_post_g,
        kv_norm_g, w_kv, rel_bias, b_pre_g, b_w_in, b_w_out, b_post_g)
    return (y_prompt, y_sample, p_conv, p_h, p_kv128, p_kv512, p_kv2048, s_conv, s_h, s_kv128, s_kv512, s_kv2048)
```

```python
import contextlib
import numpy as np
import concourse.bass as bass
import concourse.mybir as mybir
from concourse.bass_utils import run_bass_kernel_spmd

F32 = mybir.dt.float32
BF16 = mybir.dt.bfloat16
AF = mybir.ActivationFunctionType
ALU = mybir.AluOpType

SAME_ENGINE_SYNC = True
T = 256
NB = 2048 // T
EPS = 1e-6
SCALE = 128 ** -0.5
DIL = (1, 4, 16)


class Buf:
    __slots__ = ("name", "w", "r")

    def __init__(self, name=""):
        self.name = name
        self.w = None
        self.r = {}


class Prog:
    ENGS = ("pe", "act", "dve", "pool", "sp")

    def __init__(self, nc, stack):
        self.nc = nc
        self.stack = stack
        self.q = {e: [] for e in self.ENGS}
        self.cnt = {e: 0 for e in self.ENGS}
        self.seen = {e: {} for e in self.ENGS}
        self.esem = {e: stack.enter_context(nc.semaphore("es_" + e)) for e in self.ENGS}
        self.dcnt = {}

    def dma_sem(self, name):
        s = self.stack.enter_context(self.nc.semaphore(name))
        self.dcnt[id(s)] = 0
        return s

    def sb(self, name, shape, dt):
        return self.stack.enter_context(self.nc.sbuf_tensor("sb_" + name, list(shape), dt))

    def ps(self, name, shape, dt=F32):
        return self.stack.enter_context(self.nc.psum_tensor("psum_" + name, list(shape), dt))

    def _deps(self, eng, reads, writes):
        deps = []
        for b in reads:
            if b.w is not None:
                deps.append(b.w)
        for b in writes:
            if b.w is not None:
                deps.append(b.w)
            deps.extend(b.r.values())
        waits = []
        seen = self.seen[eng]
        for (sem, val, src) in deps:
            if src == eng and (eng == "pe" or not SAME_ENGINE_SYNC):
                continue
            k = id(sem)
            if seen.get(k, 0) >= val:
                continue
            seen[k] = val
            waits.append((sem, val))
        return waits

    def _commit(self, tok, reads, writes):
        for b in writes:
            b.w = tok
            b.r = {}
        for b in reads:
            if b not in writes:
                b.r[id(tok[0])] = tok

    def op(self, eng, fn, reads=(), writes=(), inc=True):
        waits = self._deps(eng, reads, writes)
        if inc:
            self.cnt[eng] += 1
            tok = (self.esem[eng], self.cnt[eng], eng)
        else:
            tok = (self.esem[eng], self.cnt[eng] + 1, eng)
        self.q[eng].append((fn, waits, (self.esem[eng], 1) if inc else None))
        self._commit(tok, reads, writes)
        return tok

    def dma(self, eng, sem, fn, reads=(), writes=()):
        waits = self._deps(eng, reads, writes)
        self.dcnt[id(sem)] += 16
        tok = (sem, self.dcnt[id(sem)], "dma")
        self.q[eng].append((fn, waits, (sem, 16)))
        self._commit(tok, reads, writes)
        return tok

    def wait_all(self, eng, bufs):
        waits = self._deps(eng, (), bufs)
        self.q[eng].append((None, waits, None))

    def replay(self, block):
        names = {"pe": "tensor", "act": "scalar", "dve": "vector", "pool": "gpsimd", "sp": "sync"}

        def run(e, engobj):
            for fn, waits, inc in self.q[e]:
                for sem, val in waits:
                    engobj.wait_ge(sem, val)
                if fn is None:
                    continue
                ins = fn(engobj)
                if inc is not None:
                    ins.then_inc(inc[0], inc[1])

        for e in self.ENGS:
            if not self.q[e]:
                continue

            def mk(e):
                def body(engobj):
                    run(e, engobj)
                return body
            getattr(block, names[e])(mk(e))


V_APRE, V_CW, V_CB, V_BGA, V_BGX, V_LAM, V_APOST, V_KVG, V_BPRE, V_BPOST = 0, 64, 320, 384, 448, 512, 576, 640, 672, 736
NV = 800
E0 = 0
E1 = 4096
E2 = 8192
ES = 10240
NE = 10240 + 48


def e0_off(pc, kh): return E0 + (pc * 4 + kh) * 512
def e1_off(half, pc, kh): return E1 + ((half * 2 + pc) * 4 + kh) * 256
def e2_off(b, kh): return E2 + (b * 4 + kh) * 64
def es_off(g, kh): return ES + (g * 4 + kh) * 4


def build():
    nc = bass.Bass("TRN2", target_bir_lowering=False)

    def D(name, shape, dt=F32, kind="ExternalInput"):
        return nc.dram_tensor(name, list(shape), dt, kind=kind).ap()

    xp = D("xp", [2048, 4096]); xs = D("xs", [1, 4096])
    sconv = D("sconv", [2, 3, 4096]); sh = D("sh", [2, 4096])
    skv = [D("skv0", [128, 1024]), D("skv1", [512, 1024]), D("skv2", [2048, 1024])]
    a_w_in = D("a_w_in", [2, 4096, 8192]); a_w_out = D("a_w_out", [2, 4096, 4096])
    a_wga = D("a_wga", [2, 16, 256, 256]); a_wgx = D("a_wgx", [2, 16, 256, 256])
    w_kv = D("w_kv", [4096, 3072])
    b_w_in = D("b_w_in", [2, 4096, 8192]); b_w_out = D("b_w_out", [2, 2048, 4096])
    vecs_d = D("vecs", [128, NV]); ein_d = D("ein", [128, NE]); em_d = D("em", [128, NE])
    enew_d = D("enew", [1, 48]); ident_d = D("ident", [128, 128])
    O = lambda n, s: D(n, s, kind="ExternalOutput")
    yp = O("yp", [2048, 4096]); ys = O("ys", [1, 4096])
    pconv = O("pconv", [2, 3, 4096]); ph = O("ph", [2, 4096]); pkv = O("pkv", [2048, 3072])
    sconv_o = O("sconv_o", [2, 3, 4096]); sh_o = O("sh_o", [2, 4096])
    skv_o = [O("skv_o0", [128, 1024]), O("skv_o1", [512, 1024]), O("skv_o2", [2048, 1024])]
    kvs = nc.dram_tensor("kvs", [2048 + 16, 3072], F32).ap()

    with contextlib.ExitStack() as st:
        P = Prog(nc, st)
        xT = P.sb("xT", [128, 32, T], F32); Bx = [Buf() for _ in range(32)]
        ub = P.sb("ub", [128, 32, T], BF16); Bu = [Buf() for _ in range(32)]
        hg = P.sb("hg", [128, 32, T], BF16); Bh = [Buf() for _ in range(32)]
        NW = 3
        wbf = [P.sb("wbf%d" % i, [128, 32, 128], BF16) for i in range(NW)]; Bw = [Buf() for _ in range(NW)]
        wsem = [P.dma_sem("wsem%d" % i) for i in range(NW)]
        wg = P.sb("wg", [128, 2, 2, 256], BF16); Bwg = Buf(); wgsem = P.dma_sem("wgsem")
        vecs = P.sb("vecs", [128, NV], F32); Bv = Buf()
        c1 = P.sb("c1", [128, 64], F32); Bc1 = Buf()
        Eall = P.sb("Eall", [128, NE], BF16); BE = Buf()
        enew = P.sb("enew", [1, 48], F32); Benew = Buf()
        ident = P.sb("ident", [128, 128], F32); Bid = Buf()
        ones = P.sb("ones", [128, 128], BF16); Bones = Buf()
        rs = P.sb("rs", [128, T], F32); Brs = Buf()
        sqb = [P.sb("sqb%d" % i, [128, T], BF16) for i in range(2)]; Bsq = [Buf(), Buf()]
        stt = [P.sb("st%d" % l, [128, 4, 32], F32) for l in range(2)]; Bst = [Buf(), Buf()]
        stT = P.sb("stT", [128, 128], F32); BstT = Buf()
        xc = [P.sb("xc%d" % i, [128, 3 + T], F32) for i in range(2)]; Bxc = [Buf(), Buf()]
        xv = [P.sb("xv%d" % i, [128, T], F32) for i in range(2)]; Bxv = [Buf(), Buf()]
        xvb = [P.sb("xvb%d" % i, [128, T], BF16) for i in range(2)]; Bxvb = [Buf(), Buf()]
        rg = [[P.sb("rg%d%d" % (a, j), [128, T], F32) for j in range(2)] for a in range(2)]
        Brg = [[Buf(), Buf()], [Buf(), Buf()]]
        ta = P.sb("ta", [128, T], F32); Bta = Buf()
        tb = P.sb("tb", [128, T], F32); Btb = Buf()
        tcc = P.sb("tcc", [128, T], F32); Btc = Buf()
        hh = [P.sb("hh%d" % i, [128, T], F32) for i in range(2)]; Bhh = [Buf(), Buf()]
        sg = P.sb("sg", [128, T], F32); Bsg = Buf()
        tmpx = P.sb("tmpx", [128, T], F32); Btmpx = Buf()
        xio = P.sb("xio", [128, 4096], F32); Bxio = Buf(); xiosem = P.dma_sem("xiosem")
        kvt = [P.sb("kvt%d" % i, [128, 3072], F32) for i in range(2)]; Bkvt = [Buf(), Buf()]
        kvosem = P.dma_sem("kvosem")
        q4 = P.sb("q4", [128, 4, T], BF16); Bq4 = Buf()
        num = P.sb("num", [128, 4, T], F32); Bnum = Buf()
        den = P.sb("den", [128, 4, T], F32); Bden = Buf()
        kvin = [P.sb("kvin%d" % i, [128, 2, 128], F32) for i in range(2)]; Bkvin = [Buf(), Buf()]
        kvinsem = [P.dma_sem("kvinsem%d" % i) for i in range(2)]
        kTb = [P.sb("kTb%d" % i, [128, 128], BF16) for i in range(2)]; BkT = [Buf(), Buf()]
        vbb = [P.sb("vbb%d" % i, [128, 128], BF16) for i in range(2)]; Bvb = [Buf(), Buf()]
        pT = [P.sb("pT%d" % i, [128, 512], BF16) for i in range(2)]; BpT = [Buf(), Buf()]
        esb = P.sb("esb", [128, 1024], F32); Besb = Buf()
        emb = P.sb("emb", [128, 1024], F32); Bemb = Buf()
        misc = P.sb("misc", [128, 64], F32); Bmisc = Buf()
        knew = P.sb("knew", [128, 16], BF16); Bknew = Buf()
        vnew = P.sb("vnew", [1, 128], BF16); Bvnew = Buf()
        pnew = P.sb("pnew", [1, 16], BF16); Bpnew = Buf()
        ps = P.ps("ps", [128, 8, 512], F32); Bps = [Buf() for _ in range(8)]
        setsem = [P.dma_sem("setsem%d" % i) for i in range(6)]
        outsem = P.dma_sem("outsem"); Bout = Buf()
        d2dsem = P.dma_sem("d2dsem"); Bd2d = Buf()

        P.dma("sp", setsem[0], lambda e: e.dma_start(out=vecs[:, :], in_=vecs_d), writes=[Bv])
        P.dma("sp", setsem[1], lambda e: e.dma_start(out=enew[:, :], in_=enew_d), writes=[Benew])
        P.dma("sp", setsem[5], lambda e: e.dma_start(out=ident[:, :], in_=ident_d), writes=[Bid])
        P.op("pool", lambda e: e.memset(ones[:, :], 1.0), writes=[Bones])
        for i in range(2):
            P.op("pool", lambda e, i=i: e.memset(kvin[i][:, :, :], 0.0), writes=[Bkvin[i]])
            P.op("pool", lambda e, i=i: e.memset(kvt[i][:, :], 0.0), writes=[Bkvt[i]])
        for l in range(2):
            P.op("pool", lambda e, l=l: e.memset(stt[l][:, :, :], 0.0), writes=[Bst[l]])
        P.op("act", lambda e: e.activation(out=c1[:, :], in_=vecs[:, V_LAM:V_LAM + 64], func=AF.Exp, scale=-1.0), reads=[Bv], writes=[Bc1])
        P.op("act", lambda e: e.activation(out=c1[:, :], in_=c1[:, :], func=AF.Ln, bias=1.0, scale=1.0), reads=[Bc1], writes=[Bc1])
        P.op("dve", lambda e: e.tensor_scalar(out=c1[:, :], in0=c1[:, :], scalar1=-8.0, scalar2=None, op0=ALU.mult), reads=[Bc1], writes=[Bc1])
        for c0 in range(0, NE, 1024):
            cw = min(1024, NE - c0)
            P.dma("sp", setsem[2], lambda e, c0=c0, cw=cw: e.dma_start(out=esb[:, 0:cw], in_=ein_d[:, c0:c0 + cw]), writes=[Besb])
            P.dma("sp", setsem[3], lambda e, c0=c0, cw=cw: e.dma_start(out=emb[:, 0:cw], in_=em_d[:, c0:c0 + cw]), writes=[Bemb])
            P.op("act", lambda e, cw=cw: e.activation(out=esb[:, 0:cw], in_=esb[:, 0:cw], func=AF.Exp), reads=[Besb], writes=[Besb])
            P.op("dve", lambda e, c0=c0, cw=cw: e.tensor_tensor(out=Eall[:, c0:c0 + cw], in0=esb[:, 0:cw], in1=emb[:, 0:cw], op=ALU.mult),
                 reads=[Besb, Bemb], writes=[BE])
        P.op("act", lambda e: e.activation(out=enew[:, :], in_=enew[:, :], func=AF.Exp), reads=[Benew], writes=[Benew])

        wctr = [0]
        pctr = [0]

        def wload(Wsl, nk, col0):
            s_ = wctr[0] % NW; wctr[0] += 1
            P.dma("pool", wsem[s_], lambda e: e.dma_start(
                out=wbf[s_][:, 0:nk, :], in_=Wsl[:, col0:col0 + 128].rearrange("(k p) m -> p k m", p=128)), writes=[Bw[s_]])
            return s_

        def lin(Wsl, nk, cols, src, Bsrc, Tn, consumer):
            slots = {}
            for i0 in range(min(2, len(cols))):
                slots[i0] = wload(Wsl, nk, cols[i0])
            for idx, col0 in enumerate(cols):
                s = slots.pop(idx)
                pb = pctr[0] % 3; pctr[0] += 1
                for k in range(nk):
                    P.op("pe", lambda e, s=s, k=k, pb=pb: e.matmul(ps[:, pb, 0:Tn], lhsT=wbf[s][:, k, :], rhs=src[:, k, 0:Tn],
                                                                  start=(k == 0), stop=(k == nk - 1)),
                         reads=[Bw[s], Bsrc[k]], writes=[Bps[pb]], inc=(k == nk - 1))
                if idx + 2 < len(cols):
                    slots[idx + 2] = wload(Wsl, nk, cols[idx + 2])
                consumer(idx, pb)

        def rstd_from_ps3(Tn):
            P.op("act", lambda e: e.activation(out=rs[:, 0:Tn], in_=ps[:, 3, 0:Tn], func=AF.Sqrt, bias=EPS, scale=1.0 / 4096), reads=[Bps[3]], writes=[Brs])
            P.op("dve", lambda e: e.reciprocal(out=rs[:, 0:Tn], in_=rs[:, 0:Tn]), reads=[Brs], writes=[Brs])

        def norm_pre(gcol, Tn):
            for k in range(32):
                i = k % 2
                P.op("dve", lambda e, k=k, i=i: e.tensor_tensor(out=sqb[i][:, 0:Tn], in0=xT[:, k, 0:Tn], in1=xT[:, k, 0:Tn], op=ALU.mult),
                     reads=[Bx[k]], writes=[Bsq[i]])
                P.op("pe", lambda e, k=k, i=i: e.matmul(ps[:, 3, 0:Tn], lhsT=ones[:, :], rhs=sqb[i][:, 0:Tn], start=(k == 0), stop=(k == 31)),
                     reads=[Bsq[i], Bones], writes=[Bps[3]], inc=True)
            rstd_from_ps3(Tn)
            for k in range(32):
                P.op("dve", lambda e, k=k: e.scalar_tensor_tensor(out=ub[:, k, 0:Tn], in0=xT[:, k, 0:Tn], scalar=vecs[:, gcol + k:gcol + k + 1],
                                                                  in1=rs[:, 0:Tn], op0=ALU.mult, op1=ALU.mult),
                     reads=[Bx[k], Brs, Bv], writes=[Bu[k]])

        def out_consumer(Tn):
            def cons(m, pb):
                i = m % 2
                P.op("act", lambda e, m=m, pb=pb: e.activation(out=ub[:, m, 0:Tn], in_=ps[:, pb, 0:Tn], func=AF.Copy), reads=[Bps[pb]], writes=[Bu[m]])
                P.op("act", lambda e, i=i, pb=pb: e.activation(out=sqb[i][:, 0:Tn], in_=ps[:, pb, 0:Tn], func=AF.Square), reads=[Bps[pb]], writes=[Bsq[i]])
                P.op("pe", lambda e, m=m, i=i: e.matmul(ps[:, 3, 0:Tn], lhsT=ones[:, :], rhs=sqb[i][:, 0:Tn], start=(m == 0), stop=(m == 31)),
                     reads=[Bsq[i], Bones], writes=[Bps[3]], inc=True)
            return cons

        def post(gcol, Tn):
            rstd_from_ps3(Tn)
            for k in range(32):
                P.op("dve", lambda e, k=k: e.scalar_tensor_tensor(out=tmpx[:, 0:Tn], in0=ub[:, k, 0:Tn], scalar=vecs[:, gcol + k:gcol + k + 1],
                                                                  in1=rs[:, 0:Tn], op0=ALU.mult, op1=ALU.mult),
                     reads=[Bu[k], Brs, Bv], writes=[Btmpx])
                P.op("dve", lambda e, k=k: e.tensor_tensor(out=xT[:, k, 0:Tn], in0=xT[:, k, 0:Tn], in1=tmpx[:, 0:Tn], op=ALU.add),
                     reads=[Btmpx, Bx[k]], writes=[Bx[k]])

        def a_layer(l, Tn):
            norm_pre(V_APRE + 32 * l, Tn)
            S = stt[l]; BS = Bst[l]
            cols = []
            for h in range(16):
                cols += [(2 * h) * 128, (2 * h + 1) * 128, 4096 + (2 * h) * 128, 4096 + (2 * h + 1) * 128]

            def cons(idx, pb):
                h, j = idx // 4, idx % 4
                if j < 2:
                    c = 2 * h + j
                    P.op("dve", lambda e: e.tensor_copy(out=xc[j][:, 0:3], in_=S[:, 0:3, c]), reads=[BS], writes=[Bxc[j]])
                    P.op("act", lambda e: e.activation(out=xc[j][:, 3:3 + Tn], in_=ps[:, pb, 0:Tn], func=AF.Copy), reads=[Bps[pb]], writes=[Bxc[j]])
                    P.op("dve", lambda e: e.tensor_copy(out=S[:, 0:3, c], in_=xc[j][:, Tn:Tn + 3]), reads=[Bxc[j]], writes=[BS])
                    cwc = V_CW + l * 128
                    P.op("dve", lambda e: e.tensor_scalar(out=xv[j][:, 0:Tn], in0=xc[j][:, 3:3 + Tn], scalar1=vecs[:, cwc + 96 + c:cwc + 97 + c],
                                                          scalar2=vecs[:, V_CB + 32 * l + c:V_CB + 32 * l + c + 1], op0=ALU.mult, op1=ALU.add),
                         reads=[Bxc[j], Bv], writes=[Bxv[j]])
                    for kk in range(3):
                        P.op("dve", lambda e, kk=kk: e.scalar_tensor_tensor(out=xv[j][:, 0:Tn], in0=xc[j][:, kk:kk + Tn],
                                                                            scalar=vecs[:, cwc + 32 * kk + c:cwc + 32 * kk + c + 1],
                                                                            in1=xv[j][:, 0:Tn], op0=ALU.mult, op1=ALU.add),
                             reads=[Bxc[j], Bv, Bxv[j]], writes=[Bxv[j]])
                    P.op("pool", lambda e: e.tensor_copy(out=xvb[j][:, 0:Tn], in_=xv[j][:, 0:Tn]), reads=[Bxv[j]], writes=[Bxvb[j]])
                    if j == 1:
                        P.dma("pool", wgsem, lambda e: e.dma_start(out=wg[:, 0, :, :], in_=a_wga[l, h].rearrange("(ic p) o -> p ic o", p=128)), writes=[Bwg])
                        P.dma("pool", wgsem, lambda e: e.dma_start(out=wg[:, 1, :, :], in_=a_wgx[l, h].rearrange("(ic p) o -> p ic o", p=128)), writes=[Bwg])
                        for a in range(2):
                            bcol = (V_BGA if a == 0 else V_BGX) + 32 * l
                            for jo in range(2):
                                gb = 4 + (a * 2 + jo) % 2
                                for ic in range(2):
                                    P.op("pe", lambda e, a=a, jo=jo, ic=ic, gb=gb: e.matmul(ps[:, gb, 0:Tn], lhsT=wg[:, a, ic, jo * 128:(jo + 1) * 128],
                                                                                            rhs=xvb[ic][:, 0:Tn], start=(ic == 0), stop=(ic == 1)),
                                         reads=[Bwg, Bxvb[ic]], writes=[Bps[gb]], inc=(ic == 1))
                                cc = 2 * h + jo
                                P.op("act", lambda e, a=a, jo=jo, gb=gb, cc=cc, bcol=bcol: e.activation(
                                    out=rg[a][jo][:, 0:Tn], in_=ps[:, gb, 0:Tn], func=AF.Sigmoid, bias=vecs[:, bcol + cc:bcol + cc + 1], scale=1.0),
                                    reads=[Bps[gb], Bv], writes=[Brg[a][jo]])
                        for jo in range(2):
                            cc = 2 * h + jo
                            P.op("act", lambda e, jo=jo, cc=cc: e.activation(out=ta[:, 0:Tn], in_=rg[0][jo][:, 0:Tn], func=AF.Exp,
                                                                             scale=c1[:, 32 * l + cc:32 * l + cc + 1]),
                                 reads=[Brg[0][jo], Bc1], writes=[Bta])
                            P.op("dve", lambda e: e.tensor_tensor(out=tb[:, 0:Tn], in0=ta[:, 0:Tn], in1=ta[:, 0:Tn], op=ALU.mult), reads=[Bta], writes=[Btb])
                            P.op("act", lambda e: e.activation(out=tb[:, 0:Tn], in_=tb[:, 0:Tn], func=AF.Sqrt, bias=1.0, scale=-1.0), reads=[Btb], writes=[Btb])
                            P.op("dve", lambda e, jo=jo: e.tensor_tensor(out=tcc[:, 0:Tn], in0=rg[1][jo][:, 0:Tn], in1=xv[jo][:, 0:Tn], op=ALU.mult),
                                 reads=[Brg[1][jo], Bxv[jo]], writes=[Btc])
                            P.op("dve", lambda e: e.tensor_tensor(out=tcc[:, 0:Tn], in0=tcc[:, 0:Tn], in1=tb[:, 0:Tn], op=ALU.mult), reads=[Btc, Btb], writes=[Btc])
                            P.op("dve", lambda e, jo=jo, cc=cc: e.tensor_tensor_scan(out=hh[jo][:, 0:Tn], data0=ta[:, 0:Tn], data1=tcc[:, 0:Tn],
                                                                                     initial=S[:, 3, cc:cc + 1], op0=ALU.mult, op1=ALU.add),
                                 reads=[Bta, Btc, BS], writes=[Bhh[jo]])
                            P.op("dve", lambda e, jo=jo, cc=cc: e.tensor_copy(out=S[:, 3, cc:cc + 1], in_=hh[jo][:, Tn - 1:Tn]), reads=[Bhh[jo]], writes=[BS])
                else:
                    jo = j - 2
                    c = 2 * h + jo
                    P.op("act", lambda e: e.activation(out=sg[:, 0:Tn], in_=ps[:, pb, 0:Tn], func=AF.Silu), reads=[Bps[pb]], writes=[Bsg])
                    P.op("dve", lambda e: e.tensor_tensor(out=hg[:, c, 0:Tn], in0=hh[jo][:, 0:Tn], in1=sg[:, 0:Tn], op=ALU.mult),
                         reads=[Bhh[jo], Bsg], writes=[Bh[c]])

            lin(a_w_in[l], 32, cols, ub, Bu, Tn, cons)
            lin(a_w_out[l], 32, [m * 128 for m in range(32)], hg, Bh, Tn, out_consumer(Tn))
            post(V_APOST + 32 * l, Tn)

        def kv_phase(row0, Tn, sample):
            norm_pre(V_KVG, Tn)
            ntc = max(1, Tn // 128)
            M = min(128, Tn)
            slots = {0: wload(w_kv, 32, 0), 1: wload(w_kv, 32, 128)}
            for n in range(24):
                s = slots.pop(n)
                if n + 2 < 24 and n >= 1:
                    pass
                for tc in range(ntc):
                    pb = pctr[0] % 3; pctr[0] += 1
                    for k in range(32):
                        P.op("pe", lambda e, s=s, k=k, pb=pb, tc=tc: e.matmul(ps[0:M, pb, 0:128], lhsT=ub[:, k, tc * 128:tc * 128 + M], rhs=wbf[s][:, k, :],
                                                                              start=(k == 0), stop=(k == 31)),
                             reads=[Bw[s], Bu[k]], writes=[Bps[pb]], inc=(k == 31))
                    P.op("act", lambda e, pb=pb, tc=tc, n=n: e.activation(out=kvt[tc][0:M, n * 128:(n + 1) * 128], in_=ps[0:M, pb, 0:128], func=AF.Copy),
                         reads=[Bps[pb]], writes=[Bkvt[tc]])
                if n + 2 < 24:
                    slots[n + 2] = wload(w_kv, 32, (n + 2) * 128)
            if not sample:
                for tc in range(ntc):
                    r0 = row0 + tc * 128
                    P.dma("sp", kvosem, lambda e, tc=tc, r0=r0: e.dma_start(out=pkv[r0:r0 + 128, :], in_=kvt[tc][:, :]), reads=[Bkvt[tc]], writes=[Bout])
                    P.dma("sp", kvosem, lambda e, tc=tc, r0=r0: e.dma_start(out=kvs[r0:r0 + 128, :], in_=kvt[tc][:, :]), reads=[Bkvt[tc]], writes=[Bd2d])

        uctr = [0]

        def key_tile(src_rows_ap, nrows, g, kh):
            i = uctr[0] % 2; uctr[0] += 1
            P.dma("sp", kvinsem[i], lambda e: e.dma_start(out=kvin[i][0:nrows, :, :], in_=src_rows_ap), reads=[Bd2d], writes=[Bkvin[i]])
            tb_ = 4 + i
            P.op("pe", lambda e: e.transpose(out=ps[:, tb_, 0:128], in_=kvin[i][:, 0, :], identity=ident[:, :]), reads=[Bkvin[i], Bid], writes=[Bps[tb_]])
            P.op("act", lambda e: e.activation(out=kTb[i][:, :], in_=ps[:, tb_, 0:128], func=AF.Copy), reads=[Bps[tb_]], writes=[BkT[i]])
            P.op("pool", lambda e: e.tensor_copy(out=vbb[i][:, :], in_=kvin[i][:, 1, :]), reads=[Bkvin[i]], writes=[Bvb[i]])
            return i

        def attn_unit(qap, nq, tiles, acc_first, accap_fn):
            W = 4 * nq
            for ti, (i, eoff) in enumerate(tiles):
                sb_ = 4 + i
                P.op("pe", lambda e, i=i, sb_=sb_: e.matmul(ps[:, sb_, 0:W].rearrange("p (s q) -> p s q", s=4), lhsT=kTb[i][:, :], rhs=qap, start=True, stop=True),
                     reads=[BkT[i], Bq4], writes=[Bps[sb_]])
                P.op("act", lambda e, i=i, sb_=sb_: e.activation(out=pT[i][:, 0:W], in_=ps[:, sb_, 0:W], func=AF.Exp, scale=SCALE), reads=[Bps[sb_]], writes=[BpT[i]])
                P.op("dve", lambda e, i=i, eoff=eoff: e.tensor_tensor(out=pT[i][:, 0:W], in0=pT[i][:, 0:W], in1=Eall[:, eoff:eoff + W], op=ALU.mult),
                     reads=[BpT[i], BE], writes=[BpT[i]])
                first, last = ti == 0, ti == len(tiles) - 1
                P.op("pe", lambda e, i=i, first=first, last=last: e.matmul(ps[:, 6, 0:W], lhsT=vbb[i][:, :], rhs=pT[i][:, 0:W], start=first, stop=last),
                     reads=[Bvb[i], BpT[i]], writes=[Bps[6]])
                P.op("pe", lambda e, i=i, first=first, last=last: e.matmul(ps[:, 7, 0:W], lhsT=ones[:, :], rhs=pT[i][:, 0:W], start=first, stop=last),
                     reads=[Bones, BpT[i]], writes=[Bps[7]])
            for (acc, Bacc, bank) in ((num, Bnum, 6), (den, Bden, 7)):
                src = ps[:, bank, 0:W].rearrange("p (s q) -> p s q", s=4)
                if acc_first:
                    P.op("act", lambda e, acc=acc, src=src: e.activation(out=accap_fn(acc), in_=src, func=AF.Copy), reads=[Bps[bank]], writes=[Bacc])
                else:
                    P.op("dve", lambda e, acc=acc, src=src: e.tensor_tensor(out=accap_fn(acc), in0=accap_fn(acc), in1=src, op=ALU.add),
                         reads=[Bps[bank], Bacc], writes=[Bacc])

        def attention_prompt(b, g, kh):
            t0 = b * T
            kc = g * 1024 + kh * 128

            def rows(start, step, count):
                v = kvs[start:start + step * count, :].rearrange("(j c) n -> j c n", c=step)[:, 0, :]
                return v[:, g * 1024:(g + 1) * 1024].rearrange("j (v k x) -> j v k x", v=2, k=4)[:, :, kh, :]

            if g == 0:
                for qb in range(T // 128):
                    tiles = []
                    if t0 + qb * 128 - 128 >= 0:
                        i = key_tile(rows(t0 + qb * 128 - 128, 1, 128), 128, g, kh)
                        tiles.append((i, e0_off(1, kh)))
                    i = key_tile(rows(t0 + qb * 128, 1, 128), 128, g, kh)
                    tiles.append((i, e0_off(0, kh)))
                    qap = q4[:, :, qb * 128:(qb + 1) * 128]
                    attn_unit(qap, 128, tiles, True, lambda acc, qb=qb: acc[:, :, qb * 128:(qb + 1) * 128])
            elif g == 1:
                nbi, half = b // 2, b % 2
                for c in range(4):
                    tiles = []
                    if nbi > 0:
                        i = key_tile(rows(512 * (nbi - 1) + c, 4, 128), 128, g, kh)
                        tiles.append((i, e1_off(half, 1, kh)))
                    cnt = 64 * (half + 1)
                    i = key_tile(rows(512 * nbi + c, 4, cnt), cnt, g, kh)
                    tiles.append((i, e1_off(half, 0, kh)))
                    qap = q4[:, :, :].rearrange("p s (j c) -> p s c j", c=4)[:, :, c, :]
                    attn_unit(qap, 64, tiles, False, lambda acc, c=c: acc[:, :, :].rearrange("p s (j c) -> p s c j", c=4)[:, :, c, :])
            else:
                for c in range(16):
                    cnt = 16 * (b + 1)
                    i = key_tile(rows(c, 16, cnt), cnt, g, kh)
                    qap = q4[:, :, :].rearrange("p s (j c) -> p s c j", c=16)[:, :, c, :]
                    attn_unit(qap, 16, [(i, e2_off(b, kh))], False, lambda acc, c=c: acc[:, :, :].rearrange("p s (j c) -> p s c j", c=16)[:, :, c, :])

        def attention_sample(g, kh):
            r = DIL[g]
            lb = 128 * r
            src = skv[g].rearrange("(j c) (v x) -> j c v x", c=r, v=2)[:, 0, :, kh * 128:(kh + 1) * 128]
            i = uctr[0] % 2; uctr[0] += 1
            P.dma("sp", kvinsem[i], lambda e: e.dma_start(out=kvin[i][:, :, :], in_=src), writes=[Bkvin[i]])
            tb_ = 4 + i
            P.op("pe", lambda e: e.transpose(out=ps[:, tb_, 0:128], in_=kvin[i][:, 0, :], identity=ident[:, :]), reads=[Bkvin[i], Bid], writes=[Bps[tb_]])
            P.op("act", lambda e: e.activation(out=kTb[i][:, :], in_=ps[:, tb_, 0:128], func=AF.Copy), reads=[Bps[tb_]], writes=[BkT[i]])
            P.op("pool", lambda e: e.tensor_copy(out=vbb[i][:, :], in_=kvin[i][:, 1, :]), reads=[Bkvin[i]], writes=[Bvb[i]])
            qap = q4[:, :, 0:1]
            P.op("pe", lambda e: e.matmul(ps[:, tb_, 0:4].rearrange("p (s q) -> p s q", s=4), lhsT=kTb[i][:, :], rhs=qap, start=True, stop=True),
                 reads=[BkT[i], Bq4], writes=[Bps[tb_]])
            P.op("act", lambda e: e.activation(out=pT[i][:, 0:4], in_=ps[:, tb_, 0:4], func=AF.Exp, scale=SCALE), reads=[Bps[tb_]], writes=[BpT[i]])
            eo = es_off(g, kh)
            P.op("dve", lambda e: e.tensor_tensor(out=pT[i][:, 0:4], in0=pT[i][:, 0:4], in1=Eall[:, eo:eo + 4], op=ALU.mult), reads=[BpT[i], BE], writes=[BpT[i]])
            P.op("pe", lambda e: e.matmul(ps[:, 6, 0:4], lhsT=vbb[i][:, :], rhs=pT[i][:, 0:4], start=True, stop=False), reads=[Bvb[i], BpT[i]], writes=[Bps[6]])
            P.op("pe", lambda e: e.matmul(ps[:, 7, 0:4], lhsT=ones[:, :], rhs=pT[i][:, 0:4], start=True, stop=False), reads=[Bones, BpT[i]], writes=[Bps[7]])
            kc = g * 1024 + kh * 128
            P.op("pe", lambda e: e.transpose(out=ps[:, tb_, 8:9], in_=kvt[0][0:1, kc:kc + 128], identity=ident[0:1, 0:1]), reads=[Bkvt[0], Bid], writes=[Bps[tb_]])
            P.op("act", lambda e: e.activation(out=knew[:, 0:1], in_=ps[:, tb_, 8:9], func=AF.Copy), reads=[Bps[tb_]], writes=[Bknew])
            P.op("pool", lambda e: e.tensor_copy(out=vnew[0:1, 0:128], in_=kvt[0][0:1, kc + 512:kc + 640]), reads=[Bkvt[0]], writes=[Bvnew])
            P.op("pe", lambda e: e.matmul(ps[0:1, tb_, 16:20].rearrange("p (s q) -> p s q", s=4), lhsT=knew[:, 0:1], rhs=qap, start=True, stop=True),
                 reads=[Bknew, Bq4], writes=[Bps[tb_]])
            P.op("act", lambda e: e.activation(out=misc[0:1, 0:4], in_=ps[0:1, tb_, 16:20], func=AF.Exp, scale=SCALE), reads=[Bps[tb_]], writes=[Bmisc])
            en = (g * 4 + kh) * 4
            P.op("dve", lambda e: e.tensor_tensor(out=pnew[0:1, 0:4], in0=misc[0:1, 0:4], in1=enew[0:1, en:en + 4], op=ALU.mult), reads=[Bmisc, Benew], writes=[Bpnew])
            P.op("pe", lambda e: e.matmul(ps[:, 6, 0:4], lhsT=vnew[0:1, 0:128], rhs=pnew[0:1, 0:4], start=False, stop=True), reads=[Bvnew, Bpnew], writes=[Bps[6]])
            P.op("pe", lambda e: e.matmul(ps[:, 7, 0:4], lhsT=ones[0:1, :], rhs=pnew[0:1, 0:4], start=False, stop=True), reads=[Bones, Bpnew], writes=[Bps[7]])
            for (acc, Bacc, bank) in ((num, Bnum, 6), (den, Bden, 7)):
                src2 = ps[:, bank, 0:4].rearrange("p (s q) -> p s q", s=4)
                if g == 0:
                    P.op("act", lambda e, acc=acc, src2=src2: e.activation(out=acc[:, :, 0:1], in_=src2, func=AF.Copy), reads=[Bps[bank]], writes=[Bacc])
                else:
                    P.op("dve", lambda e, acc=acc, src2=src2: e.tensor_tensor(out=acc[:, :, 0:1], in0=acc[:, :, 0:1], in1=src2, op=ALU.add),
                         reads=[Bps[bank], Bacc], writes=[Bacc])

        def b_layer(l, Tn, b, sample):
            norm_pre(V_BPRE + 32 * l, Tn)
            cols = []
            for kh in range(4):
                for g in range(3):
                    cols += [g * 2048 + (4 * kh + s) * 128 for s in range(4)]
                cols += [6144 + (4 * kh + s) * 128 for s in range(4)]

            def cons(idx, pb):
                kh, j = idx // 16, idx % 16
                if j < 12:
                    g, s = j // 4, j % 4
                    P.op("act", lambda e: e.activation(out=q4[:, s, 0:Tn], in_=ps[:, pb, 0:Tn], func=AF.Copy), reads=[Bps[pb]], writes=[Bq4])
                    if s == 3:
                        if sample:
                            attention_sample(g, kh)
                        else:
                            attention_prompt(b, g, kh)
                else:
                    s = j - 12
                    if s == 0:
                        P.op("dve", lambda e: e.reciprocal(out=den[:, :, 0:Tn], in_=den[:, :, 0:Tn]), reads=[Bden], writes=[Bden])
                        P.op("dve", lambda e: e.tensor_tensor(out=num[:, :, 0:Tn], in0=num[:, :, 0:Tn], in1=den[:, :, 0:Tn], op=ALU.mult),
                             reads=[Bnum, Bden], writes=[Bnum])
                    P.op("act", lambda e: e.activation(out=sg[:, 0:Tn], in_=ps[:, pb, 0:Tn], func=AF.Silu), reads=[Bps[pb]], writes=[Bsg])
                    P.op("dve", lambda e: e.tensor_tensor(out=hg[:, 4 * kh + s, 0:Tn], in0=num[:, s, 0:Tn], in1=sg[:, 0:Tn], op=ALU.mult),
                         reads=[Bnum, Bsg], writes=[Bh[4 * kh + s]])

            lin(b_w_in[l], 32, cols, ub, Bu, Tn, cons)
            lin(b_w_out[l], 16, [m * 128 for m in range(32)], hg, Bh, Tn, out_consumer(Tn))
            post(V_BPOST + 32 * l, Tn)

        def load_x(src_rows, M, col0):
            P.dma("sp", xiosem, lambda e: e.dma_start(out=xio[0:M, :], in_=src_rows), writes=[Bxio])
            for k in range(32):
                tb_ = 4 + k % 2
                P.op("pe", lambda e, k=k, tb_=tb_: e.transpose(out=ps[:, tb_, 0:M], in_=xio[0:M, k * 128:(k + 1) * 128], identity=ident[0:M, 0:M]),
                     reads=[Bxio, Bid], writes=[Bps[tb_]])
                P.op("act", lambda e, k=k, tb_=tb_: e.activation(out=xT[:, k, col0:col0 + M], in_=ps[:, tb_, 0:M], func=AF.Copy), reads=[Bps[tb_]], writes=[Bx[k]])

        def store_x(dst_rows, M, col0):
            for k in range(32):
                tb_ = 4 + k % 2
                P.op("pe", lambda e, k=k, tb_=tb_: e.transpose(out=ps[0:M, tb_, 0:128], in_=xT[:, k, col0:col0 + M], identity=ident[:, :]),
                     reads=[Bx[k], Bid], writes=[Bps[tb_]])
                P.op("act", lambda e, k=k, tb_=tb_: e.activation(out=xio[0:M, k * 128:(k + 1) * 128], in_=ps[0:M, tb_, 0:128], func=AF.Copy),
                     reads=[Bps[tb_]], writes=[Bxio])
            P.dma("sp", xiosem, lambda e: e.dma_start(out=dst_rows, in_=xio[0:M, :]), reads=[Bxio], writes=[Bout])

        def store_states(conv_o, h_o):
            for l in range(2):
                P.op("pe", lambda e, l=l: e.transpose(out=ps[:, 4, 0:128], in_=stt[l][:, :, :].rearrange("p j c -> p (j c)"), identity=ident[:, :]),
                     reads=[Bst[l], Bid], writes=[Bps[4]])
                P.op("act", lambda e: e.activation(out=stT[:, :], in_=ps[:, 4, 0:128], func=AF.Copy), reads=[Bps[4]], writes=[BstT])
                for j in range(3):
                    P.dma("sp", outsem, lambda e, l=l, j=j: e.dma_start(out=conv_o[l, j].rearrange("(c p) -> c p", p=128), in_=stT[j * 32:(j + 1) * 32, :]),
                          reads=[BstT], writes=[Bout])
                P.dma("sp", outsem, lambda e, l=l: e.dma_start(out=h_o[l].rearrange("(c p) -> c p", p=128), in_=stT[96:128, :]), reads=[BstT], writes=[Bout])

        def load_states():
            for l in range(2):
                for j in range(3):
                    P.dma("sp", setsem[4], lambda e, l=l, j=j: e.dma_start(out=stT[j * 32:(j + 1) * 32, :], in_=sconv[l, j].rearrange("(c p) -> c p", p=128)), writes=[BstT])
                P.dma("sp", setsem[4], lambda e, l=l: e.dma_start(out=stT[96:128, :], in_=sh[l].rearrange("(c p) -> c p", p=128)), writes=[BstT])
                P.op("pe", lambda e, l=l: e.transpose(out=ps[:, 4, 0:128], in_=stT[:, :], identity=ident[:, :]), reads=[BstT, Bid], writes=[Bps[4]])
                P.op("act", lambda e, l=l: e.activation(out=stt[l][:, :, :].rearrange("p j c -> p (j c)"), in_=ps[:, 4, 0:128], func=AF.Copy),
                     reads=[Bps[4]], writes=[Bst[l]])

        for b in range(NB):
            for tc in range(T // 128):
                load_x(xp[b * T + tc * 128:b * T + (tc + 1) * 128, :], 128, tc * 128)
            a_layer(0, T)
            a_layer(1, T)
            kv_phase(b * T, T, False)
            b_layer(0, T, b, False)
            b_layer(1, T, b, False)
            for tc in range(T // 128):
                store_x(yp[b * T + tc * 128:b * T + (tc + 1) * 128, :], 128, tc * 128)
        store_states(pconv, ph)
        load_states()
        load_x(xs[0:1, :], 1, 0)
        a_layer(0, 1)
        a_layer(1, 1)
        kv_phase(0, 1, True)
        for g in range(3):
            lbg = 128 * DIL[g]
            P.dma("sp", d2dsem, lambda e, g=g, lbg=lbg: e.dma_start(out=skv_o[g][0:lbg - 1, :], in_=skv[g][1:lbg, :]), writes=[Bout])
            P.dma("sp", d2dsem, lambda e, g=g, lbg=lbg: e.dma_start(out=skv_o[g][lbg - 1:lbg, :], in_=kvt[0][0:1, g * 1024:(g + 1) * 1024]),
                  reads=[Bkvt[0]], writes=[Bout])
        b_layer(0, 1, 0, True)
        b_layer(1, 1, 0, True)
        store_x(ys[0:1, :], 1, 0)
        store_states(sconv_o, sh_o)
        fin = [(sm, P.dcnt[id(sm)]) for sm in (kvosem, outsem, xiosem, d2dsem) if P.dcnt[id(sm)] > 0]
        P.q["sp"].append((None, fin, None))
        with nc.Block() as block:
            P.replay(block)
    return nc


def _rel_bucket(dist):
    dist = np.asarray(dist)
    d = np.maximum(dist, 1).astype(np.float32)
    large = 16 + (np.log(d / 16) / np.log(2048 / 16) * (32 - 16)).astype(np.int32)
    large = np.minimum(large, 31)
    return np.where(dist < 16, dist, large).astype(np.int32)


def _etiles(rel_bias):
    ein = np.zeros((128, NE), np.float32)
    em = np.zeros((128, NE), np.float32)
    enew = np.zeros((1, 48), np.float32)
    p = np.arange(128)[:, None]

    def fill(off, g, kh, diff, valid):
        nq = diff.shape[1]
        bk = _rel_bucket(np.clip(diff, 0, 128) * DIL[g])
        for s in range(4):
            col = g * 16 + 4 * kh + s
            ein[:, off + s * nq:off + (s + 1) * nq] = rel_bias[bk, col]
            em[:, off + s * nq:off + (s + 1) * nq] = valid.astype(np.float32)

    for kh in range(4):
        q = np.arange(128)[None, :]
        fill(e0_off(0, kh), 0, kh, q - p, (q - p) >= 0)
        fill(e0_off(1, kh), 0, kh, 128 + q - p, (128 + q - p) <= 128)
        q = np.arange(64)[None, :]
        for half in range(2):
            d = 64 * half + q - p
            fill(e1_off(half, 0, kh), 1, kh, d, d >= 0)
            d = 128 + 64 * half + q - p
            fill(e1_off(half, 1, kh), 1, kh, d, d <= 128)
        q = np.arange(16)[None, :]
        for b in range(NB):
            d = 16 * b + q - p
            fill(e2_off(b, kh), 2, kh, d, d >= 0)
        for g in range(3):
            d = 128 - p + np.zeros((1, 1), np.int64)
            fill(es_off(g, kh), g, kh, d, d >= 0)
            for s in range(4):
                enew[0, (g * 4 + kh) * 4 + s] = rel_bias[0, g * 16 + 4 * kh + s]
    return ein, em, enew


def _colvec(v):
    v = np.asarray(v, np.float32).reshape(-1, 32, 128)
    return np.ascontiguousarray(v.transpose(2, 0, 1).reshape(128, -1))


_NC_CACHE = {}


def kernel(x_prompt, x_sample, state_conv, state_h, state_kv_w128, state_kv_w512, state_kv_w2048,
           a_pre_g, a_w_in, a_conv_w, a_conv_b, a_w_gate_a, a_b_gate_a, a_w_gate_x, a_b_gate_x,
           a_lambda, a_w_out, a_post_g, kv_norm_g, w_kv, rel_bias, b_pre_g, b_w_in, b_w_out, b_post_g):
    f = lambda a: np.ascontiguousarray(np.asarray(a, np.float32))
    vecs = np.concatenate([
        _colvec(a_pre_g), _colvec(np.asarray(a_conv_w)), _colvec(a_conv_b),
        _colvec(np.asarray(a_b_gate_a).reshape(2, 4096)), _colvec(np.asarray(a_b_gate_x).reshape(2, 4096)),
        _colvec(a_lambda), _colvec(a_post_g), _colvec(kv_norm_g), _colvec(b_pre_g), _colvec(b_post_g)], axis=1)
    assert vecs.shape == (128, NV)
    ein, em, enew = _etiles(np.asarray(rel_bias, np.float32))
    shared = dict(a_w_in=f(a_w_in), a_w_out=f(a_w_out), a_wga=f(a_w_gate_a), a_wgx=f(a_w_gate_x), w_kv=f(w_kv),
                  b_w_in=f(b_w_in), b_w_out=f(b_w_out), vecs=f(vecs), ein=ein, em=em, enew=enew,
                  ident=np.eye(128, dtype=np.float32))
    xp = f(x_prompt); xs = f(x_sample); sc = f(state_conv); shh = f(state_h)
    k0 = f(state_kv_w128).reshape(8, 128, 1024); k1 = f(state_kv_w512).reshape(8, 512, 1024); k2 = f(state_kv_w2048).reshape(8, 2048, 1024)
    in_maps = []
    for c in range(8):
        m = dict(shared)
        m.update(xp=xp[c % 4], xs=xs[c], sconv=np.ascontiguousarray(sc[:, c]), sh=np.ascontiguousarray(shh[:, c]),
                 skv0=k0[c], skv1=k1[c], skv2=k2[c])
        in_maps.append(m)
    if "nc" not in _NC_CACHE:
        _NC_CACHE["nc"] = build()
    res = run_bass_kernel_spmd(_NC_CACHE["nc"], in_maps, core_ids=list(range(8)))
    R = res.results
    y_prompt = np.stack([R[c]["yp"] for c in range(4)])
    y_sample = np.stack([R[c]["ys"] for c in range(8)])
    p_conv = np.stack([R[c]["pconv"] for c in range(4)], axis=1)
    p_h = np.stack([R[c]["ph"] for c in range(4)], axis=1)
    pkv = np.stack([R[c]["pkv"] for c in range(4)]).reshape(4, 2048, 3, 2, 4, 128)
    p_kv128 = np.ascontiguousarray(pkv[:, 2048 - 128:, 0])
    p_kv512 = np.ascontiguousarray(pkv[:, 2048 - 512:, 1])
    p_kv2048 = np.ascontiguousarray(pkv[:, :, 2])
    s_conv = np.stack([R[c]["sconv_o"] for c in range(8)], axis=1)
    s_h = np.stack([R[c]["sh_o"] for c in range(8)], axis=1)
    s_kv128 = np.stack([R[c]["skv_o0"] for c in range(8)]).reshape(8, 128, 2, 4, 128)
    s_kv512 = np.stack([R[c]["skv_o1"] for c in range(8)]).reshape(8, 512, 2, 4, 128)
    s_kv2048 = np.stack([R[c]["skv_o2"] for c in range(8)]).reshape(8, 2048, 2, 4, 128)
    f32 = lambda a: np.asarray(a, np.float32)
    return tuple(f32(a) for a in (y_prompt, y_sample, p_conv, p_h, p_kv128, p_kv512, p_kv2048, s_conv, s_h, s_kv128, s_kv512, s_kv2048))
```

```python
import contextlib
import numpy as np
import concourse.bass as bass
import concourse.mybir as mybir
from concourse.bass_utils import run_bass_kernel_spmd

F32 = mybir.dt.float32
BF16 = mybir.dt.bfloat16
AF = mybir.ActivationFunctionType
ALU = mybir.AluOpType

SAME_ENGINE_SYNC = True
T = 256
NB = 2048 // T
EPS = 1e-6
SCALE = 128 ** -0.5
DIL = (1, 4, 16)


class Buf:
    __slots__ = ("name", "w", "r")

    def __init__(self, name=""):
        self.name = name
        self.w = None
        self.r = {}


class Prog:
    ENGS = ("pe", "act", "dve", "pool", "sp")

    def __init__(self, nc, stack):
        self.nc = nc
        self.stack = stack
        self.q = {e: [] for e in self.ENGS}
        self.cnt = {e: 0 for e in self.ENGS}
        self.seen = {e: {} for e in self.ENGS}
        self.esem = {e: stack.enter_context(nc.semaphore("es_" + e)) for e in self.ENGS}
        self.dcnt = {}

    def dma_sem(self, name):
        s = self.stack.enter_context(self.nc.semaphore(name))
        self.dcnt[id(s)] = 0
        return s

    def sb(self, name, shape, dt):
        return self.stack.enter_context(self.nc.sbuf_tensor("sb_" + name, list(shape), dt))

    def ps(self, name, shape, dt=F32):
        return self.stack.enter_context(self.nc.psum_tensor("psum_" + name, list(shape), dt))

    def _deps(self, eng, reads, writes):
        deps = []
        for b in reads:
            if b.w is not None:
                deps.append(b.w)
        for b in writes:
            if b.w is not None:
                deps.append(b.w)
            deps.extend(b.r.values())
        waits = []
        seen = self.seen[eng]
        for (sem, val, src) in deps:
            if src == eng and (eng == "pe" or not SAME_ENGINE_SYNC):
                continue
            k = id(sem)
            if seen.get(k, 0) >= val:
                continue
            seen[k] = val
            waits.append((sem, val))
        return waits

    def _commit(self, tok, reads, writes):
        for b in writes:
            b.w = tok
            b.r = {}
        for b in reads:
            if b not in writes:
                b.r[id(tok[0])] = tok

    def op(self, eng, fn, reads=(), writes=(), inc=True):
        waits = self._deps(eng, reads, writes)
        if inc:
            self.cnt[eng] += 1
            tok = (self.esem[eng], self.cnt[eng], eng)
        else:
            tok = (self.esem[eng], self.cnt[eng] + 1, eng)
        self.q[eng].append((fn, waits, (self.esem[eng], 1) if inc else None))
        self._commit(tok, reads, writes)
        return tok

    def dma(self, eng, sem, fn, reads=(), writes=()):
        waits = self._deps(eng, reads, writes)
        self.dcnt[id(sem)] += 16
        tok = (sem, self.dcnt[id(sem)], "dma")
        self.q[eng].append((fn, waits, (sem, 16)))
        self._commit(tok, reads, writes)
        return tok

    def wait_all(self, eng, bufs):
        waits = self._deps(eng, (), bufs)
        self.q[eng].append((None, waits, None))

    def replay(self, block):
        names = {"pe": "tensor", "act": "scalar", "dve": "vector", "pool": "gpsimd", "sp": "sync"}

        def run(e, engobj):
            for fn, waits, inc in self.q[e]:
                for sem, val in waits:
                    engobj.wait_ge(sem, val)
                if fn is None:
                    continue
                ins = fn(engobj)
                if inc is not None:
                    ins.then_inc(inc[0], inc[1])

        for e in self.ENGS:
            if not self.q[e]:
                continue

            def mk(e):
                def body(engobj):
                    run(e, engobj)
                return body
            getattr(block, names[e])(mk(e))


V_APRE, V_CW, V_CB, V_BGA, V_BGX, V_LAM, V_APOST, V_KVG, V_BPRE, V_BPOST = 0, 64, 320, 384, 448, 512, 576, 640, 672, 736
NV = 800
E0 = 0
E1 = 4096
E2 = 8192
ES = 10240
NE = 10240 + 48


def e0_off(pc, kh): return E0 + (pc * 4 + kh) * 512
def e1_off(half, pc, kh): return E1 + ((half * 2 + pc) * 4 + kh) * 256
def e2_off(b, kh): return E2 + (b * 4 + kh) * 64
def es_off(g, kh): return ES + (g * 4 + kh) * 4


def build():
    nc = bass.Bass("TRN2", target_bir_lowering=False)

    def D(name, shape, dt=F32, kind="ExternalInput"):
        return nc.dram_tensor(name, list(shape), dt, kind=kind).ap()

    xp = D("xp", [2048, 4096]); xs = D("xs", [1, 4096])
    sconv = D("sconv", [2, 3, 4096]); sh = D("sh", [2, 4096])
    skv = [D("skv0", [128, 1024]), D("skv1", [512, 1024]), D("skv2", [2048, 1024])]
    a_w_in = D("a_w_in", [2, 64, 128, 4096]); a_w_out = D("a_w_out", [2, 32, 128, 4096])
    a_wga = D("a_wga", [2, 16, 256, 256]); a_wgx = D("a_wgx", [2, 16, 256, 256])
    w_kv = D("w_kv", [24, 128, 4096])
    b_w_in = D("b_w_in", [2, 64, 128, 4096]); b_w_out = D("b_w_out", [2, 32, 128, 2048])
    vecs_d = D("vecs", [128, NV]); ein_d = D("ein", [128, NE]); em_d = D("em", [128, NE])
    enew_d = D("enew", [1, 48]); ident_d = D("ident", [128, 128])
    O = lambda n, s: D(n, s, kind="ExternalOutput")
    yp = O("yp", [2048, 4096]); ys = O("ys", [1, 4096])
    pconv = O("pconv", [2, 3, 4096]); ph = O("ph", [2, 4096]); pkv = O("pkv", [2048, 3072])
    sconv_o = O("sconv_o", [2, 3, 4096]); sh_o = O("sh_o", [2, 4096])
    skv_o = [O("skv_o0", [128, 1024]), O("skv_o1", [512, 1024]), O("skv_o2", [2048, 1024])]
    kvs = nc.dram_tensor("kvs", [2048 + 16, 3072], F32).ap()

    with contextlib.ExitStack() as st:
        P = Prog(nc, st)
        xT = P.sb("xT", [128, 32, T], F32); Bx = [Buf() for _ in range(32)]
        ub = P.sb("ub", [128, 32, T], BF16); Bu = [Buf() for _ in range(32)]
        hg = P.sb("hg", [128, 32, T], BF16); Bh = [Buf() for _ in range(32)]
        NW = 3
        wbf = [P.sb("wbf%d" % i, [128, 32, 128], BF16) for i in range(NW)]; Bw = [Buf() for _ in range(NW)]
        wsem = [P.dma_sem("wsem%d" % i) for i in range(NW)]
        wg = P.sb("wg", [128, 2, 2, 256], BF16); Bwg = Buf(); wgsem = P.dma_sem("wgsem")
        vecs = P.sb("vecs", [128, NV], F32); Bv = Buf()
        c1 = P.sb("c1", [128, 64], F32); Bc1 = Buf()
        Eall = P.sb("Eall", [128, NE], BF16); BE = Buf()
        enew = P.sb("enew", [1, 48], F32); Benew = Buf()
        ident = P.sb("ident", [128, 128], F32); Bid = Buf()
        ones = P.sb("ones", [128, 128], BF16); Bones = Buf()
        rs = P.sb("rs", [128, T], F32); Brs = Buf()
        sqb = [P.sb("sqb%d" % i, [128, T], BF16) for i in range(2)]; Bsq = [Buf(), Buf()]
        stt = [P.sb("st%d" % l, [128, 4, 32], F32) for l in range(2)]; Bst = [Buf(), Buf()]
        stT = P.sb("stT", [128, 128], F32); BstT = Buf()
        xc = [P.sb("xc%d" % i, [128, 3 + T], F32) for i in range(2)]; Bxc = [Buf(), Buf()]
        xv = [P.sb("xv%d" % i, [128, T], F32) for i in range(2)]; Bxv = [Buf(), Buf()]
        xvb = [P.sb("xvb%d" % i, [128, T], BF16) for i in range(2)]; Bxvb = [Buf(), Buf()]
        rg = [[P.sb("rg%d%d" % (a, j), [128, T], F32) for j in range(2)] for a in range(2)]
        Brg = [[Buf(), Buf()], [Buf(), Buf()]]
        ta = P.sb("ta", [128, T], F32); Bta = Buf()
        tb = P.sb("tb", [128, T], F32); Btb = Buf()
        tcc = P.sb("tcc", [128, T], F32); Btc = Buf()
        hh = [P.sb("hh%d" % i, [128, T], F32) for i in range(2)]; Bhh = [Buf(), Buf()]
        sg = P.sb("sg", [128, T], F32); Bsg = Buf()
        tmpx = P.sb("tmpx", [128, T], F32); Btmpx = Buf()
        xio = P.sb("xio", [128, 4096], F32); Bxio = Buf(); xiosem = P.dma_sem("xiosem")
        kvt = [P.sb("kvt%d" % i, [128, 3072], F32) for i in range(2)]; Bkvt = [Buf(), Buf()]
        kvosem = P.dma_sem("kvosem")
        q4 = P.sb("q4", [128, 4, T], BF16); Bq4 = Buf()
        num = P.sb("num", [128, 4, T], F32); Bnum = Buf()
        den = P.sb("den", [128, 4, T], F32); Bden = Buf()
        kvin = [P.sb("kvin%d" % i, [128, 2, 128], F32) for i in range(2)]; Bkvin = [Buf(), Buf()]
        kvinsem = [P.dma_sem("kvinsem%d" % i) for i in range(2)]
        kTb = [P.sb("kTb%d" % i, [128, 128], BF16) for i in range(2)]; BkT = [Buf(), Buf()]
        vbb = [P.sb("vbb%d" % i, [128, 128], BF16) for i in range(2)]; Bvb = [Buf(), Buf()]
        pT = [P.sb("pT%d" % i, [128, 512], BF16) for i in range(2)]; BpT = [Buf(), Buf()]
        esb = P.sb("esb", [128, 1024], F32); Besb = Buf()
        emb = P.sb("emb", [128, 1024], F32); Bemb = Buf()
        misc = P.sb("misc", [128, 64], F32); Bmisc = Buf()
        knew = P.sb("knew", [128, 16], BF16); Bknew = Buf()
        vnew = P.sb("vnew", [1, 128], BF16); Bvnew = Buf()
        pnew = P.sb("pnew", [1, 16], BF16); Bpnew = Buf()
        ps = P.ps("ps", [128, 8, 512], F32); Bps = [Buf() for _ in range(8)]
        setsem = [P.dma_sem("setsem%d" % i) for i in range(6)]
        outsem = P.dma_sem("outsem"); Bout = Buf()
        d2dsem = P.dma_sem("d2dsem"); Bd2d = Buf()

        P.dma("sp", setsem[0], lambda e: e.dma_start(out=vecs[:, :], in_=vecs_d), writes=[Bv])
        P.dma("sp", setsem[1], lambda e: e.dma_start(out=enew[:, :], in_=enew_d), writes=[Benew])
        P.dma("sp", setsem[5], lambda e: e.dma_start(out=ident[:, :], in_=ident_d), writes=[Bid])
        P.op("pool", lambda e: e.memset(ones[:, :], 1.0), writes=[Bones])
        for i in range(2):
            P.op("pool", lambda e, i=i: e.memset(kvin[i][:, :, :], 0.0), writes=[Bkvin[i]])
            P.op("pool", lambda e, i=i: e.memset(kvt[i][:, :], 0.0), writes=[Bkvt[i]])
        for l in range(2):
            P.op("pool", lambda e, l=l: e.memset(stt[l][:, :, :], 0.0), writes=[Bst[l]])
        P.op("act", lambda e: e.activation(out=c1[:, :], in_=vecs[:, V_LAM:V_LAM + 64], func=AF.Exp, scale=-1.0), reads=[Bv], writes=[Bc1])
        P.op("act", lambda e: e.activation(out=c1[:, :], in_=c1[:, :], func=AF.Ln, bias=1.0, scale=1.0), reads=[Bc1], writes=[Bc1])
        P.op("dve", lambda e: e.tensor_scalar(out=c1[:, :], in0=c1[:, :], scalar1=-8.0, scalar2=None, op0=ALU.mult), reads=[Bc1], writes=[Bc1])
        for c0 in range(0, NE, 1024):
            cw = min(1024, NE - c0)
            P.dma("sp", setsem[2], lambda e, c0=c0, cw=cw: e.dma_start(out=esb[:, 0:cw], in_=ein_d[:, c0:c0 + cw]), writes=[Besb])
            P.dma("sp", setsem[3], lambda e, c0=c0, cw=cw: e.dma_start(out=emb[:, 0:cw], in_=em_d[:, c0:c0 + cw]), writes=[Bemb])
            P.op("act", lambda e, cw=cw: e.activation(out=esb[:, 0:cw], in_=esb[:, 0:cw], func=AF.Exp), reads=[Besb], writes=[Besb])
            P.op("dve", lambda e, c0=c0, cw=cw: e.tensor_tensor(out=Eall[:, c0:c0 + cw], in0=esb[:, 0:cw], in1=emb[:, 0:cw], op=ALU.mult),
                 reads=[Besb, Bemb], writes=[BE])
        P.op("act", lambda e: e.activation(out=enew[:, :], in_=enew[:, :], func=AF.Exp), reads=[Benew], writes=[Benew])

        wctr = [0]
        pctr = [0]

        def wload(Wsl, nk, col0):
            s_ = wctr[0] % NW; wctr[0] += 1
            P.dma("pool", wsem[s_], lambda e: e.dma_start(
                out=wbf[s_][:, 0:nk, :].rearrange("p k m -> p (k m)"), in_=Wsl[col0 // 128], max_dma_last_dim=8192), writes=[Bw[s_]])
            return s_

        def lin(Wsl, nk, cols, src, Bsrc, Tn, consumer):
            slots = {}
            for i0 in range(min(2, len(cols))):
                slots[i0] = wload(Wsl, nk, cols[i0])
            for idx, col0 in enumerate(cols):
                s = slots.pop(idx)
                pb = pctr[0] % 3; pctr[0] += 1
                for k in range(nk):
                    P.op("pe", lambda e, s=s, k=k, pb=pb: e.matmul(ps[:, pb, 0:Tn], lhsT=wbf[s][:, k, :], rhs=src[:, k, 0:Tn],
                                                                  start=(k == 0), stop=(k == nk - 1)),
                         reads=[Bw[s], Bsrc[k]], writes=[Bps[pb]], inc=(k == nk - 1))
                if idx + 2 < len(cols):
                    slots[idx + 2] = wload(Wsl, nk, cols[idx + 2])
                consumer(idx, pb)

        def rstd_from_ps3(Tn):
            P.op("act", lambda e: e.activation(out=rs[:, 0:Tn], in_=ps[:, 3, 0:Tn], func=AF.Sqrt, bias=EPS, scale=1.0 / 4096), reads=[Bps[3]], writes=[Brs])
            P.op("dve", lambda e: e.reciprocal(out=rs[:, 0:Tn], in_=rs[:, 0:Tn]), reads=[Brs], writes=[Brs])

        def norm_pre(gcol, Tn):
            for k in range(32):
                i = k % 2
                P.op("dve", lambda e, k=k, i=i: e.tensor_tensor(out=sqb[i][:, 0:Tn], in0=xT[:, k, 0:Tn], in1=xT[:, k, 0:Tn], op=ALU.mult),
                     reads=[Bx[k]], writes=[Bsq[i]])
                P.op("pe", lambda e, k=k, i=i: e.matmul(ps[:, 3, 0:Tn], lhsT=ones[:, :], rhs=sqb[i][:, 0:Tn], start=(k == 0), stop=(k == 31)),
                     reads=[Bsq[i], Bones], writes=[Bps[3]], inc=True)
            rstd_from_ps3(Tn)
            for k in range(32):
                P.op("dve", lambda e, k=k: e.scalar_tensor_tensor(out=ub[:, k, 0:Tn], in0=xT[:, k, 0:Tn], scalar=vecs[:, gcol + k:gcol + k + 1],
                                                                  in1=rs[:, 0:Tn], op0=ALU.mult, op1=ALU.mult),
                     reads=[Bx[k], Brs, Bv], writes=[Bu[k]])

        def out_consumer(Tn):
            def cons(m, pb):
                i = m % 2
                P.op("act", lambda e, m=m, pb=pb: e.activation(out=ub[:, m, 0:Tn], in_=ps[:, pb, 0:Tn], func=AF.Copy), reads=[Bps[pb]], writes=[Bu[m]])
                P.op("act", lambda e, i=i, pb=pb: e.activation(out=sqb[i][:, 0:Tn], in_=ps[:, pb, 0:Tn], func=AF.Square), reads=[Bps[pb]], writes=[Bsq[i]])
                P.op("pe", lambda e, m=m, i=i: e.matmul(ps[:, 3, 0:Tn], lhsT=ones[:, :], rhs=sqb[i][:, 0:Tn], start=(m == 0), stop=(m == 31)),
                     reads=[Bsq[i], Bones], writes=[Bps[3]], inc=True)
            return cons

        def post(gcol, Tn):
            rstd_from_ps3(Tn)
            for k in range(32):
                P.op("dve", lambda e, k=k: e.scalar_tensor_tensor(out=tmpx[:, 0:Tn], in0=ub[:, k, 0:Tn], scalar=vecs[:, gcol + k:gcol + k + 1],
                                                                  in1=rs[:, 0:Tn], op0=ALU.mult, op1=ALU.mult),
                     reads=[Bu[k], Brs, Bv], writes=[Btmpx])
                P.op("dve", lambda e, k=k: e.tensor_tensor(out=xT[:, k, 0:Tn], in0=xT[:, k, 0:Tn], in1=tmpx[:, 0:Tn], op=ALU.add),
                     reads=[Btmpx, Bx[k]], writes=[Bx[k]])

        def a_layer(l, Tn):
            norm_pre(V_APRE + 32 * l, Tn)
            S = stt[l]; BS = Bst[l]
            cols = []
            for h in range(16):
                cols += [(2 * h) * 128, (2 * h + 1) * 128, 4096 + (2 * h) * 128, 4096 + (2 * h + 1) * 128]

            def cons(idx, pb):
                h, j = idx // 4, idx % 4
                if j < 2:
                    c = 2 * h + j
                    P.op("dve", lambda e: e.tensor_copy(out=xc[j][:, 0:3], in_=S[:, 0:3, c]), reads=[BS], writes=[Bxc[j]])
                    P.op("act", lambda e: e.activation(out=xc[j][:, 3:3 + Tn], in_=ps[:, pb, 0:Tn], func=AF.Copy), reads=[Bps[pb]], writes=[Bxc[j]])
                    P.op("dve", lambda e: e.tensor_copy(out=S[:, 0:3, c], in_=xc[j][:, Tn:Tn + 3]), reads=[Bxc[j]], writes=[BS])
                    cwc = V_CW + l * 128
                    P.op("dve", lambda e: e.tensor_scalar(out=xv[j][:, 0:Tn], in0=xc[j][:, 3:3 + Tn], scalar1=vecs[:, cwc + 96 + c:cwc + 97 + c],
                                                          scalar2=vecs[:, V_CB + 32 * l + c:V_CB + 32 * l + c + 1], op0=ALU.mult, op1=ALU.add),
                         reads=[Bxc[j], Bv], writes=[Bxv[j]])
                    for kk in range(3):
                        P.op("dve", lambda e, kk=kk: e.scalar_tensor_tensor(out=xv[j][:, 0:Tn], in0=xc[j][:, kk:kk + Tn],
                                                                            scalar=vecs[:, cwc + 32 * kk + c:cwc + 32 * kk + c + 1],
                                                                            in1=xv[j][:, 0:Tn], op0=ALU.mult, op1=ALU.add),
                             reads=[Bxc[j], Bv, Bxv[j]], writes=[Bxv[j]])
                    P.op("pool", lambda e: e.tensor_copy(out=xvb[j][:, 0:Tn], in_=xv[j][:, 0:Tn]), reads=[Bxv[j]], writes=[Bxvb[j]])
                    if j == 1:
                        P.dma("pool", wgsem, lambda e: e.dma_start(out=wg[:, 0, :, :], in_=a_wga[l, h].rearrange("(ic p) o -> p ic o", p=128)), writes=[Bwg])
                        P.dma("pool", wgsem, lambda e: e.dma_start(out=wg[:, 1, :, :], in_=a_wgx[l, h].rearrange("(ic p) o -> p ic o", p=128)), writes=[Bwg])
                        for a in range(2):
                            bcol = (V_BGA if a == 0 else V_BGX) + 32 * l
                            for jo in range(2):
                                gb = 4 + (a * 2 + jo) % 2
                                for ic in range(2):
                                    P.op("pe", lambda e, a=a, jo=jo, ic=ic, gb=gb: e.matmul(ps[:, gb, 0:Tn], lhsT=wg[:, a, ic, jo * 128:(jo + 1) * 128],
                                                                                            rhs=xvb[ic][:, 0:Tn], start=(ic == 0), stop=(ic == 1)),
                                         reads=[Bwg, Bxvb[ic]], writes=[Bps[gb]], inc=(ic == 1))
                                cc = 2 * h + jo
                                P.op("act", lambda e, a=a, jo=jo, gb=gb, cc=cc, bcol=bcol: e.activation(
                                    out=rg[a][jo][:, 0:Tn], in_=ps[:, gb, 0:Tn], func=AF.Sigmoid, bias=vecs[:, bcol + cc:bcol + cc + 1], scale=1.0),
                                    reads=[Bps[gb], Bv], writes=[Brg[a][jo]])
                        for jo in range(2):
                            cc = 2 * h + jo
                            P.op("act", lambda e, jo=jo, cc=cc: e.activation(out=ta[:, 0:Tn], in_=rg[0][jo][:, 0:Tn], func=AF.Exp,
                                                                             scale=c1[:, 32 * l + cc:32 * l + cc + 1]),
                                 reads=[Brg[0][jo], Bc1], writes=[Bta])
                            P.op("dve", lambda e: e.tensor_tensor(out=tb[:, 0:Tn], in0=ta[:, 0:Tn], in1=ta[:, 0:Tn], op=ALU.mult), reads=[Bta], writes=[Btb])
                            P.op("act", lambda e: e.activation(out=tb[:, 0:Tn], in_=tb[:, 0:Tn], func=AF.Sqrt, bias=1.0, scale=-1.0), reads=[Btb], writes=[Btb])
                            P.op("dve", lambda e, jo=jo: e.tensor_tensor(out=tcc[:, 0:Tn], in0=rg[1][jo][:, 0:Tn], in1=xv[jo][:, 0:Tn], op=ALU.mult),
                                 reads=[Brg[1][jo], Bxv[jo]], writes=[Btc])
                            P.op("dve", lambda e: e.tensor_tensor(out=tcc[:, 0:Tn], in0=tcc[:, 0:Tn], in1=tb[:, 0:Tn], op=ALU.mult), reads=[Btc, Btb], writes=[Btc])
                            P.op("dve", lambda e, jo=jo, cc=cc: e.tensor_tensor_scan(out=hh[jo][:, 0:Tn], data0=ta[:, 0:Tn], data1=tcc[:, 0:Tn],
                                                                                     initial=S[:, 3, cc:cc + 1], op0=ALU.mult, op1=ALU.add),
                                 reads=[Bta, Btc, BS], writes=[Bhh[jo]])
                            P.op("dve", lambda e, jo=jo, cc=cc: e.tensor_copy(out=S[:, 3, cc:cc + 1], in_=hh[jo][:, Tn - 1:Tn]), reads=[Bhh[jo]], writes=[BS])
                else:
                    jo = j - 2
                    c = 2 * h + jo
                    P.op("act", lambda e: e.activation(out=sg[:, 0:Tn], in_=ps[:, pb, 0:Tn], func=AF.Silu), reads=[Bps[pb]], writes=[Bsg])
                    P.op("dve", lambda e: e.tensor_tensor(out=hg[:, c, 0:Tn], in0=hh[jo][:, 0:Tn], in1=sg[:, 0:Tn], op=ALU.mult),
                         reads=[Bhh[jo], Bsg], writes=[Bh[c]])

            lin(a_w_in[l], 32, cols, ub, Bu, Tn, cons)
            lin(a_w_out[l], 32, [m * 128 for m in range(32)], hg, Bh, Tn, out_consumer(Tn))
            post(V_APOST + 32 * l, Tn)

        def kv_phase(row0, Tn, sample):
            norm_pre(V_KVG, Tn)
            ntc = max(1, Tn // 128)
            M = min(128, Tn)
            slots = {0: wload(w_kv, 32, 0), 1: wload(w_kv, 32, 128)}
            for n in range(24):
                s = slots.pop(n)
                if n + 2 < 24 and n >= 1:
                    pass
                for tc in range(ntc):
                    pb = pctr[0] % 3; pctr[0] += 1
                    for k in range(32):
                        P.op("pe", lambda e, s=s, k=k, pb=pb, tc=tc: e.matmul(ps[0:M, pb, 0:128], lhsT=ub[:, k, tc * 128:tc * 128 + M], rhs=wbf[s][:, k, :],
                                                                              start=(k == 0), stop=(k == 31)),
                             reads=[Bw[s], Bu[k]], writes=[Bps[pb]], inc=(k == 31))
                    P.op("act", lambda e, pb=pb, tc=tc, n=n: e.activation(out=kvt[tc][0:M, n * 128:(n + 1) * 128], in_=ps[0:M, pb, 0:128], func=AF.Copy),
                         reads=[Bps[pb]], writes=[Bkvt[tc]])
                if n + 2 < 24:
                    slots[n + 2] = wload(w_kv, 32, (n + 2) * 128)
            if not sample:
                for tc in range(ntc):
                    r0 = row0 + tc * 128
                    P.dma("sp", kvosem, lambda e, tc=tc, r0=r0: e.dma_start(out=pkv[r0:r0 + 128, :], in_=kvt[tc][:, :]), reads=[Bkvt[tc]], writes=[Bout])
                    P.dma("sp", kvosem, lambda e, tc=tc, r0=r0: e.dma_start(out=kvs[r0:r0 + 128, :], in_=kvt[tc][:, :]), reads=[Bkvt[tc]], writes=[Bd2d])

        uctr = [0]

        def key_tile(src_rows_ap, nrows, g, kh):
            i = uctr[0] % 2; uctr[0] += 1
            P.dma("sp", kvinsem[i], lambda e: e.dma_start(out=kvin[i][0:nrows, :, :], in_=src_rows_ap), reads=[Bd2d], writes=[Bkvin[i]])
            tb_ = 4 + i
            P.op("pe", lambda e: e.transpose(out=ps[:, tb_, 0:128], in_=kvin[i][:, 0, :], identity=ident[:, :]), reads=[Bkvin[i], Bid], writes=[Bps[tb_]])
            P.op("act", lambda e: e.activation(out=kTb[i][:, :], in_=ps[:, tb_, 0:128], func=AF.Copy), reads=[Bps[tb_]], writes=[BkT[i]])
            P.op("pool", lambda e: e.tensor_copy(out=vbb[i][:, :], in_=kvin[i][:, 1, :]), reads=[Bkvin[i]], writes=[Bvb[i]])
            return i

        def attn_unit(qap, nq, tiles, acc_first, accap_fn):
            W = 4 * nq
            for ti, (i, eoff) in enumerate(tiles):
                sb_ = 4 + i
                P.op("pe", lambda e, i=i, sb_=sb_: e.matmul(ps[:, sb_, 0:W].rearrange("p (s q) -> p s q", s=4), lhsT=kTb[i][:, :], rhs=qap, start=True, stop=True),
                     reads=[BkT[i], Bq4], writes=[Bps[sb_]])
                P.op("act", lambda e, i=i, sb_=sb_: e.activation(out=pT[i][:, 0:W], in_=ps[:, sb_, 0:W], func=AF.Exp, scale=SCALE), reads=[Bps[sb_]], writes=[BpT[i]])
                P.op("dve", lambda e, i=i, eoff=eoff: e.tensor_tensor(out=pT[i][:, 0:W], in0=pT[i][:, 0:W], in1=Eall[:, eoff:eoff + W], op=ALU.mult),
                     reads=[BpT[i], BE], writes=[BpT[i]])
                first, last = ti == 0, ti == len(tiles) - 1
                P.op("pe", lambda e, i=i, first=first, last=last: e.matmul(ps[:, 6, 0:W], lhsT=vbb[i][:, :], rhs=pT[i][:, 0:W], start=first, stop=last),
                     reads=[Bvb[i], BpT[i]], writes=[Bps[6]])
                P.op("pe", lambda e, i=i, first=first, last=last: e.matmul(ps[:, 7, 0:W], lhsT=ones[:, :], rhs=pT[i][:, 0:W], start=first, stop=last),
                     reads=[Bones, BpT[i]], writes=[Bps[7]])
            for (acc, Bacc, bank) in ((num, Bnum, 6), (den, Bden, 7)):
                src = ps[:, bank, 0:W].rearrange("p (s q) -> p s q", s=4)
                if acc_first:
                    P.op("act", lambda e, acc=acc, src=src: e.activation(out=accap_fn(acc), in_=src, func=AF.Copy), reads=[Bps[bank]], writes=[Bacc])
                else:
                    P.op("dve", lambda e, acc=acc, src=src: e.tensor_tensor(out=accap_fn(acc), in0=accap_fn(acc), in1=src, op=ALU.add),
                         reads=[Bps[bank], Bacc], writes=[Bacc])

        def attention_prompt(b, g, kh):
            t0 = b * T
            kc = g * 1024 + kh * 128

            def rows(start, step, count):
                v = kvs[start:start + step * count, :].rearrange("(j c) n -> j c n", c=step)[:, 0, :]
                return v[:, g * 1024:(g + 1) * 1024].rearrange("j (v k x) -> j v k x", v=2, k=4)[:, :, kh, :]

            if g == 0:
                for qb in range(T // 128):
                    tiles = []
                    if t0 + qb * 128 - 128 >= 0:
                        i = key_tile(rows(t0 + qb * 128 - 128, 1, 128), 128, g, kh)
                        tiles.append((i, e0_off(1, kh)))
                    i = key_tile(rows(t0 + qb * 128, 1, 128), 128, g, kh)
                    tiles.append((i, e0_off(0, kh)))
                    qap = q4[:, :, qb * 128:(qb + 1) * 128]
                    attn_unit(qap, 128, tiles, True, lambda acc, qb=qb: acc[:, :, qb * 128:(qb + 1) * 128])
            elif g == 1:
                nbi, half = b // 2, b % 2
                for c in range(4):
                    tiles = []
                    if nbi > 0:
                        i = key_tile(rows(512 * (nbi - 1) + c, 4, 128), 128, g, kh)
                        tiles.append((i, e1_off(half, 1, kh)))
                    cnt = 64 * (half + 1)
                    i = key_tile(rows(512 * nbi + c, 4, cnt), cnt, g, kh)
                    tiles.append((i, e1_off(half, 0, kh)))
                    qap = q4[:, :, :].rearrange("p s (j c) -> p s c j", c=4)[:, :, c, :]
                    attn_unit(qap, 64, tiles, False, lambda acc, c=c: acc[:, :, :].rearrange("p s (j c) -> p s c j", c=4)[:, :, c, :])
            else:
                for c in range(16):
                    cnt = 16 * (b + 1)
                    i = key_tile(rows(c, 16, cnt), cnt, g, kh)
                    qap = q4[:, :, :].rearrange("p s (j c) -> p s c j", c=16)[:, :, c, :]
                    attn_unit(qap, 16, [(i, e2_off(b, kh))], False, lambda acc, c=c: acc[:, :, :].rearrange("p s (j c) -> p s c j", c=16)[:, :, c, :])

        def attention_sample(g, kh):
            r = DIL[g]
            lb = 128 * r
            src = skv[g].rearrange("(j c) (v x) -> j c v x", c=r, v=2)[:, 0, :, kh * 128:(kh + 1) * 128]
            i = uctr[0] % 2; uctr[0] += 1
            P.dma("sp", kvinsem[i], lambda e: e.dma_start(out=kvin[i][:, :, :], in_=src), writes=[Bkvin[i]])
            tb_ = 4 + i
            P.op("pe", lambda e: e.transpose(out=ps[:, tb_, 0:128], in_=kvin[i][:, 0, :], identity=ident[:, :]), reads=[Bkvin[i], Bid], writes=[Bps[tb_]])
            P.op("act", lambda e: e.activation(out=kTb[i][:, :], in_=ps[:, tb_, 0:128], func=AF.Copy), reads=[Bps[tb_]], writes=[BkT[i]])
            P.op("pool", lambda e: e.tensor_copy(out=vbb[i][:, :], in_=kvin[i][:, 1, :]), reads=[Bkvin[i]], writes=[Bvb[i]])
            qap = q4[:, :, 0:1]
            P.op("pe", lambda e: e.matmul(ps[:, tb_, 0:4].rearrange("p (s q) -> p s q", s=4), lhsT=kTb[i][:, :], rhs=qap, start=True, stop=True),
                 reads=[BkT[i], Bq4], writes=[Bps[tb_]])
            P.op("act", lambda e: e.activation(out=pT[i][:, 0:4], in_=ps[:, tb_, 0:4], func=AF.Exp, scale=SCALE), reads=[Bps[tb_]], writes=[BpT[i]])
            eo = es_off(g, kh)
            P.op("dve", lambda e: e.tensor_tensor(out=pT[i][:, 0:4], in0=pT[i][:, 0:4], in1=Eall[:, eo:eo + 4], op=ALU.mult), reads=[BpT[i], BE], writes=[BpT[i]])
            P.op("pe", lambda e: e.matmul(ps[:, 6, 0:4], lhsT=vbb[i][:, :], rhs=pT[i][:, 0:4], start=True, stop=False), reads=[Bvb[i], BpT[i]], writes=[Bps[6]])
            P.op("pe", lambda e: e.matmul(ps[:, 7, 0:4], lhsT=ones[:, :], rhs=pT[i][:, 0:4], start=True, stop=False), reads=[Bones, BpT[i]], writes=[Bps[7]])
            kc = g * 1024 + kh * 128
            P.op("pe", lambda e: e.transpose(out=ps[:, tb_, 8:9], in_=kvt[0][0:1, kc:kc + 128], identity=ident[0:1, 0:1]), reads=[Bkvt[0], Bid], writes=[Bps[tb_]])
            P.op("act", lambda e: e.activation(out=knew[:, 0:1], in_=ps[:, tb_, 8:9], func=AF.Copy), reads=[Bps[tb_]], writes=[Bknew])
            P.op("pool", lambda e: e.tensor_copy(out=vnew[0:1, 0:128], in_=kvt[0][0:1, kc + 512:kc + 640]), reads=[Bkvt[0]], writes=[Bvnew])
            P.op("pe", lambda e: e.matmul(ps[0:1, tb_, 16:20].rearrange("p (s q) -> p s q", s=4), lhsT=knew[:, 0:1], rhs=qap, start=True, stop=True),
                 reads=[Bknew, Bq4], writes=[Bps[tb_]])
            P.op("act", lambda e: e.activation(out=misc[0:1, 0:4], in_=ps[0:1, tb_, 16:20], func=AF.Exp, scale=SCALE), reads=[Bps[tb_]], writes=[Bmisc])
            en = (g * 4 + kh) * 4
            P.op("dve", lambda e: e.tensor_tensor(out=pnew[0:1, 0:4], in0=misc[0:1, 0:4], in1=enew[0:1, en:en + 4], op=ALU.mult), reads=[Bmisc, Benew], writes=[Bpnew])
            P.op("pe", lambda e: e.matmul(ps[:, 6, 0:4], lhsT=vnew[0:1, 0:128], rhs=pnew[0:1, 0:4], start=False, stop=True), reads=[Bvnew, Bpnew], writes=[Bps[6]])
            P.op("pe", lambda e: e.matmul(ps[:, 7, 0:4], lhsT=ones[0:1, :], rhs=pnew[0:1, 0:4], start=False, stop=True), reads=[Bones, Bpnew], writes=[Bps[7]])
            for (acc, Bacc, bank) in ((num, Bnum, 6), (den, Bden, 7)):
                src2 = ps[:, bank, 0:4].rearrange("p (s q) -> p s q", s=4)
                if g == 0:
                    P.op("act", lambda e, acc=acc, src2=src2: e.activation(out=acc[:, :, 0:1], in_=src2, func=AF.Copy), reads=[Bps[bank]], writes=[Bacc])
                else:
                    P.op("dve", lambda e, acc=acc, src2=src2: e.tensor_tensor(out=acc[:, :, 0:1], in0=acc[:, :, 0:1], in1=src2, op=ALU.add),
                         reads=[Bps[bank], Bacc], writes=[Bacc])

        def b_layer(l, Tn, b, sample):
            norm_pre(V_BPRE + 32 * l, Tn)
            cols = []
            for kh in range(4):
                for g in range(3):
                    cols += [g * 2048 + (4 * kh + s) * 128 for s in range(4)]
                cols += [6144 + (4 * kh + s) * 128 for s in range(4)]

            def cons(idx, pb):
                kh, j = idx // 16, idx % 16
                if j < 12:
                    g, s = j // 4, j % 4
                    P.op("act", lambda e: e.activation(out=q4[:, s, 0:Tn], in_=ps[:, pb, 0:Tn], func=AF.Copy), reads=[Bps[pb]], writes=[Bq4])
                    if s == 3:
                        if sample:
                            attention_sample(g, kh)
                        else:
                            attention_prompt(b, g, kh)
                else:
                    s = j - 12
                    if s == 0:
                        P.op("dve", lambda e: e.reciprocal(out=den[:, :, 0:Tn], in_=den[:, :, 0:Tn]), reads=[Bden], writes=[Bden])
                        P.op("dve", lambda e: e.tensor_tensor(out=num[:, :, 0:Tn], in0=num[:, :, 0:Tn], in1=den[:, :, 0:Tn], op=ALU.mult),
                             reads=[Bnum, Bden], writes=[Bnum])
                    P.op("act", lambda e: e.activation(out=sg[:, 0:Tn], in_=ps[:, pb, 0:Tn], func=AF.Silu), reads=[Bps[pb]], writes=[Bsg])
                    P.op("dve", lambda e: e.tensor_tensor(out=hg[:, 4 * kh + s, 0:Tn], in0=num[:, s, 0:Tn], in1=sg[:, 0:Tn], op=ALU.mult),
                         reads=[Bnum, Bsg], writes=[Bh[4 * kh + s]])

            lin(b_w_in[l], 32, cols, ub, Bu, Tn, cons)
            lin(b_w_out[l], 16, [m * 128 for m in range(32)], hg, Bh, Tn, out_consumer(Tn))
            post(V_BPOST + 32 * l, Tn)

        def load_x(src_rows, M, col0):
            P.dma("sp", xiosem, lambda e: e.dma_start(out=xio[0:M, :], in_=src_rows), writes=[Bxio])
            for k in range(32):
                tb_ = 4 + k % 2
                P.op("pe", lambda e, k=k, tb_=tb_: e.transpose(out=ps[:, tb_, 0:M], in_=xio[0:M, k * 128:(k + 1) * 128], identity=ident[0:M, 0:M]),
                     reads=[Bxio, Bid], writes=[Bps[tb_]])
                P.op("act", lambda e, k=k, tb_=tb_: e.activation(out=xT[:, k, col0:col0 + M], in_=ps[:, tb_, 0:M], func=AF.Copy), reads=[Bps[tb_]], writes=[Bx[k]])

        def store_x(dst_rows, M, col0):
            for k in range(32):
                tb_ = 4 + k % 2
                P.op("pe", lambda e, k=k, tb_=tb_: e.transpose(out=ps[0:M, tb_, 0:128], in_=xT[:, k, col0:col0 + M], identity=ident[:, :]),
                     reads=[Bx[k], Bid], writes=[Bps[tb_]])
                P.op("act", lambda e, k=k, tb_=tb_: e.activation(out=xio[0:M, k * 128:(k + 1) * 128], in_=ps[0:M, tb_, 0:128], func=AF.Copy),
                     reads=[Bps[tb_]], writes=[Bxio])
            P.dma("sp", xiosem, lambda e: e.dma_start(out=dst_rows, in_=xio[0:M, :]), reads=[Bxio], writes=[Bout])

        def store_states(conv_o, h_o):
            for l in range(2):
                P.op("pe", lambda e, l=l: e.transpose(out=ps[:, 4, 0:128], in_=stt[l][:, :, :].rearrange("p j c -> p (j c)"), identity=ident[:, :]),
                     reads=[Bst[l], Bid], writes=[Bps[4]])
                P.op("act", lambda e: e.activation(out=stT[:, :], in_=ps[:, 4, 0:128], func=AF.Copy), reads=[Bps[4]], writes=[BstT])
                for j in range(3):
                    P.dma("sp", outsem, lambda e, l=l, j=j: e.dma_start(out=conv_o[l, j].rearrange("(c p) -> c p", p=128), in_=stT[j * 32:(j + 1) * 32, :]),
                          reads=[BstT], writes=[Bout])
                P.dma("sp", outsem, lambda e, l=l: e.dma_start(out=h_o[l].rearrange("(c p) -> c p", p=128), in_=stT[96:128, :]), reads=[BstT], writes=[Bout])

        def load_states():
            for l in range(2):
                for j in range(3):
                    P.dma("sp", setsem[4], lambda e, l=l, j=j: e.dma_start(out=stT[j * 32:(j + 1) * 32, :], in_=sconv[l, j].rearrange("(c p) -> c p", p=128)), writes=[BstT])
                P.dma("sp", setsem[4], lambda e, l=l: e.dma_start(out=stT[96:128, :], in_=sh[l].rearrange("(c p) -> c p", p=128)), writes=[BstT])
                P.op("pe", lambda e, l=l: e.transpose(out=ps[:, 4, 0:128], in_=stT[:, :], identity=ident[:, :]), reads=[BstT, Bid], writes=[Bps[4]])
                P.op("act", lambda e, l=l: e.activation(out=stt[l][:, :, :].rearrange("p j c -> p (j c)"), in_=ps[:, 4, 0:128], func=AF.Copy),
                     reads=[Bps[4]], writes=[Bst[l]])

        for b in range(NB):
            for tc in range(T // 128):
                load_x(xp[b * T + tc * 128:b * T + (tc + 1) * 128, :], 128, tc * 128)
            a_layer(0, T)
            a_layer(1, T)
            kv_phase(b * T, T, False)
            b_layer(0, T, b, False)
            b_layer(1, T, b, False)
            for tc in range(T // 128):
                store_x(yp[b * T + tc * 128:b * T + (tc + 1) * 128, :], 128, tc * 128)
        store_states(pconv, ph)
        load_states()
        load_x(xs[0:1, :], 1, 0)
        a_layer(0, 1)
        a_layer(1, 1)
        kv_phase(0, 1, True)
        for g in range(3):
            lbg = 128 * DIL[g]
            P.dma("sp", d2dsem, lambda e, g=g, lbg=lbg: e.dma_start(out=skv_o[g][0:lbg - 1, :], in_=skv[g][1:lbg, :]), writes=[Bout])
            P.dma("sp", d2dsem, lambda e, g=g, lbg=lbg: e.dma_start(out=skv_o[g][lbg - 1:lbg, :], in_=kvt[0][0:1, g * 1024:(g + 1) * 1024]),
                  reads=[Bkvt[0]], writes=[Bout])
        b_layer(0, 1, 0, True)
        b_layer(1, 1, 0, True)
        store_x(ys[0:1, :], 1, 0)
        store_states(sconv_o, sh_o)
        fin = [(sm, P.dcnt[id(sm)]) for sm in (kvosem, outsem, xiosem, d2dsem) if P.dcnt[id(sm)] > 0]
        P.q["sp"].append((None, fin, None))
        with nc.Block() as block:
            P.replay(block)
    return nc


def _rel_bucket(dist):
    dist = np.asarray(dist)
    d = np.maximum(dist, 1).astype(np.float32)
    large = 16 + (np.log(d / 16) / np.log(2048 / 16) * (32 - 16)).astype(np.int32)
    large = np.minimum(large, 31)
    return np.where(dist < 16, dist, large).astype(np.int32)


def _etiles(rel_bias):
    ein = np.zeros((128, NE), np.float32)
    em = np.zeros((128, NE), np.float32)
    enew = np.zeros((1, 48), np.float32)
    p = np.arange(128)[:, None]

    def fill(off, g, kh, diff, valid):
        nq = diff.shape[1]
        bk = _rel_bucket(np.clip(diff, 0, 128) * DIL[g])
        for s in range(4):
            col = g * 16 + 4 * kh + s
            ein[:, off + s * nq:off + (s + 1) * nq] = rel_bias[bk, col]
            em[:, off + s * nq:off + (s + 1) * nq] = valid.astype(np.float32)

    for kh in range(4):
        q = np.arange(128)[None, :]
        fill(e0_off(0, kh), 0, kh, q - p, (q - p) >= 0)
        fill(e0_off(1, kh), 0, kh, 128 + q - p, (128 + q - p) <= 128)
        q = np.arange(64)[None, :]
        for half in range(2):
            d = 64 * half + q - p
            fill(e1_off(half, 0, kh), 1, kh, d, d >= 0)
            d = 128 + 64 * half + q - p
            fill(e1_off(half, 1, kh), 1, kh, d, d <= 128)
        q = np.arange(16)[None, :]
        for b in range(NB):
            d = 16 * b + q - p
            fill(e2_off(b, kh), 2, kh, d, d >= 0)
        for g in range(3):
            d = 128 - p + np.zeros((1, 1), np.int64)
            fill(es_off(g, kh), g, kh, d, d >= 0)
            for s in range(4):
                enew[0, (g * 4 + kh) * 4 + s] = rel_bias[0, g * 16 + 4 * kh + s]
    return ein, em, enew


def _colvec(v):
    v = np.asarray(v, np.float32).reshape(-1, 32, 128)
    return np.ascontiguousarray(v.transpose(2, 0, 1).reshape(128, -1))


_NC_CACHE = {}


def kernel(x_prompt, x_sample, state_conv, state_h, state_kv_w128, state_kv_w512, state_kv_w2048,
           a_pre_g, a_w_in, a_conv_w, a_conv_b, a_w_gate_a, a_b_gate_a, a_w_gate_x, a_b_gate_x,
           a_lambda, a_w_out, a_post_g, kv_norm_g, w_kv, rel_bias, b_pre_g, b_w_in, b_w_out, b_post_g):
    f = lambda a: np.ascontiguousarray(np.asarray(a, np.float32))
    vecs = np.concatenate([
        _colvec(a_pre_g), _colvec(np.asarray(a_conv_w)), _colvec(a_conv_b),
        _colvec(np.asarray(a_b_gate_a).reshape(2, 4096)), _colvec(np.asarray(a_b_gate_x).reshape(2, 4096)),
        _colvec(a_lambda), _colvec(a_post_g), _colvec(kv_norm_g), _colvec(b_pre_g), _colvec(b_post_g)], axis=1)
    assert vecs.shape == (128, NV)
    ein, em, enew = _etiles(np.asarray(rel_bias, np.float32))
    def tile_w(w):
        w = np.asarray(w, np.float32)
        lead = w.shape[:-2]
        K, M = w.shape[-2:]
        w = w.reshape(lead + (K // 128, 128, M // 128, 128))
        nl = len(lead)
        perm = tuple(range(nl)) + (nl + 2, nl + 1, nl + 0, nl + 3)
        return np.ascontiguousarray(w.transpose(perm)).reshape(lead + (M // 128, 128, K))

    shared = dict(a_w_in=tile_w(a_w_in), a_w_out=tile_w(a_w_out), a_wga=f(a_w_gate_a), a_wgx=f(a_w_gate_x), w_kv=tile_w(w_kv),
                  b_w_in=tile_w(b_w_in), b_w_out=tile_w(b_w_out), vecs=f(vecs), ein=ein, em=em, enew=enew,
                  ident=np.eye(128, dtype=np.float32))
    xp = f(x_prompt); xs = f(x_sample); sc = f(state_conv); shh = f(state_h)
    k0 = f(state_kv_w128).reshape(8, 128, 1024); k1 = f(state_kv_w512).reshape(8, 512, 1024); k2 = f(state_kv_w2048).reshape(8, 2048, 1024)
    in_maps = []
    for c in range(8):
        m = dict(shared)
        m.update(xp=xp[c % 4], xs=xs[c], sconv=np.ascontiguousarray(sc[:, c]), sh=np.ascontiguousarray(shh[:, c]),
                 skv0=k0[c], skv1=k1[c], skv2=k2[c])
        in_maps.append(m)
    if "nc" not in _NC_CACHE:
        _NC_CACHE["nc"] = build()
    res = run_bass_kernel_spmd(_NC_CACHE["nc"], in_maps, core_ids=list(range(8)))
    R = res.results
    y_prompt = np.stack([R[c]["yp"] for c in range(4)])
    y_sample = np.stack([R[c]["ys"] for c in range(8)])
    p_conv = np.stack([R[c]["pconv"] for c in range(4)], axis=1)
    p_h = np.stack([R[c]["ph"] for c in range(4)], axis=1)
    pkv = np.stack([R[c]["pkv"] for c in range(4)]).reshape(4, 2048, 3, 2, 4, 128)
    p_kv128 = np.ascontiguousarray(pkv[:, 2048 - 128:, 0])
    p_kv512 = np.ascontiguousarray(pkv[:, 2048 - 512:, 1])
    p_kv2048 = np.ascontiguousarray(pkv[:, :, 2])
    s_conv = np.stack([R[c]["sconv_o"] for c in range(8)], axis=1)
    s_h = np.stack([R[c]["sh_o"] for c in range(8)], axis=1)
    s_kv128 = np.stack([R[c]["skv_o0"] for c in range(8)]).reshape(8, 128, 2, 4, 128)
    s_kv512 = np.stack([R[c]["skv_o1"] for c in range(8)]).reshape(8, 512, 2, 4, 128)
    s_kv2048 = np.stack([R[c]["skv_o2"] for c in range(8)]).reshape(8, 2048, 2, 4, 128)
    f32 = lambda a: np.asarray(a, np.float32)
    return tuple(f32(a) for a in (y_prompt, y_sample, p_conv, p_h, p_kv128, p_kv512, p_kv2048, s_conv, s_h, s_kv128, s_kv512, s_kv2048))
```

```python
import contextlib
import numpy as np
import concourse.bass as bass
import concourse.mybir as mybir
from concourse.bass_utils import run_bass_kernel_spmd

F32 = mybir.dt.float32
BF16 = mybir.dt.bfloat16
AF = mybir.ActivationFunctionType
ALU = mybir.AluOpType

SAME_ENGINE_SYNC = True
T = 256
NB = 2048 // T
EPS = 1e-6
SCALE = 128 ** -0.5
DIL = (1, 4, 16)


class Buf:
    __slots__ = ("name", "w", "r")

    def __init__(self, name=""):
        self.name = name
        self.w = None
        self.r = {}


class Prog:
    ENGS = ("pe", "act", "dve", "pool", "sp")

    def __init__(self, nc, stack):
        self.nc = nc
        self.stack = stack
        self.q = {e: [] for e in self.ENGS}
        self.cnt = {e: 0 for e in self.ENGS}
        self.seen = {e: {} for e in self.ENGS}
        self.esem = {e: stack.enter_context(nc.semaphore("es_" + e)) for e in self.ENGS}
        self.dcnt = {}

    def dma_sem(self, name):
        s = self.stack.enter_context(self.nc.semaphore(name))
        self.dcnt[id(s)] = 0
        return s

    def sb(self, name, shape, dt):
        return self.stack.enter_context(self.nc.sbuf_tensor("sb_" + name, list(shape), dt))

    def ps(self, name, shape, dt=F32):
        return self.stack.enter_context(self.nc.psum_tensor("psum_" + name, list(shape), dt))

    def _deps(self, eng, reads, writes):
        deps = []
        for b in reads:
            if b.w is not None:
                deps.append(b.w)
        for b in writes:
            if b.w is not None:
                deps.append(b.w)
            deps.extend(b.r.values())
        waits = []
        seen = self.seen[eng]
        for (sem, val, src) in deps:
            if src == eng and (eng == "pe" or not SAME_ENGINE_SYNC):
                continue
            k = id(sem)
            if seen.get(k, 0) >= val:
                continue
            seen[k] = val
            waits.append((sem, val))
        return waits

    def _commit(self, tok, reads, writes):
        for b in writes:
            b.w = tok
            b.r = {}
        for b in reads:
            if b not in writes:
                b.r[id(tok[0])] = tok

    def op(self, eng, fn, reads=(), writes=(), inc=True):
        waits = self._deps(eng, reads, writes)
        if inc:
            self.cnt[eng] += 1
            tok = (self.esem[eng], self.cnt[eng], eng)
        else:
            tok = (self.esem[eng], self.cnt[eng] + 1, eng)
        self.q[eng].append((fn, waits, (self.esem[eng], 1) if inc else None))
        self._commit(tok, reads, writes)
        return tok

    def dma(self, eng, sem, fn, reads=(), writes=()):
        waits = self._deps(eng, reads, writes)
        self.dcnt[id(sem)] += 16
        tok = (sem, self.dcnt[id(sem)], "dma")
        self.q[eng].append((fn, waits, (sem, 16)))
        self._commit(tok, reads, writes)
        return tok

    def wait_all(self, eng, bufs):
        waits = self._deps(eng, (), bufs)
        self.q[eng].append((None, waits, None))

    def replay(self, block):
        names = {"pe": "tensor", "act": "scalar", "dve": "vector", "pool": "gpsimd", "sp": "sync"}

        def run(e, engobj):
            for fn, waits, inc in self.q[e]:
                for sem, val in waits:
                    engobj.wait_ge(sem, val)
                if fn is None:
                    continue
                ins = fn(engobj)
                if inc is not None:
                    ins.then_inc(inc[0], inc[1])

        for e in self.ENGS:
            if not self.q[e]:
                continue

            def mk(e):
                def body(engobj):
                    run(e, engobj)
                return body
            getattr(block, names[e])(mk(e))


V_APRE, V_CW, V_CB, V_BGA, V_BGX, V_LAM, V_APOST, V_KVG, V_BPRE, V_BPOST = 0, 64, 320, 384, 448, 512, 576, 640, 672, 736
NV = 800
E0 = 0
E1 = 4096
E2 = 8192
ES = 10240
NE = 10240 + 48


def e0_off(pc, kh): return E0 + (pc * 4 + kh) * 512
def e1_off(half, pc, kh): return E1 + ((half * 2 + pc) * 4 + kh) * 256
def e2_off(b, kh): return E2 + (b * 4 + kh) * 64
def es_off(g, kh): return ES + (g * 4 + kh) * 4


def build():
    nc = bass.Bass("TRN2", target_bir_lowering=False)

    def D(name, shape, dt=F32, kind="ExternalInput"):
        return nc.dram_tensor(name, list(shape), dt, kind=kind).ap()

    xp = D("xp", [2048, 4096]); xs = D("xs", [1, 4096])
    sconv = D("sconv", [2, 3, 4096]); sh = D("sh", [2, 4096])
    skv = [D("skv0", [128, 1024]), D("skv1", [512, 1024]), D("skv2", [2048, 1024])]
    a_w_in = D("a_w_in", [2, 64, 128, 4096]); a_w_out = D("a_w_out", [2, 32, 128, 4096])
    a_wga = D("a_wga", [2, 16, 256, 256]); a_wgx = D("a_wgx", [2, 16, 256, 256])
    w_kv = D("w_kv", [24, 128, 4096])
    b_w_in = D("b_w_in", [2, 64, 128, 4096]); b_w_out = D("b_w_out", [2, 32, 128, 2048])
    vecs_d = D("vecs", [128, NV]); ein_d = D("ein", [128, NE]); em_d = D("em", [128, NE])
    enew_d = D("enew", [1, 48]); ident_d = D("ident", [128, 128]); flg_d = D("flg", [128, 2])
    O = lambda n, s: D(n, s, kind="ExternalOutput")
    yp = O("yp", [1024, 4096]); ys = O("ys", [1, 4096])
    pconv = O("pconv", [2, 3, 4096]); ph = O("ph", [2, 4096]); pkv = O("pkv", [2048, 3072])
    sconv_o = O("sconv_o", [2, 3, 4096]); sh_o = O("sh_o", [2, 4096])
    skv_o = [O("skv_o0", [128, 1024]), O("skv_o1", [512, 1024]), O("skv_o2", [2048, 1024])]
    kvs = nc.dram_tensor("kvs", [2048 + 16, 3072], F32).ap()

    with contextlib.ExitStack() as st:
        P = Prog(nc, st)
        xT = P.sb("xT", [128, 32, T], F32); Bx = [Buf() for _ in range(32)]
        ub = P.sb("ub", [128, 32, T], BF16); Bu = [Buf() for _ in range(32)]
        hg = P.sb("hg", [128, 32, T], BF16); Bh = [Buf() for _ in range(32)]
        NW = 3
        wbf = [P.sb("wbf%d" % i, [128, 32, 128], BF16) for i in range(NW)]; Bw = [Buf() for _ in range(NW)]
        wsem = [P.dma_sem("wsem%d" % i) for i in range(NW)]
        wg = P.sb("wg", [128, 2, 2, 256], BF16); Bwg = Buf(); wgsem = P.dma_sem("wgsem")
        vecs = P.sb("vecs", [128, NV], F32); Bv = Buf()
        c1 = P.sb("c1", [128, 64], F32); Bc1 = Buf()
        Eall = P.sb("Eall", [128, NE], BF16); BE = Buf()
        enew = P.sb("enew", [1, 48], F32); Benew = Buf()
        ident = P.sb("ident", [128, 128], F32); Bid = Buf()
        flg = P.sb("flg", [128, 2], F32); Bflg = Buf(); flgsem = P.dma_sem("flgsem")
        ones = P.sb("ones", [128, 128], BF16); Bones = Buf()
        rs = P.sb("rs", [128, T], F32); Brs = Buf()
        sqb = [P.sb("sqb%d" % i, [128, T], BF16) for i in range(2)]; Bsq = [Buf(), Buf()]
        stt = [P.sb("st%d" % l, [128, 4, 32], F32) for l in range(2)]; Bst = [Buf(), Buf()]
        stT = P.sb("stT", [128, 128], F32); BstT = Buf()
        xc = [P.sb("xc%d" % i, [128, 3 + T], F32) for i in range(2)]; Bxc = [Buf(), Buf()]
        xv = [P.sb("xv%d" % i, [128, T], F32) for i in range(2)]; Bxv = [Buf(), Buf()]
        xvb = [P.sb("xvb%d" % i, [128, T], BF16) for i in range(2)]; Bxvb = [Buf(), Buf()]
        rg = [[P.sb("rg%d%d" % (a, j), [128, T], F32) for j in range(2)] for a in range(2)]
        Brg = [[Buf(), Buf()], [Buf(), Buf()]]
        ta = P.sb("ta", [128, T], F32); Bta = Buf()
        tb = P.sb("tb", [128, T], F32); Btb = Buf()
        tcc = P.sb("tcc", [128, T], F32); Btc = Buf()
        hh = [P.sb("hh%d" % i, [128, T], F32) for i in range(2)]; Bhh = [Buf(), Buf()]
        sg = P.sb("sg", [128, T], F32); Bsg = Buf()
        tmpx = P.sb("tmpx", [128, T], F32); Btmpx = Buf()
        xio = P.sb("xio", [128, 4096], F32); Bxio = Buf(); xiosem = P.dma_sem("xiosem")
        kvt = [P.sb("kvt%d" % i, [128, 3072], F32) for i in range(2)]; Bkvt = [Buf(), Buf()]
        kvosem = P.dma_sem("kvosem")
        q4 = P.sb("q4", [128, 4, T], BF16); Bq4 = Buf()
        num = P.sb("num", [128, 4, T], F32); Bnum = Buf()
        den = P.sb("den", [128, 4, T], F32); Bden = Buf()
        kvin = [P.sb("kvin%d" % i, [128, 2, 128], F32) for i in range(2)]; Bkvin = [Buf(), Buf()]
        kvinsem = [P.dma_sem("kvinsem%d" % i) for i in range(2)]
        kTb = [P.sb("kTb%d" % i, [128, 128], BF16) for i in range(2)]; BkT = [Buf(), Buf()]
        vbb = [P.sb("vbb%d" % i, [128, 128], BF16) for i in range(2)]; Bvb = [Buf(), Buf()]
        pT = [P.sb("pT%d" % i, [128, 512], BF16) for i in range(2)]; BpT = [Buf(), Buf()]
        esb = P.sb("esb", [128, 1024], F32); Besb = Buf()
        emb = P.sb("emb", [128, 1024], F32); Bemb = Buf()
        misc = P.sb("misc", [128, 64], F32); Bmisc = Buf()
        knew = P.sb("knew", [128, 16], BF16); Bknew = Buf()
        vnew = P.sb("vnew", [1, 128], BF16); Bvnew = Buf()
        pnew = P.sb("pnew", [1, 16], BF16); Bpnew = Buf()
        ps = P.ps("ps", [128, 8, 512], F32); Bps = [Buf() for _ in range(8)]
        setsem = [P.dma_sem("setsem%d" % i) for i in range(6)]
        outsem = P.dma_sem("outsem"); Bout = Buf()
        d2dsem = P.dma_sem("d2dsem"); Bd2d = Buf()

        P.dma("sp", setsem[0], lambda e: e.dma_start(out=vecs[:, :], in_=vecs_d), writes=[Bv])
        P.dma("sp", setsem[1], lambda e: e.dma_start(out=enew[:, :], in_=enew_d), writes=[Benew])
        P.dma("sp", flgsem, lambda e: e.dma_start(out=flg[:, :], in_=flg_d), writes=[Bflg])
        P.dma("sp", setsem[5], lambda e: e.dma_start(out=ident[:, :], in_=ident_d), writes=[Bid])
        P.op("pool", lambda e: e.memset(ones[:, :], 1.0), writes=[Bones])
        for i in range(2):
            P.op("pool", lambda e, i=i: e.memset(kvin[i][:, :, :], 0.0), writes=[Bkvin[i]])
            P.op("pool", lambda e, i=i: e.memset(kvt[i][:, :], 0.0), writes=[Bkvt[i]])
        for l in range(2):
            P.op("pool", lambda e, l=l: e.memset(stt[l][:, :, :], 0.0), writes=[Bst[l]])
        P.op("act", lambda e: e.activation(out=c1[:, :], in_=vecs[:, V_LAM:V_LAM + 64], func=AF.Exp, scale=-1.0), reads=[Bv], writes=[Bc1])
        P.op("act", lambda e: e.activation(out=c1[:, :], in_=c1[:, :], func=AF.Ln, bias=1.0, scale=1.0), reads=[Bc1], writes=[Bc1])
        P.op("dve", lambda e: e.tensor_scalar(out=c1[:, :], in0=c1[:, :], scalar1=-8.0, scalar2=None, op0=ALU.mult), reads=[Bc1], writes=[Bc1])
        for c0 in range(0, NE, 1024):
            cw = min(1024, NE - c0)
            P.dma("sp", setsem[2], lambda e, c0=c0, cw=cw: e.dma_start(out=esb[:, 0:cw], in_=ein_d[:, c0:c0 + cw]), writes=[Besb])
            P.dma("sp", setsem[3], lambda e, c0=c0, cw=cw: e.dma_start(out=emb[:, 0:cw], in_=em_d[:, c0:c0 + cw]), writes=[Bemb])
            P.op("act", lambda e, cw=cw: e.activation(out=esb[:, 0:cw], in_=esb[:, 0:cw], func=AF.Exp), reads=[Besb], writes=[Besb])
            P.op("dve", lambda e, c0=c0, cw=cw: e.tensor_tensor(out=Eall[:, c0:c0 + cw], in0=esb[:, 0:cw], in1=emb[:, 0:cw], op=ALU.mult),
                 reads=[Besb, Bemb], writes=[BE])
        P.op("act", lambda e: e.activation(out=enew[:, :], in_=enew[:, :], func=AF.Exp), reads=[Benew], writes=[Benew])

        wctr = [0]
        pctr = [0]

        def wload(Wsl, nk, col0):
            s_ = wctr[0] % NW; wctr[0] += 1
            P.dma("pool", wsem[s_], lambda e: e.dma_start(
                out=wbf[s_][:, 0:nk, :].rearrange("p k m -> p (k m)"), in_=Wsl[col0 // 128], max_dma_last_dim=8192), writes=[Bw[s_]])
            return s_

        def lin(Wsl, nk, cols, src, Bsrc, Tn, consumer):
            slots = {}
            for i0 in range(min(2, len(cols))):
                slots[i0] = wload(Wsl, nk, cols[i0])
            for idx, col0 in enumerate(cols):
                s = slots.pop(idx)
                pb = pctr[0] % 3; pctr[0] += 1
                for k in range(nk):
                    P.op("pe", lambda e, s=s, k=k, pb=pb: e.matmul(ps[:, pb, 0:Tn], lhsT=wbf[s][:, k, :], rhs=src[:, k, 0:Tn],
                                                                  start=(k == 0), stop=(k == nk - 1)),
                         reads=[Bw[s], Bsrc[k]], writes=[Bps[pb]], inc=(k == nk - 1))
                if idx + 2 < len(cols):
                    slots[idx + 2] = wload(Wsl, nk, cols[idx + 2])
                consumer(idx, pb)

        def rstd_from_ps3(Tn):
            P.op("act", lambda e: e.activation(out=rs[:, 0:Tn], in_=ps[:, 3, 0:Tn], func=AF.Sqrt, bias=EPS, scale=1.0 / 4096), reads=[Bps[3]], writes=[Brs])
            P.op("dve", lambda e: e.reciprocal(out=rs[:, 0:Tn], in_=rs[:, 0:Tn]), reads=[Brs], writes=[Brs])

        def norm_pre(gcol, Tn):
            for k in range(32):
                i = k % 2
                P.op("dve", lambda e, k=k, i=i: e.tensor_tensor(out=sqb[i][:, 0:Tn], in0=xT[:, k, 0:Tn], in1=xT[:, k, 0:Tn], op=ALU.mult),
                     reads=[Bx[k]], writes=[Bsq[i]])
                P.op("pe", lambda e, k=k, i=i: e.matmul(ps[:, 3, 0:Tn], lhsT=ones[:, :], rhs=sqb[i][:, 0:Tn], start=(k == 0), stop=(k == 31)),
                     reads=[Bsq[i], Bones], writes=[Bps[3]], inc=True)
            rstd_from_ps3(Tn)
            for k in range(32):
                P.op("dve", lambda e, k=k: e.scalar_tensor_tensor(out=ub[:, k, 0:Tn], in0=xT[:, k, 0:Tn], scalar=vecs[:, gcol + k:gcol + k + 1],
                                                                  in1=rs[:, 0:Tn], op0=ALU.mult, op1=ALU.mult),
                     reads=[Bx[k], Brs, Bv], writes=[Bu[k]])

        def out_consumer(Tn):
            def cons(m, pb):
                i = m % 2
                P.op("act", lambda e, m=m, pb=pb: e.activation(out=ub[:, m, 0:Tn], in_=ps[:, pb, 0:Tn], func=AF.Copy), reads=[Bps[pb]], writes=[Bu[m]])
                P.op("act", lambda e, i=i, pb=pb: e.activation(out=sqb[i][:, 0:Tn], in_=ps[:, pb, 0:Tn], func=AF.Square), reads=[Bps[pb]], writes=[Bsq[i]])
                P.op("pe", lambda e, m=m, i=i: e.matmul(ps[:, 3, 0:Tn], lhsT=ones[:, :], rhs=sqb[i][:, 0:Tn], start=(m == 0), stop=(m == 31)),
                     reads=[Bsq[i], Bones], writes=[Bps[3]], inc=True)
            return cons

        def post(gcol, Tn):
            rstd_from_ps3(Tn)
            for k in range(32):
                P.op("dve", lambda e, k=k: e.scalar_tensor_tensor(out=tmpx[:, 0:Tn], in0=ub[:, k, 0:Tn], scalar=vecs[:, gcol + k:gcol + k + 1],
                                                                  in1=rs[:, 0:Tn], op0=ALU.mult, op1=ALU.mult),
                     reads=[Bu[k], Brs, Bv], writes=[Btmpx])
                P.op("dve", lambda e, k=k: e.tensor_tensor(out=xT[:, k, 0:Tn], in0=xT[:, k, 0:Tn], in1=tmpx[:, 0:Tn], op=ALU.add),
                     reads=[Btmpx, Bx[k]], writes=[Bx[k]])

        def a_layer(l, Tn):
            norm_pre(V_APRE + 32 * l, Tn)
            S = stt[l]; BS = Bst[l]
            cols = []
            for h in range(16):
                cols += [(2 * h) * 128, (2 * h + 1) * 128, 4096 + (2 * h) * 128, 4096 + (2 * h + 1) * 128]

            def cons(idx, pb):
                h, j = idx // 4, idx % 4
                if j < 2:
                    c = 2 * h + j
                    P.op("dve", lambda e: e.tensor_copy(out=xc[j][:, 0:3], in_=S[:, 0:3, c]), reads=[BS], writes=[Bxc[j]])
                    P.op("act", lambda e: e.activation(out=xc[j][:, 3:3 + Tn], in_=ps[:, pb, 0:Tn], func=AF.Copy), reads=[Bps[pb]], writes=[Bxc[j]])
                    P.op("dve", lambda e: e.tensor_copy(out=S[:, 0:3, c], in_=xc[j][:, Tn:Tn + 3]), reads=[Bxc[j]], writes=[BS])
                    cwc = V_CW + l * 128
                    P.op("dve", lambda e: e.tensor_scalar(out=xv[j][:, 0:Tn], in0=xc[j][:, 3:3 + Tn], scalar1=vecs[:, cwc + 96 + c:cwc + 97 + c],
                                                          scalar2=vecs[:, V_CB + 32 * l + c:V_CB + 32 * l + c + 1], op0=ALU.mult, op1=ALU.add),
                         reads=[Bxc[j], Bv], writes=[Bxv[j]])
                    for kk in range(3):
                        P.op("dve", lambda e, kk=kk: e.scalar_tensor_tensor(out=xv[j][:, 0:Tn], in0=xc[j][:, kk:kk + Tn],
                                                                            scalar=vecs[:, cwc + 32 * kk + c:cwc + 32 * kk + c + 1],
                                                                            in1=xv[j][:, 0:Tn], op0=ALU.mult, op1=ALU.add),
                             reads=[Bxc[j], Bv, Bxv[j]], writes=[Bxv[j]])
                    P.op("pool", lambda e: e.tensor_copy(out=xvb[j][:, 0:Tn], in_=xv[j][:, 0:Tn]), reads=[Bxv[j]], writes=[Bxvb[j]])
                else:
                    jo = j - 2
                    c = 2 * h + jo
                    if jo == 0:
                        P.dma("pool", wgsem, lambda e: e.dma_start(out=wg[:, 0, :, :], in_=a_wga[l, h].rearrange("(ic p) o -> p ic o", p=128)), writes=[Bwg])
                        P.dma("pool", wgsem, lambda e: e.dma_start(out=wg[:, 1, :, :], in_=a_wgx[l, h].rearrange("(ic p) o -> p ic o", p=128)), writes=[Bwg])
                        for a in range(2):
                            bcol = (V_BGA if a == 0 else V_BGX) + 32 * l
                            for jo in range(2):
                                gb = 4 + (a * 2 + jo) % 2
                                for ic in range(2):
                                    P.op("pe", lambda e, a=a, jo=jo, ic=ic, gb=gb: e.matmul(ps[:, gb, 0:Tn], lhsT=wg[:, a, ic, jo * 128:(jo + 1) * 128],
                                                                                            rhs=xvb[ic][:, 0:Tn], start=(ic == 0), stop=(ic == 1)),
                                         reads=[Bwg, Bxvb[ic]], writes=[Bps[gb]], inc=(ic == 1))
                                cc = 2 * h + jo
                                P.op("act", lambda e, a=a, jo=jo, gb=gb, cc=cc, bcol=bcol: e.activation(
                                    out=rg[a][jo][:, 0:Tn], in_=ps[:, gb, 0:Tn], func=AF.Sigmoid, bias=vecs[:, bcol + cc:bcol + cc + 1], scale=1.0),
                                    reads=[Bps[gb], Bv], writes=[Brg[a][jo]])
                        for jo in range(2):
                            cc = 2 * h + jo
                            P.op("act", lambda e, jo=jo, cc=cc: e.activation(out=ta[:, 0:Tn], in_=rg[0][jo][:, 0:Tn], func=AF.Exp,
                                                                             scale=c1[:, 32 * l + cc:32 * l + cc + 1]),
                                 reads=[Brg[0][jo], Bc1], writes=[Bta])
                            P.op("dve", lambda e: e.tensor_tensor(out=tb[:, 0:Tn], in0=ta[:, 0:Tn], in1=ta[:, 0:Tn], op=ALU.mult), reads=[Bta], writes=[Btb])
                            P.op("act", lambda e: e.activation(out=tb[:, 0:Tn], in_=tb[:, 0:Tn], func=AF.Sqrt, bias=1.0, scale=-1.0), reads=[Btb], writes=[Btb])
                            P.op("dve", lambda e, jo=jo: e.tensor_tensor(out=tcc[:, 0:Tn], in0=rg[1][jo][:, 0:Tn], in1=xv[jo][:, 0:Tn], op=ALU.mult),
                                 reads=[Brg[1][jo], Bxv[jo]], writes=[Btc])
                            P.op("dve", lambda e: e.tensor_tensor(out=tcc[:, 0:Tn], in0=tcc[:, 0:Tn], in1=tb[:, 0:Tn], op=ALU.mult), reads=[Btc, Btb], writes=[Btc])
                            P.op("dve", lambda e, jo=jo, cc=cc: e.tensor_tensor_scan(out=hh[jo][:, 0:Tn], data0=ta[:, 0:Tn], data1=tcc[:, 0:Tn],
                                                                                     initial=S[:, 3, cc:cc + 1], op0=ALU.mult, op1=ALU.add),
                                 reads=[Bta, Btc, BS], writes=[Bhh[jo]])
                            P.op("dve", lambda e, jo=jo, cc=cc: e.tensor_copy(out=S[:, 3, cc:cc + 1], in_=hh[jo][:, Tn - 1:Tn]), reads=[Bhh[jo]], writes=[BS])
                    jo = j - 2
                    P.op("act", lambda e: e.activation(out=sg[:, 0:Tn], in_=ps[:, pb, 0:Tn], func=AF.Silu), reads=[Bps[pb]], writes=[Bsg])
                    P.op("dve", lambda e: e.tensor_tensor(out=hg[:, c, 0:Tn], in0=hh[jo][:, 0:Tn], in1=sg[:, 0:Tn], op=ALU.mult),
                         reads=[Bhh[jo], Bsg], writes=[Bh[c]])

            lin(a_w_in[l], 32, cols, ub, Bu, Tn, cons)
            lin(a_w_out[l], 32, [m * 128 for m in range(32)], hg, Bh, Tn, out_consumer(Tn))
            post(V_APOST + 32 * l, Tn)

        def kv_phase(row0, Tn, sample):
            norm_pre(V_KVG, Tn)
            ntc = max(1, Tn // 128)
            M = min(128, Tn)
            slots = {0: wload(w_kv, 32, 0), 1: wload(w_kv, 32, 128)}
            for n in range(24):
                s = slots.pop(n)
                if n + 2 < 24 and n >= 1:
                    pass
                for tc in range(ntc):
                    pb = pctr[0] % 3; pctr[0] += 1
                    for k in range(32):
                        P.op("pe", lambda e, s=s, k=k, pb=pb, tc=tc: e.matmul(ps[0:M, pb, 0:128], lhsT=ub[:, k, tc * 128:tc * 128 + M], rhs=wbf[s][:, k, :],
                                                                              start=(k == 0), stop=(k == 31)),
                             reads=[Bw[s], Bu[k]], writes=[Bps[pb]], inc=(k == 31))
                    P.op("act", lambda e, pb=pb, tc=tc, n=n: e.activation(out=kvt[tc][0:M, n * 128:(n + 1) * 128], in_=ps[0:M, pb, 0:128], func=AF.Copy),
                         reads=[Bps[pb]], writes=[Bkvt[tc]])
                if n + 2 < 24:
                    slots[n + 2] = wload(w_kv, 32, (n + 2) * 128)
            if not sample:
                for tc in range(ntc):
                    r0 = row0 + tc * 128
                    P.dma("sp", kvosem, lambda e, tc=tc, r0=r0: e.dma_start(out=pkv[r0:r0 + 128, :], in_=kvt[tc][:, :]), reads=[Bkvt[tc]], writes=[Bout])
                    P.dma("sp", kvosem, lambda e, tc=tc, r0=r0: e.dma_start(out=kvs[r0:r0 + 128, :], in_=kvt[tc][:, :]), reads=[Bkvt[tc]], writes=[Bd2d])

        uctr = [0]

        def key_tile(src_rows_ap, nrows, g, kh):
            i = uctr[0] % 2; uctr[0] += 1
            P.dma("sp", kvinsem[i], lambda e: e.dma_start(out=kvin[i][0:nrows, :, :], in_=src_rows_ap), reads=[Bd2d], writes=[Bkvin[i]])
            tb_ = 4 + i
            P.op("pe", lambda e: e.transpose(out=ps[:, tb_, 0:128], in_=kvin[i][:, 0, :], identity=ident[:, :]), reads=[Bkvin[i], Bid], writes=[Bps[tb_]])
            P.op("act", lambda e: e.activation(out=kTb[i][:, :], in_=ps[:, tb_, 0:128], func=AF.Copy), reads=[Bps[tb_]], writes=[BkT[i]])
            P.op("pool", lambda e: e.tensor_copy(out=vbb[i][:, :], in_=kvin[i][:, 1, :]), reads=[Bkvin[i]], writes=[Bvb[i]])
            return i

        def attn_unit(qap, nq, tiles, acc_first, accap_fn):
            W = 4 * nq
            for ti, (i, eoff, fcol) in enumerate(tiles):
                sb_ = 4 + i
                P.op("pe", lambda e, i=i, sb_=sb_: e.matmul(ps[:, sb_, 0:W].rearrange("p (s q) -> p s q", s=4), lhsT=kTb[i][:, :], rhs=qap, start=True, stop=True),
                     reads=[BkT[i], Bq4], writes=[Bps[sb_]])
                P.op("act", lambda e, i=i, sb_=sb_: e.activation(out=pT[i][:, 0:W], in_=ps[:, sb_, 0:W], func=AF.Exp, scale=SCALE), reads=[Bps[sb_]], writes=[BpT[i]])
                if fcol is None:
                    P.op("dve", lambda e, i=i, eoff=eoff: e.tensor_tensor(out=pT[i][:, 0:W], in0=pT[i][:, 0:W], in1=Eall[:, eoff:eoff + W], op=ALU.mult),
                         reads=[BpT[i], BE], writes=[BpT[i]])
                else:
                    P.op("dve", lambda e, i=i, eoff=eoff, fcol=fcol: e.scalar_tensor_tensor(
                        out=pT[i][:, 0:W], in0=pT[i][:, 0:W], scalar=flg[:, fcol:fcol + 1], in1=Eall[:, eoff:eoff + W], op0=ALU.mult, op1=ALU.mult),
                         reads=[BpT[i], BE, Bflg], writes=[BpT[i]])
                first, last = ti == 0, ti == len(tiles) - 1
                P.op("pe", lambda e, i=i, first=first, last=last: e.matmul(ps[:, 6, 0:W], lhsT=vbb[i][:, :], rhs=pT[i][:, 0:W], start=first, stop=last),
                     reads=[Bvb[i], BpT[i]], writes=[Bps[6]])
                P.op("pe", lambda e, i=i, first=first, last=last: e.matmul(ps[:, 7, 0:W], lhsT=ones[:, :], rhs=pT[i][:, 0:W], start=first, stop=last),
                     reads=[Bones, BpT[i]], writes=[Bps[7]])
            for (acc, Bacc, bank) in ((num, Bnum, 6), (den, Bden, 7)):
                src = ps[:, bank, 0:W].rearrange("p (s q) -> p s q", s=4)
                if acc_first:
                    P.op("act", lambda e, acc=acc, src=src: e.activation(out=accap_fn(acc), in_=src, func=AF.Copy), reads=[Bps[bank]], writes=[Bacc])
                else:
                    P.op("dve", lambda e, acc=acc, src=src: e.tensor_tensor(out=accap_fn(acc), in0=accap_fn(acc), in1=src, op=ALU.add),
                         reads=[Bps[bank], Bacc], writes=[Bacc])

        def attention_prompt(b, g, kh):
            t0 = b * T
            kc = g * 1024 + kh * 128

            def rows(start, step, count):
                v = kvs[start:start + step * count, :].rearrange("(j c) n -> j c n", c=step)[:, 0, :]
                return v[:, g * 1024:(g + 1) * 1024].rearrange("j (v k x) -> j v k x", v=2, k=4)[:, :, kh, :]

            if g == 0:
                for qb in range(T // 128):
                    tiles = []
                    if t0 + qb * 128 - 128 >= 0:
                        i = key_tile(rows(t0 + qb * 128 - 128, 1, 128), 128, g, kh)
                        tiles.append((i, e0_off(1, kh), 0 if (t0 + qb * 128 - 128) < 1024 else None))
                    i = key_tile(rows(t0 + qb * 128, 1, 128), 128, g, kh)
                    tiles.append((i, e0_off(0, kh), None))
                    qap = q4[:, :, qb * 128:(qb + 1) * 128]
                    attn_unit(qap, 128, tiles, True, lambda acc, qb=qb: acc[:, :, qb * 128:(qb + 1) * 128])
            elif g == 1:
                nbi, half = b // 2, b % 2
                for c in range(4):
                    tiles = []
                    if nbi > 0:
                        i = key_tile(rows(512 * (nbi - 1) + c, 4, 128), 128, g, kh)
                        tiles.append((i, e1_off(half, 1, kh), 0 if 512 * (nbi - 1) < 1024 else None))
                    cnt = 64 * (half + 1)
                    i = key_tile(rows(512 * nbi + c, 4, cnt), cnt, g, kh)
                    tiles.append((i, e1_off(half, 0, kh), None))
                    qap = q4[:, :, :].rearrange("p s (j c) -> p s c j", c=4)[:, :, c, :]
                    attn_unit(qap, 64, tiles, False, lambda acc, c=c: acc[:, :, :].rearrange("p s (j c) -> p s c j", c=4)[:, :, c, :])
            else:
                for c in range(16):
                    cnt = 16 * (b + 1)
                    i = key_tile(rows(c, 16, cnt), cnt, g, kh)
                    qap = q4[:, :, :].rearrange("p s (j c) -> p s c j", c=16)[:, :, c, :]
                    attn_unit(qap, 16, [(i, e2_off(b, kh), 1)], False, lambda acc, c=c: acc[:, :, :].rearrange("p s (j c) -> p s c j", c=16)[:, :, c, :])

        def attention_sample(g, kh):
            r = DIL[g]
            lb = 128 * r
            src = skv[g].rearrange("(j c) (v x) -> j c v x", c=r, v=2)[:, 0, :, kh * 128:(kh + 1) * 128]
            i = uctr[0] % 2; uctr[0] += 1
            P.dma("sp", kvinsem[i], lambda e: e.dma_start(out=kvin[i][:, :, :], in_=src), writes=[Bkvin[i]])
            tb_ = 4 + i
            P.op("pe", lambda e: e.transpose(out=ps[:, tb_, 0:128], in_=kvin[i][:, 0, :], identity=ident[:, :]), reads=[Bkvin[i], Bid], writes=[Bps[tb_]])
            P.op("act", lambda e: e.activation(out=kTb[i][:, :], in_=ps[:, tb_, 0:128], func=AF.Copy), reads=[Bps[tb_]], writes=[BkT[i]])
            P.op("pool", lambda e: e.tensor_copy(out=vbb[i][:, :], in_=kvin[i][:, 1, :]), reads=[Bkvin[i]], writes=[Bvb[i]])
            qap = q4[:, :, 0:1]
            P.op("pe", lambda e: e.matmul(ps[:, tb_, 0:4].rearrange("p (s q) -> p s q", s=4), lhsT=kTb[i][:, :], rhs=qap, start=True, stop=True),
                 reads=[BkT[i], Bq4], writes=[Bps[tb_]])
            P.op("act", lambda e: e.activation(out=pT[i][:, 0:4], in_=ps[:, tb_, 0:4], func=AF.Exp, scale=SCALE), reads=[Bps[tb_]], writes=[BpT[i]])
            eo = es_off(g, kh)
            P.op("dve", lambda e: e.tensor_tensor(out=pT[i][:, 0:4], in0=pT[i][:, 0:4], in1=Eall[:, eo:eo + 4], op=ALU.mult), reads=[BpT[i], BE], writes=[BpT[i]])
            P.op("pe", lambda e: e.matmul(ps[:, 6, 0:4], lhsT=vbb[i][:, :], rhs=pT[i][:, 0:4], start=True, stop=False), reads=[Bvb[i], BpT[i]], writes=[Bps[6]])
            P.op("pe", lambda e: e.matmul(ps[:, 7, 0:4], lhsT=ones[:, :], rhs=pT[i][:, 0:4], start=True, stop=False), reads=[Bones, BpT[i]], writes=[Bps[7]])
            kc = g * 1024 + kh * 128
            P.op("pe", lambda e: e.transpose(out=ps[:, tb_, 8:9], in_=kvt[0][0:1, kc:kc + 128], identity=ident[0:1, 0:1]), reads=[Bkvt[0], Bid], writes=[Bps[tb_]])
            P.op("act", lambda e: e.activation(out=knew[:, 0:1], in_=ps[:, tb_, 8:9], func=AF.Copy), reads=[Bps[tb_]], writes=[Bknew])
            P.op("pool", lambda e: e.tensor_copy(out=vnew[0:1, 0:128], in_=kvt[0][0:1, kc + 512:kc + 640]), reads=[Bkvt[0]], writes=[Bvnew])
            P.op("pe", lambda e: e.matmul(ps[0:1, tb_, 16:20].rearrange("p (s q) -> p s q", s=4), lhsT=knew[:, 0:1], rhs=qap, start=True, stop=True),
                 reads=[Bknew, Bq4], writes=[Bps[tb_]])
            P.op("act", lambda e: e.activation(out=misc[0:1, 0:4], in_=ps[0:1, tb_, 16:20], func=AF.Exp, scale=SCALE), reads=[Bps[tb_]], writes=[Bmisc])
            en = (g * 4 + kh) * 4
            P.op("dve", lambda e: e.tensor_tensor(out=pnew[0:1, 0:4], in0=misc[0:1, 0:4], in1=enew[0:1, en:en + 4], op=ALU.mult), reads=[Bmisc, Benew], writes=[Bpnew])
            P.op("pe", lambda e: e.matmul(ps[:, 6, 0:4], lhsT=vnew[0:1, 0:128], rhs=pnew[0:1, 0:4], start=False, stop=True), reads=[Bvnew, Bpnew], writes=[Bps[6]])
            P.op("pe", lambda e: e.matmul(ps[:, 7, 0:4], lhsT=ones[0:1, :], rhs=pnew[0:1, 0:4], start=False, stop=True), reads=[Bones, Bpnew], writes=[Bps[7]])
            for (acc, Bacc, bank) in ((num, Bnum, 6), (den, Bden, 7)):
                src2 = ps[:, bank, 0:4].rearrange("p (s q) -> p s q", s=4)
                if g == 0:
                    P.op("act", lambda e, acc=acc, src2=src2: e.activation(out=acc[:, :, 0:1], in_=src2, func=AF.Copy), reads=[Bps[bank]], writes=[Bacc])
                else:
                    P.op("dve", lambda e, acc=acc, src2=src2: e.tensor_tensor(out=acc[:, :, 0:1], in0=acc[:, :, 0:1], in1=src2, op=ALU.add),
                         reads=[Bps[bank], Bacc], writes=[Bacc])

        def b_layer(l, Tn, b, sample):
            norm_pre(V_BPRE + 32 * l, Tn)
            cols = []
            for kh in range(4):
                for g in range(3):
                    cols += [g * 2048 + (4 * kh + s) * 128 for s in range(4)]
                cols += [6144 + (4 * kh + s) * 128 for s in range(4)]

            def cons(idx, pb):
                kh, j = idx // 16, idx % 16
                if j < 12:
                    g, s = j // 4, j % 4
                    P.op("act", lambda e: e.activation(out=q4[:, s, 0:Tn], in_=ps[:, pb, 0:Tn], func=AF.Copy), reads=[Bps[pb]], writes=[Bq4])
                    if s == 3:
                        if sample:
                            attention_sample(g, kh)
                        else:
                            attention_prompt(b, g, kh)
                else:
                    s = j - 12
                    if s == 0:
                        P.op("dve", lambda e: e.reciprocal(out=den[:, :, 0:Tn], in_=den[:, :, 0:Tn]), reads=[Bden], writes=[Bden])
                        P.op("dve", lambda e: e.tensor_tensor(out=num[:, :, 0:Tn], in0=num[:, :, 0:Tn], in1=den[:, :, 0:Tn], op=ALU.mult),
                             reads=[Bnum, Bden], writes=[Bnum])
                    P.op("act", lambda e: e.activation(out=sg[:, 0:Tn], in_=ps[:, pb, 0:Tn], func=AF.Silu), reads=[Bps[pb]], writes=[Bsg])
                    P.op("dve", lambda e: e.tensor_tensor(out=hg[:, 4 * kh + s, 0:Tn], in0=num[:, s, 0:Tn], in1=sg[:, 0:Tn], op=ALU.mult),
                         reads=[Bnum, Bsg], writes=[Bh[4 * kh + s]])

            lin(b_w_in[l], 32, cols, ub, Bu, Tn, cons)
            lin(b_w_out[l], 16, [m * 128 for m in range(32)], hg, Bh, Tn, out_consumer(Tn))
            post(V_BPOST + 32 * l, Tn)

        def load_x(src_rows, M, col0):
            P.dma("sp", xiosem, lambda e: e.dma_start(out=xio[0:M, :], in_=src_rows), writes=[Bxio])
            for k in range(32):
                tb_ = 4 + k % 2
                P.op("pe", lambda e, k=k, tb_=tb_: e.transpose(out=ps[:, tb_, 0:M], in_=xio[0:M, k * 128:(k + 1) * 128], identity=ident[0:M, 0:M]),
                     reads=[Bxio, Bid], writes=[Bps[tb_]])
                P.op("act", lambda e, k=k, tb_=tb_: e.activation(out=xT[:, k, col0:col0 + M], in_=ps[:, tb_, 0:M], func=AF.Copy), reads=[Bps[tb_]], writes=[Bx[k]])

        def store_x(dst_rows, M, col0):
            for k in range(32):
                tb_ = 4 + k % 2
                P.op("pe", lambda e, k=k, tb_=tb_: e.transpose(out=ps[0:M, tb_, 0:128], in_=xT[:, k, col0:col0 + M], identity=ident[:, :]),
                     reads=[Bx[k], Bid], writes=[Bps[tb_]])
                P.op("act", lambda e, k=k, tb_=tb_: e.activation(out=xio[0:M, k * 128:(k + 1) * 128], in_=ps[0:M, tb_, 0:128], func=AF.Copy),
                     reads=[Bps[tb_]], writes=[Bxio])
            P.dma("sp", xiosem, lambda e: e.dma_start(out=dst_rows, in_=xio[0:M, :]), reads=[Bxio], writes=[Bout])

        def store_states(conv_o, h_o):
            for l in range(2):
                P.op("pe", lambda e, l=l: e.transpose(out=ps[:, 4, 0:128], in_=stt[l][:, :, :].rearrange("p j c -> p (j c)"), identity=ident[:, :]),
                     reads=[Bst[l], Bid], writes=[Bps[4]])
                P.op("act", lambda e: e.activation(out=stT[:, :], in_=ps[:, 4, 0:128], func=AF.Copy), reads=[Bps[4]], writes=[BstT])
                for j in range(3):
                    P.dma("sp", outsem, lambda e, l=l, j=j: e.dma_start(out=conv_o[l, j].rearrange("(c p) -> c p", p=128), in_=stT[j * 32:(j + 1) * 32, :]),
                          reads=[BstT], writes=[Bout])
                P.dma("sp", outsem, lambda e, l=l: e.dma_start(out=h_o[l].rearrange("(c p) -> c p", p=128), in_=stT[96:128, :]), reads=[BstT], writes=[Bout])

        def load_states():
            for l in range(2):
                for j in range(3):
                    P.dma("sp", setsem[4], lambda e, l=l, j=j: e.dma_start(out=stT[j * 32:(j + 1) * 32, :], in_=sconv[l, j].rearrange("(c p) -> c p", p=128)), writes=[BstT])
                P.dma("sp", setsem[4], lambda e, l=l: e.dma_start(out=stT[96:128, :], in_=sh[l].rearrange("(c p) -> c p", p=128)), writes=[BstT])
                P.op("pe", lambda e, l=l: e.transpose(out=ps[:, 4, 0:128], in_=stT[:, :], identity=ident[:, :]), reads=[BstT, Bid], writes=[Bps[4]])
                P.op("act", lambda e, l=l: e.activation(out=stt[l][:, :, :].rearrange("p j c -> p (j c)"), in_=ps[:, 4, 0:128], func=AF.Copy),
                     reads=[Bps[4]], writes=[Bst[l]])

        for b in range(NB):
            for tc in range(T // 128):
                load_x(xp[b * T + tc * 128:b * T + (tc + 1) * 128, :], 128, tc * 128)
            a_layer(0, T)
            a_layer(1, T)
            kv_phase(b * T, T, False)
            if b == NB // 2 - 1:
                for l in range(2):
                    P.op("dve", lambda e, l=l: e.tensor_scalar(out=stt[l][:, :, :].rearrange("p j c -> p (j c)"), in0=stt[l][:, :, :].rearrange("p j c -> p (j c)"),
                                                               scalar1=flg[:, 0:1], scalar2=None, op0=ALU.mult), reads=[Bst[l], Bflg], writes=[Bst[l]])
            if b >= NB // 2:
                b_layer(0, T, b, False)
                b_layer(1, T, b, False)
                bo = b - NB // 2
                for tc in range(T // 128):
                    store_x(yp[bo * T + tc * 128:bo * T + (tc + 1) * 128, :], 128, tc * 128)
        store_states(pconv, ph)
        load_states()
        load_x(xs[0:1, :], 1, 0)
        a_layer(0, 1)
        a_layer(1, 1)
        kv_phase(0, 1, True)
        for g in range(3):
            lbg = 128 * DIL[g]
            P.dma("sp", d2dsem, lambda e, g=g, lbg=lbg: e.dma_start(out=skv_o[g][0:lbg - 1, :], in_=skv[g][1:lbg, :]), writes=[Bout])
            P.dma("sp", d2dsem, lambda e, g=g, lbg=lbg: e.dma_start(out=skv_o[g][lbg - 1:lbg, :], in_=kvt[0][0:1, g * 1024:(g + 1) * 1024]),
                  reads=[Bkvt[0]], writes=[Bout])
        b_layer(0, 1, 0, True)
        b_layer(1, 1, 0, True)
        store_x(ys[0:1, :], 1, 0)
        store_states(sconv_o, sh_o)
        fin = [(sm, P.dcnt[id(sm)]) for sm in (kvosem, outsem, xiosem, d2dsem) if P.dcnt[id(sm)] > 0]
        P.q["sp"].append((None, fin, None))
        with nc.Block() as block:
            P.replay(block)
    return nc


def _rel_bucket(dist):
    dist = np.asarray(dist)
    d = np.maximum(dist, 1).astype(np.float32)
    large = 16 + (np.log(d / 16) / np.log(2048 / 16) * (32 - 16)).astype(np.int32)
    large = np.minimum(large, 31)
    return np.where(dist < 16, dist, large).astype(np.int32)


def _etiles(rel_bias):
    ein = np.zeros((128, NE), np.float32)
    em = np.zeros((128, NE), np.float32)
    enew = np.zeros((1, 48), np.float32)
    p = np.arange(128)[:, None]

    def fill(off, g, kh, diff, valid):
        nq = diff.shape[1]
        bk = _rel_bucket(np.clip(diff, 0, 128) * DIL[g])
        for s in range(4):
            col = g * 16 + 4 * kh + s
            ein[:, off + s * nq:off + (s + 1) * nq] = rel_bias[bk, col]
            em[:, off + s * nq:off + (s + 1) * nq] = valid.astype(np.float32)

    for kh in range(4):
        q = np.arange(128)[None, :]
        fill(e0_off(0, kh), 0, kh, q - p, (q - p) >= 0)
        fill(e0_off(1, kh), 0, kh, 128 + q - p, (128 + q - p) <= 128)
        q = np.arange(64)[None, :]
        for half in range(2):
            d = 64 * half + q - p
            fill(e1_off(half, 0, kh), 1, kh, d, d >= 0)
            d = 128 + 64 * half + q - p
            fill(e1_off(half, 1, kh), 1, kh, d, d <= 128)
        q = np.arange(16)[None, :]
        for b in range(NB):
            d = 16 * b + q - p
            fill(e2_off(b, kh), 2, kh, d, d >= 0)
        for g in range(3):
            d = 128 - p + np.zeros((1, 1), np.int64)
            fill(es_off(g, kh), g, kh, d, d >= 0)
            for s in range(4):
                enew[0, (g * 4 + kh) * 4 + s] = rel_bias[0, g * 16 + 4 * kh + s]
    return ein, em, enew


def _colvec(v):
    v = np.asarray(v, np.float32).reshape(-1, 32, 128)
    return np.ascontiguousarray(v.transpose(2, 0, 1).reshape(128, -1))


_NC_CACHE = {}


def kernel(x_prompt, x_sample, state_conv, state_h, state_kv_w128, state_kv_w512, state_kv_w2048,
           a_pre_g, a_w_in, a_conv_w, a_conv_b, a_w_gate_a, a_b_gate_a, a_w_gate_x, a_b_gate_x,
           a_lambda, a_w_out, a_post_g, kv_norm_g, w_kv, rel_bias, b_pre_g, b_w_in, b_w_out, b_post_g):
    f = lambda a: np.ascontiguousarray(np.asarray(a, np.float32))
    vecs = np.concatenate([
        _colvec(a_pre_g), _colvec(np.asarray(a_conv_w)), _colvec(a_conv_b),
        _colvec(np.asarray(a_b_gate_a).reshape(2, 4096)), _colvec(np.asarray(a_b_gate_x).reshape(2, 4096)),
        _colvec(a_lambda), _colvec(a_post_g), _colvec(kv_norm_g), _colvec(b_pre_g), _colvec(b_post_g)], axis=1)
    assert vecs.shape == (128, NV)
    ein, em, enew = _etiles(np.asarray(rel_bias, np.float32))
    def tile_w(w):
        w = np.asarray(w, np.float32)
        lead = w.shape[:-2]
        K, M = w.shape[-2:]
        w = w.reshape(lead + (K // 128, 128, M // 128, 128))
        nl = len(lead)
        perm = tuple(range(nl)) + (nl + 2, nl + 1, nl + 0, nl + 3)
        return np.ascontiguousarray(w.transpose(perm)).reshape(lead + (M // 128, 128, K))

    shared = dict(a_w_in=tile_w(a_w_in), a_w_out=tile_w(a_w_out), a_wga=f(a_w_gate_a), a_wgx=f(a_w_gate_x), w_kv=tile_w(w_kv),
                  b_w_in=tile_w(b_w_in), b_w_out=tile_w(b_w_out), vecs=f(vecs), ein=ein, em=em, enew=enew,
                  ident=np.eye(128, dtype=np.float32))
    xp = f(x_prompt); xs = f(x_sample); sc = f(state_conv); shh = f(state_h)
    k0 = f(state_kv_w128).reshape(8, 128, 1024); k1 = f(state_kv_w512).reshape(8, 512, 1024); k2 = f(state_kv_w2048).reshape(8, 2048, 1024)
    in_maps = []
    for c in range(8):
        m = dict(shared)
        sq_, hf = c // 2, c % 2
        xpc = xp[sq_] if hf == 1 else np.concatenate([np.zeros((1024, 4096), np.float32), xp[sq_][:1024]], axis=0)
        fl = np.zeros((128, 2), np.float32); fl[:, 0] = hf; fl[:, 1] = 1.0; fl[:64, 1] = hf
        m.update(xp=xpc, flg=fl, xs=xs[c], sconv=np.ascontiguousarray(sc[:, c]), sh=np.ascontiguousarray(shh[:, c]),
                 skv0=k0[c], skv1=k1[c], skv2=k2[c])
        in_maps.append(m)
    if "nc" not in _NC_CACHE:
        _NC_CACHE["nc"] = build()
    res = run_bass_kernel_spmd(_NC_CACHE["nc"], in_maps, core_ids=list(range(8)))
    R = res.results
    y_prompt = np.stack([np.concatenate([R[2 * q]["yp"], R[2 * q + 1]["yp"]], axis=0) for q in range(4)])
    y_sample = np.stack([R[c]["ys"] for c in range(8)])
    p_conv = np.stack([R[2 * q + 1]["pconv"] for q in range(4)], axis=1)
    p_h = np.stack([R[2 * q + 1]["ph"] for q in range(4)], axis=1)
    pkv = np.stack([R[2 * q + 1]["pkv"] for q in range(4)]).reshape(4, 2048, 3, 2, 4, 128)
    p_kv128 = np.ascontiguousarray(pkv[:, 2048 - 128:, 0])
    p_kv512 = np.ascontiguousarray(pkv[:, 2048 - 512:, 1])
    p_kv2048 = np.ascontiguousarray(pkv[:, :, 2])
    s_conv = np.stack([R[c]["sconv_o"] for c in range(8)], axis=1)
    s_h = np.stack([R[c]["sh_o"] for c in range(8)], axis=1)
    s_kv128 = np.stack([R[c]["skv_o0"] for c in range(8)]).reshape(8, 128, 2, 4, 128)
    s_kv512 = np.stack([R[c]["skv_o1"] for c in range(8)]).reshape(8, 512, 2, 4, 128)
    s_kv2048 = np.stack([R[c]["skv_o2"] for c in range(8)]).reshape(8, 2048, 2, 4, 128)
    f32 = lambda a: np.asarray(a, np.float32)
    return tuple(f32(a) for a in (y_prompt, y_sample, p_conv, p_h, p_kv128, p_kv512, p_kv2048, s_conv, s_h, s_kv128, s_kv512, s_kv2048))
```

```python
import contextlib
import numpy as np
import concourse.bass as bass
import concourse.mybir as mybir
from concourse.bass_utils import run_bass_kernel_spmd

F32 = mybir.dt.float32
BF16 = mybir.dt.bfloat16
AF = mybir.ActivationFunctionType
ALU = mybir.AluOpType

SAME_ENGINE_SYNC = True
T = 256
NB = 2048 // T
EPS = 1e-6
SCALE = 128 ** -0.5
DIL = (1, 4, 16)


class Buf:
    __slots__ = ("name", "w", "r")

    def __init__(self, name=""):
        self.name = name
        self.w = None
        self.r = {}


class Prog:
    ENGS = ("pe", "act", "dve", "pool", "sp")

    def __init__(self, nc, stack):
        self.nc = nc
        self.stack = stack
        self.q = {e: [] for e in self.ENGS}
        self.cnt = {e: 0 for e in self.ENGS}
        self.seen = {e: {} for e in self.ENGS}
        self.esem = {e: stack.enter_context(nc.semaphore("es_" + e)) for e in self.ENGS}
        self.dcnt = {}

    def dma_sem(self, name):
        s = self.stack.enter_context(self.nc.semaphore(name))
        self.dcnt[id(s)] = 0
        return s

    def sb(self, name, shape, dt):
        return self.stack.enter_context(self.nc.sbuf_tensor("sb_" + name, list(shape), dt))

    def ps(self, name, shape, dt=F32):
        return self.stack.enter_context(self.nc.psum_tensor("psum_" + name, list(shape), dt))

    def _deps(self, eng, reads, writes):
        deps = []
        for b in reads:
            if b.w is not None:
                deps.append(b.w)
        for b in writes:
            if b.w is not None:
                deps.append(b.w)
            deps.extend(b.r.values())
        waits = []
        seen = self.seen[eng]
        for (sem, val, src) in deps:
            if src == eng and (eng == "pe" or not SAME_ENGINE_SYNC):
                continue
            k = id(sem)
            if seen.get(k, 0) >= val:
                continue
            seen[k] = val
            waits.append((sem, val))
        return waits

    def _commit(self, tok, reads, writes):
        for b in writes:
            b.w = tok
            b.r = {}
        for b in reads:
            if b not in writes:
                b.r[id(tok[0])] = tok

    def op(self, eng, fn, reads=(), writes=(), inc=True):
        waits = self._deps(eng, reads, writes)
        if inc:
            self.cnt[eng] += 1
            tok = (self.esem[eng], self.cnt[eng], eng)
        else:
            tok = (self.esem[eng], self.cnt[eng] + 1, eng)
        self.q[eng].append((fn, waits, (self.esem[eng], 1) if inc else None))
        self._commit(tok, reads, writes)
        return tok

    def dma(self, eng, sem, fn, reads=(), writes=()):
        waits = self._deps(eng, reads, writes)
        self.dcnt[id(sem)] += 16
        tok = (sem, self.dcnt[id(sem)], "dma")
        self.q[eng].append((fn, waits, (sem, 16)))
        self._commit(tok, reads, writes)
        return tok

    def wait_all(self, eng, bufs):
        waits = self._deps(eng, (), bufs)
        self.q[eng].append((None, waits, None))

    def replay(self, block):
        names = {"pe": "tensor", "act": "scalar", "dve": "vector", "pool": "gpsimd", "sp": "sync"}

        def run(e, engobj):
            for fn, waits, inc in self.q[e]:
                for sem, val in waits:
                    engobj.wait_ge(sem, val)
                if fn is None:
                    continue
                ins = fn(engobj)
                if inc is not None:
                    ins.then_inc(inc[0], inc[1])

        for e in self.ENGS:
            if not self.q[e]:
                continue

            def mk(e):
                def body(engobj):
                    run(e, engobj)
                return body
            getattr(block, names[e])(mk(e))


V_APRE, V_CW, V_CB, V_BGA, V_BGX, V_LAM, V_APOST, V_KVG, V_BPRE, V_BPOST = 0, 64, 320, 384, 448, 512, 576, 640, 672, 736
NV = 800
E0 = 0
E1 = 4096
E2 = 8192
ES = 10240
NE = 10240 + 48


def e0_off(pc, kh): return E0 + (pc * 4 + kh) * 512
def e1_off(half, pc, kh): return E1 + ((half * 2 + pc) * 4 + kh) * 256
def e2_off(b, kh): return E2 + (b * 4 + kh) * 64
def es_off(g, kh): return ES + (g * 4 + kh) * 4


def build():
    nc = bass.Bass("TRN2", target_bir_lowering=False)

    def D(name, shape, dt=F32, kind="ExternalInput"):
        return nc.dram_tensor(name, list(shape), dt, kind=kind).ap()

    xp = D("xp", [2048, 4096]); xs = D("xs", [1, 4096])
    sconv = D("sconv", [2, 3, 4096]); sh = D("sh", [2, 4096])
    skv = [D("skv0", [128, 1024]), D("skv1", [512, 1024]), D("skv2", [2048, 1024])]
    a_w_in = D("a_w_in", [2, 64, 128, 4096]); a_w_out = D("a_w_out", [2, 32, 128, 4096])
    a_wga = D("a_wga", [2, 16, 256, 256]); a_wgx = D("a_wgx", [2, 16, 256, 256])
    w_kv = D("w_kv", [24, 128, 4096])
    b_w_in = D("b_w_in", [2, 64, 128, 4096]); b_w_out = D("b_w_out", [2, 32, 128, 2048])
    vecs_d = D("vecs", [128, NV]); ein_d = D("ein", [128, NE]); em_d = D("em", [128, NE])
    enew_d = D("enew", [1, 48]); ident_d = D("ident", [128, 128]); flg_d = D("flg", [128, 2])
    O = lambda n, s: D(n, s, kind="ExternalOutput")
    yp = O("yp", [1024, 4096]); ys = O("ys", [1, 4096])
    pconv = O("pconv", [2, 3, 4096]); ph = O("ph", [2, 4096]); pkv = O("pkv", [2048, 3072])
    sconv_o = O("sconv_o", [2, 3, 4096]); sh_o = O("sh_o", [2, 4096])
    skv_o = [O("skv_o0", [128, 1024]), O("skv_o1", [512, 1024]), O("skv_o2", [2048, 1024])]
    kvs = nc.dram_tensor("kvs", [2048 + 16, 3072], F32).ap()
    wsc = {}
    for l in range(2):
        wsc[("ain", l)] = nc.dram_tensor("wsc_ain%d" % l, [64, 128, 4096], BF16).ap()
        wsc[("aout", l)] = nc.dram_tensor("wsc_aout%d" % l, [32, 128, 4096], BF16).ap()
        wsc[("bin", l)] = nc.dram_tensor("wsc_bin%d" % l, [64, 128, 4096], BF16).ap()
        wsc[("bout", l)] = nc.dram_tensor("wsc_bout%d" % l, [32, 128, 2048], BF16).ap()
    wsc[("kv", 0)] = nc.dram_tensor("wsc_kv", [24, 128, 4096], BF16).ap()
    wdone = {}

    with contextlib.ExitStack() as st:
        P = Prog(nc, st)
        xT = P.sb("xT", [128, 32, T], F32); Bx = [Buf() for _ in range(32)]
        ub = P.sb("ub", [128, 32, T], BF16); Bu = [Buf() for _ in range(32)]
        hg = P.sb("hg", [128, 32, T], BF16); Bh = [Buf() for _ in range(32)]
        NW = 3
        wbf = [P.sb("wbf%d" % i, [128, 32, 128], BF16) for i in range(NW)]; Bw = [Buf() for _ in range(NW)]
        wsem = [P.dma_sem("wsem%d" % i) for i in range(NW)]
        ssem = [P.dma_sem("ssem%d" % i) for i in range(NW)]
        wg = P.sb("wg", [128, 2, 2, 256], BF16); Bwg = Buf(); wgsem = P.dma_sem("wgsem")
        vecs = P.sb("vecs", [128, NV], F32); Bv = Buf()
        c1 = P.sb("c1", [128, 64], F32); Bc1 = Buf()
        Eall = P.sb("Eall", [128, NE], BF16); BE = Buf()
        enew = P.sb("enew", [1, 48], F32); Benew = Buf()
        ident = P.sb("ident", [128, 128], F32); Bid = Buf()
        flg = P.sb("flg", [128, 2], F32); Bflg = Buf(); flgsem = P.dma_sem("flgsem")
        ones = P.sb("ones", [128, 128], BF16); Bones = Buf()
        rs = P.sb("rs", [128, T], F32); Brs = Buf()
        sqb = [P.sb("sqb%d" % i, [128, T], BF16) for i in range(2)]; Bsq = [Buf(), Buf()]
        stt = [P.sb("st%d" % l, [128, 4, 32], F32) for l in range(2)]; Bst = [Buf(), Buf()]
        stT = P.sb("stT", [128, 128], F32); BstT = Buf()
        xc = [P.sb("xc%d" % i, [128, 3 + T], F32) for i in range(2)]; Bxc = [Buf(), Buf()]
        xv = [P.sb("xv%d" % i, [128, T], F32) for i in range(2)]; Bxv = [Buf(), Buf()]
        xvb = [P.sb("xvb%d" % i, [128, T], BF16) for i in range(2)]; Bxvb = [Buf(), Buf()]
        rg = [[P.sb("rg%d%d" % (a, j), [128, T], F32) for j in range(2)] for a in range(2)]
        Brg = [[Buf(), Buf()], [Buf(), Buf()]]
        ta = P.sb("ta", [128, T], F32); Bta = Buf()
        tb = P.sb("tb", [128, T], F32); Btb = Buf()
        tcc = P.sb("tcc", [128, T], F32); Btc = Buf()
        hh = [P.sb("hh%d" % i, [128, T], F32) for i in range(2)]; Bhh = [Buf(), Buf()]
        sg = P.sb("sg", [128, T], F32); Bsg = Buf()
        tmpx = P.sb("tmpx", [128, T], F32); Btmpx = Buf()
        xio = P.sb("xio", [128, 4096], F32); Bxio = Buf(); xiosem = P.dma_sem("xiosem")
        kvt = [P.sb("kvt%d" % i, [128, 3072], F32) for i in range(2)]; Bkvt = [Buf(), Buf()]
        kvosem = P.dma_sem("kvosem")
        q4 = P.sb("q4", [128, 4, T], BF16); Bq4 = Buf()
        num = P.sb("num", [128, 4, T], F32); Bnum = Buf()
        den = P.sb("den", [128, 4, T], F32); Bden = Buf()
        kvin = [P.sb("kvin%d" % i, [128, 2, 128], F32) for i in range(2)]; Bkvin = [Buf(), Buf()]
        kvinsem = [P.dma_sem("kvinsem%d" % i) for i in range(2)]
        kTb = [P.sb("kTb%d" % i, [128, 128], BF16) for i in range(2)]; BkT = [Buf(), Buf()]
        vbb = [P.sb("vbb%d" % i, [128, 128], BF16) for i in range(2)]; Bvb = [Buf(), Buf()]
        pT = [P.sb("pT%d" % i, [128, 512], BF16) for i in range(2)]; BpT = [Buf(), Buf()]
        esb = P.sb("esb", [128, 1024], F32); Besb = Buf()
        emb = P.sb("emb", [128, 1024], F32); Bemb = Buf()
        misc = P.sb("misc", [128, 64], F32); Bmisc = Buf()
        knew = P.sb("knew", [128, 16], BF16); Bknew = Buf()
        vnew = P.sb("vnew", [1, 128], BF16); Bvnew = Buf()
        pnew = P.sb("pnew", [1, 16], BF16); Bpnew = Buf()
        ps = P.ps("ps", [128, 8, 512], F32); Bps = [Buf() for _ in range(8)]
        setsem = [P.dma_sem("setsem%d" % i) for i in range(6)]
        outsem = P.dma_sem("outsem"); Bout = Buf()
        d2dsem = P.dma_sem("d2dsem"); Bd2d = Buf()

        P.dma("sp", setsem[0], lambda e: e.dma_start(out=vecs[:, :], in_=vecs_d), writes=[Bv])
        P.dma("sp", setsem[1], lambda e: e.dma_start(out=enew[:, :], in_=enew_d), writes=[Benew])
        P.dma("sp", flgsem, lambda e: e.dma_start(out=flg[:, :], in_=flg_d), writes=[Bflg])
        P.dma("sp", setsem[5], lambda e: e.dma_start(out=ident[:, :], in_=ident_d), writes=[Bid])
        P.op("pool", lambda e: e.memset(ones[:, :], 1.0), writes=[Bones])
        for i in range(2):
            P.op("pool", lambda e, i=i: e.memset(kvin[i][:, :, :], 0.0), writes=[Bkvin[i]])
            P.op("pool", lambda e, i=i: e.memset(kvt[i][:, :], 0.0), writes=[Bkvt[i]])
        for l in range(2):
            P.op("pool", lambda e, l=l: e.memset(stt[l][:, :, :], 0.0), writes=[Bst[l]])
        P.op("act", lambda e: e.activation(out=c1[:, :], in_=vecs[:, V_LAM:V_LAM + 64], func=AF.Exp, scale=-1.0), reads=[Bv], writes=[Bc1])
        P.op("act", lambda e: e.activation(out=c1[:, :], in_=c1[:, :], func=AF.Ln, bias=1.0, scale=1.0), reads=[Bc1], writes=[Bc1])
        P.op("dve", lambda e: e.tensor_scalar(out=c1[:, :], in0=c1[:, :], scalar1=-8.0, scalar2=None, op0=ALU.mult), reads=[Bc1], writes=[Bc1])
        for c0 in range(0, NE, 1024):
            cw = min(1024, NE - c0)
            P.dma("sp", setsem[2], lambda e, c0=c0, cw=cw: e.dma_start(out=esb[:, 0:cw], in_=ein_d[:, c0:c0 + cw]), writes=[Besb])
            P.dma("sp", setsem[3], lambda e, c0=c0, cw=cw: e.dma_start(out=emb[:, 0:cw], in_=em_d[:, c0:c0 + cw]), writes=[Bemb])
            P.op("act", lambda e, cw=cw: e.activation(out=esb[:, 0:cw], in_=esb[:, 0:cw], func=AF.Exp), reads=[Besb], writes=[Besb])
            P.op("dve", lambda e, c0=c0, cw=cw: e.tensor_tensor(out=Eall[:, c0:c0 + cw], in0=esb[:, 0:cw], in1=emb[:, 0:cw], op=ALU.mult),
                 reads=[Besb, Bemb], writes=[BE])
        P.op("act", lambda e: e.activation(out=enew[:, :], in_=enew[:, :], func=AF.Exp), reads=[Benew], writes=[Benew])

        wctr = [0]
        pctr = [0]

        def wload(Wsl, nk, col0, key):
            s_ = wctr[0] % NW; wctr[0] += 1
            strip = col0 // 128
            dst = wbf[s_][:, 0:nk, :].rearrange("p k m -> p (k m)")
            kk = (key, strip)
            if kk not in wdone:
                P.dma("pool", wsem[s_], lambda e: e.dma_start(out=dst, in_=Wsl[strip], max_dma_last_dim=8192), writes=[Bw[s_]])
                bb = Buf(); wdone[kk] = bb
                P.dma("sp", ssem[s_], lambda e: e.dma_start(out=wsc[key][strip], in_=dst), reads=[Bw[s_]], writes=[bb])
            else:
                P.dma("pool", wsem[s_], lambda e: e.dma_start(out=dst, in_=wsc[key][strip]), reads=[wdone[kk]], writes=[Bw[s_]])
            return s_

        def lin(Wsl, nk, cols, src, Bsrc, Tn, consumer, key):
            slots = {}
            for i0 in range(min(2, len(cols))):
                slots[i0] = wload(Wsl, nk, cols[i0], key)
            for idx, col0 in enumerate(cols):
                s = slots.pop(idx)
                pb = pctr[0] % 3; pctr[0] += 1
                for k in range(nk):
                    P.op("pe", lambda e, s=s, k=k, pb=pb: e.matmul(ps[:, pb, 0:Tn], lhsT=wbf[s][:, k, :], rhs=src[:, k, 0:Tn],
                                                                  start=(k == 0), stop=(k == nk - 1)),
                         reads=[Bw[s], Bsrc[k]], writes=[Bps[pb]], inc=(k == nk - 1))
                if idx + 2 < len(cols):
                    slots[idx + 2] = wload(Wsl, nk, cols[idx + 2], key)
                consumer(idx, pb)

        def rstd_from_ps3(Tn):
            P.op("act", lambda e: e.activation(out=rs[:, 0:Tn], in_=ps[:, 3, 0:Tn], func=AF.Sqrt, bias=EPS, scale=1.0 / 4096), reads=[Bps[3]], writes=[Brs])
            P.op("dve", lambda e: e.reciprocal(out=rs[:, 0:Tn], in_=rs[:, 0:Tn]), reads=[Brs], writes=[Brs])

        def norm_pre(gcol, Tn):
            for k in range(32):
                i = k % 2
                P.op("dve", lambda e, k=k, i=i: e.tensor_tensor(out=sqb[i][:, 0:Tn], in0=xT[:, k, 0:Tn], in1=xT[:, k, 0:Tn], op=ALU.mult),
                     reads=[Bx[k]], writes=[Bsq[i]])
                P.op("pe", lambda e, k=k, i=i: e.matmul(ps[:, 3, 0:Tn], lhsT=ones[:, :], rhs=sqb[i][:, 0:Tn], start=(k == 0), stop=(k == 31)),
                     reads=[Bsq[i], Bones], writes=[Bps[3]], inc=True)
            rstd_from_ps3(Tn)
            for k in range(32):
                P.op("dve", lambda e, k=k: e.scalar_tensor_tensor(out=ub[:, k, 0:Tn], in0=xT[:, k, 0:Tn], scalar=vecs[:, gcol + k:gcol + k + 1],
                                                                  in1=rs[:, 0:Tn], op0=ALU.mult, op1=ALU.mult),
                     reads=[Bx[k], Brs, Bv], writes=[Bu[k]])

        def out_consumer(Tn):
            def cons(m, pb):
                i = m % 2
                P.op("act", lambda e, m=m, pb=pb: e.activation(out=ub[:, m, 0:Tn], in_=ps[:, pb, 0:Tn], func=AF.Copy), reads=[Bps[pb]], writes=[Bu[m]])
                P.op("act", lambda e, i=i, pb=pb: e.activation(out=sqb[i][:, 0:Tn], in_=ps[:, pb, 0:Tn], func=AF.Square), reads=[Bps[pb]], writes=[Bsq[i]])
                P.op("pe", lambda e, m=m, i=i: e.matmul(ps[:, 3, 0:Tn], lhsT=ones[:, :], rhs=sqb[i][:, 0:Tn], start=(m == 0), stop=(m == 31)),
                     reads=[Bsq[i], Bones], writes=[Bps[3]], inc=True)
            return cons

        def post(gcol, Tn):
            rstd_from_ps3(Tn)
            for k in range(32):
                P.op("dve", lambda e, k=k: e.scalar_tensor_tensor(out=tmpx[:, 0:Tn], in0=ub[:, k, 0:Tn], scalar=vecs[:, gcol + k:gcol + k + 1],
                                                                  in1=rs[:, 0:Tn], op0=ALU.mult, op1=ALU.mult),
                     reads=[Bu[k], Brs, Bv], writes=[Btmpx])
                P.op("dve", lambda e, k=k: e.tensor_tensor(out=xT[:, k, 0:Tn], in0=xT[:, k, 0:Tn], in1=tmpx[:, 0:Tn], op=ALU.add),
                     reads=[Btmpx, Bx[k]], writes=[Bx[k]])

        def a_layer(l, Tn):
            norm_pre(V_APRE + 32 * l, Tn)
            S = stt[l]; BS = Bst[l]
            cols = []
            for h in range(16):
                cols += [(2 * h) * 128, (2 * h + 1) * 128, 4096 + (2 * h) * 128, 4096 + (2 * h + 1) * 128]

            def cons(idx, pb):
                h, j = idx // 4, idx % 4
                if j < 2:
                    c = 2 * h + j
                    P.op("dve", lambda e: e.tensor_copy(out=xc[j][:, 0:3], in_=S[:, 0:3, c]), reads=[BS], writes=[Bxc[j]])
                    P.op("act", lambda e: e.activation(out=xc[j][:, 3:3 + Tn], in_=ps[:, pb, 0:Tn], func=AF.Copy), reads=[Bps[pb]], writes=[Bxc[j]])
                    P.op("dve", lambda e: e.tensor_copy(out=S[:, 0:3, c], in_=xc[j][:, Tn:Tn + 3]), reads=[Bxc[j]], writes=[BS])
                    cwc = V_CW + l * 128
                    P.op("dve", lambda e: e.tensor_scalar(out=xv[j][:, 0:Tn], in0=xc[j][:, 3:3 + Tn], scalar1=vecs[:, cwc + 96 + c:cwc + 97 + c],
                                                          scalar2=vecs[:, V_CB + 32 * l + c:V_CB + 32 * l + c + 1], op0=ALU.mult, op1=ALU.add),
                         reads=[Bxc[j], Bv], writes=[Bxv[j]])
                    for kk in range(3):
                        P.op("dve", lambda e, kk=kk: e.scalar_tensor_tensor(out=xv[j][:, 0:Tn], in0=xc[j][:, kk:kk + Tn],
                                                                            scalar=vecs[:, cwc + 32 * kk + c:cwc + 32 * kk + c + 1],
                                                                            in1=xv[j][:, 0:Tn], op0=ALU.mult, op1=ALU.add),
                             reads=[Bxc[j], Bv, Bxv[j]], writes=[Bxv[j]])
                    P.op("pool", lambda e: e.tensor_copy(out=xvb[j][:, 0:Tn], in_=xv[j][:, 0:Tn]), reads=[Bxv[j]], writes=[Bxvb[j]])
                else:
                    jo = j - 2
                    c = 2 * h + jo
                    if jo == 0:
                        P.dma("pool", wgsem, lambda e: e.dma_start(out=wg[:, 0, :, :], in_=a_wga[l, h].rearrange("(ic p) o -> p ic o", p=128)), writes=[Bwg])
                        P.dma("pool", wgsem, lambda e: e.dma_start(out=wg[:, 1, :, :], in_=a_wgx[l, h].rearrange("(ic p) o -> p ic o", p=128)), writes=[Bwg])
                        for a in range(2):
                            bcol = (V_BGA if a == 0 else V_BGX) + 32 * l
                            for jo in range(2):
                                gb = 4 + (a * 2 + jo) % 2
                                for ic in range(2):
                                    P.op("pe", lambda e, a=a, jo=jo, ic=ic, gb=gb: e.matmul(ps[:, gb, 0:Tn], lhsT=wg[:, a, ic, jo * 128:(jo + 1) * 128],
                                                                                            rhs=xvb[ic][:, 0:Tn], start=(ic == 0), stop=(ic == 1)),
                                         reads=[Bwg, Bxvb[ic]], writes=[Bps[gb]], inc=(ic == 1))
                                cc = 2 * h + jo
                                P.op("act", lambda e, a=a, jo=jo, gb=gb, cc=cc, bcol=bcol: e.activation(
                                    out=rg[a][jo][:, 0:Tn], in_=ps[:, gb, 0:Tn], func=AF.Sigmoid, bias=vecs[:, bcol + cc:bcol + cc + 1], scale=1.0),
                                    reads=[Bps[gb], Bv], writes=[Brg[a][jo]])
                        for jo in range(2):
                            cc = 2 * h + jo
                            P.op("act", lambda e, jo=jo, cc=cc: e.activation(out=ta[:, 0:Tn], in_=rg[0][jo][:, 0:Tn], func=AF.Exp,
                                                                             scale=c1[:, 32 * l + cc:32 * l + cc + 1]),
                                 reads=[Brg[0][jo], Bc1], writes=[Bta])
                            P.op("dve", lambda e: e.tensor_tensor(out=tb[:, 0:Tn], in0=ta[:, 0:Tn], in1=ta[:, 0:Tn], op=ALU.mult), reads=[Bta], writes=[Btb])
                            P.op("act", lambda e: e.activation(out=tb[:, 0:Tn], in_=tb[:, 0:Tn], func=AF.Sqrt, bias=1.0, scale=-1.0), reads=[Btb], writes=[Btb])
                            P.op("dve", lambda e, jo=jo: e.tensor_tensor(out=tcc[:, 0:Tn], in0=rg[1][jo][:, 0:Tn], in1=xv[jo][:, 0:Tn], op=ALU.mult),
                                 reads=[Brg[1][jo], Bxv[jo]], writes=[Btc])
                            P.op("dve", lambda e: e.tensor_tensor(out=tcc[:, 0:Tn], in0=tcc[:, 0:Tn], in1=tb[:, 0:Tn], op=ALU.mult), reads=[Btc, Btb], writes=[Btc])
                            P.op("dve", lambda e, jo=jo, cc=cc: e.tensor_tensor_scan(out=hh[jo][:, 0:Tn], data0=ta[:, 0:Tn], data1=tcc[:, 0:Tn],
                                                                                     initial=S[:, 3, cc:cc + 1], op0=ALU.mult, op1=ALU.add),
                                 reads=[Bta, Btc, BS], writes=[Bhh[jo]])
                            P.op("dve", lambda e, jo=jo, cc=cc: e.tensor_copy(out=S[:, 3, cc:cc + 1], in_=hh[jo][:, Tn - 1:Tn]), reads=[Bhh[jo]], writes=[BS])
                    jo = j - 2
                    P.op("act", lambda e: e.activation(out=sg[:, 0:Tn], in_=ps[:, pb, 0:Tn], func=AF.Silu), reads=[Bps[pb]], writes=[Bsg])
                    P.op("dve", lambda e: e.tensor_tensor(out=hg[:, c, 0:Tn], in0=hh[jo][:, 0:Tn], in1=sg[:, 0:Tn], op=ALU.mult),
                         reads=[Bhh[jo], Bsg], writes=[Bh[c]])

            lin(a_w_in[l], 32, cols, ub, Bu, Tn, cons, ("ain", l))
            lin(a_w_out[l], 32, [m * 128 for m in range(32)], hg, Bh, Tn, out_consumer(Tn), ("aout", l))
            post(V_APOST + 32 * l, Tn)

        def kv_phase(row0, Tn, sample):
            norm_pre(V_KVG, Tn)
            ntc = max(1, Tn // 128)
            M = min(128, Tn)
            slots = {0: wload(w_kv, 32, 0, ("kv", 0)), 1: wload(w_kv, 32, 128, ("kv", 0))}
            for n in range(24):
                s = slots.pop(n)
                if n + 2 < 24 and n >= 1:
                    pass
                for tc in range(ntc):
                    pb = pctr[0] % 3; pctr[0] += 1
                    for k in range(32):
                        P.op("pe", lambda e, s=s, k=k, pb=pb, tc=tc: e.matmul(ps[0:M, pb, 0:128], lhsT=ub[:, k, tc * 128:tc * 128 + M], rhs=wbf[s][:, k, :],
                                                                              start=(k == 0), stop=(k == 31)),
                             reads=[Bw[s], Bu[k]], writes=[Bps[pb]], inc=(k == 31))
                    P.op("act", lambda e, pb=pb, tc=tc, n=n: e.activation(out=kvt[tc][0:M, n * 128:(n + 1) * 128], in_=ps[0:M, pb, 0:128], func=AF.Copy),
                         reads=[Bps[pb]], writes=[Bkvt[tc]])
                if n + 2 < 24:
                    slots[n + 2] = wload(w_kv, 32, (n + 2) * 128, ("kv", 0))
            if not sample:
                for tc in range(ntc):
                    r0 = row0 + tc * 128
                    P.dma("sp", kvosem, lambda e, tc=tc, r0=r0: e.dma_start(out=pkv[r0:r0 + 128, :], in_=kvt[tc][:, :]), reads=[Bkvt[tc]], writes=[Bout])
                    P.dma("sp", kvosem, lambda e, tc=tc, r0=r0: e.dma_start(out=kvs[r0:r0 + 128, :], in_=kvt[tc][:, :]), reads=[Bkvt[tc]], writes=[Bd2d])

        uctr = [0]

        def key_tile(src_rows_ap, nrows, g, kh):
            i = uctr[0] % 2; uctr[0] += 1
            P.dma("sp", kvinsem[i], lambda e: e.dma_start(out=kvin[i][0:nrows, :, :], in_=src_rows_ap), reads=[Bd2d], writes=[Bkvin[i]])
            tb_ = 4 + i
            P.op("pe", lambda e: e.transpose(out=ps[:, tb_, 0:128], in_=kvin[i][:, 0, :], identity=ident[:, :]), reads=[Bkvin[i], Bid], writes=[Bps[tb_]])
            P.op("act", lambda e: e.activation(out=kTb[i][:, :], in_=ps[:, tb_, 0:128], func=AF.Copy), reads=[Bps[tb_]], writes=[BkT[i]])
            P.op("pool", lambda e: e.tensor_copy(out=vbb[i][:, :], in_=kvin[i][:, 1, :]), reads=[Bkvin[i]], writes=[Bvb[i]])
            return i

        def attn_unit(qap, nq, tiles, acc_first, accap_fn):
            W = 4 * nq
            for ti, (i, eoff, fcol) in enumerate(tiles):
                sb_ = 4 + i
                P.op("pe", lambda e, i=i, sb_=sb_: e.matmul(ps[:, sb_, 0:W].rearrange("p (s q) -> p s q", s=4), lhsT=kTb[i][:, :], rhs=qap, start=True, stop=True),
                     reads=[BkT[i], Bq4], writes=[Bps[sb_]])
                P.op("act", lambda e, i=i, sb_=sb_: e.activation(out=pT[i][:, 0:W], in_=ps[:, sb_, 0:W], func=AF.Exp, scale=SCALE), reads=[Bps[sb_]], writes=[BpT[i]])
                if fcol is None:
                    P.op("dve", lambda e, i=i, eoff=eoff: e.tensor_tensor(out=pT[i][:, 0:W], in0=pT[i][:, 0:W], in1=Eall[:, eoff:eoff + W], op=ALU.mult),
                         reads=[BpT[i], BE], writes=[BpT[i]])
                else:
                    P.op("dve", lambda e, i=i, eoff=eoff, fcol=fcol: e.scalar_tensor_tensor(
                        out=pT[i][:, 0:W], in0=pT[i][:, 0:W], scalar=flg[:, fcol:fcol + 1], in1=Eall[:, eoff:eoff + W], op0=ALU.mult, op1=ALU.mult),
                         reads=[BpT[i], BE, Bflg], writes=[BpT[i]])
                first, last = ti == 0, ti == len(tiles) - 1
                P.op("pe", lambda e, i=i, first=first, last=last: e.matmul(ps[:, 6, 0:W], lhsT=vbb[i][:, :], rhs=pT[i][:, 0:W], start=first, stop=last),
                     reads=[Bvb[i], BpT[i]], writes=[Bps[6]])
                P.op("pe", lambda e, i=i, first=first, last=last: e.matmul(ps[:, 7, 0:W], lhsT=ones[:, :], rhs=pT[i][:, 0:W], start=first, stop=last),
                     reads=[Bones, BpT[i]], writes=[Bps[7]])
            for (acc, Bacc, bank) in ((num, Bnum, 6), (den, Bden, 7)):
                src = ps[:, bank, 0:W].rearrange("p (s q) -> p s q", s=4)
                if acc_first:
                    P.op("act", lambda e, acc=acc, src=src: e.activation(out=accap_fn(acc), in_=src, func=AF.Copy), reads=[Bps[bank]], writes=[Bacc])
                else:
                    P.op("dve", lambda e, acc=acc, src=src: e.tensor_tensor(out=accap_fn(acc), in0=accap_fn(acc), in1=src, op=ALU.add),
                         reads=[Bps[bank], Bacc], writes=[Bacc])

        def attention_prompt(b, g, kh):
            t0 = b * T
            kc = g * 1024 + kh * 128

            def rows(start, step, count):
                v = kvs[start:start + step * count, :].rearrange("(j c) n -> j c n", c=step)[:, 0, :]
                return v[:, g * 1024:(g + 1) * 1024].rearrange("j (v k x) -> j v k x", v=2, k=4)[:, :, kh, :]

            if g == 0:
                for qb in range(T // 128):
                    tiles = []
                    if t0 + qb * 128 - 128 >= 0:
                        i = key_tile(rows(t0 + qb * 128 - 128, 1, 128), 128, g, kh)
                        tiles.append((i, e0_off(1, kh), 0 if (t0 + qb * 128 - 128) < 1024 else None))
                    i = key_tile(rows(t0 + qb * 128, 1, 128), 128, g, kh)
                    tiles.append((i, e0_off(0, kh), None))
                    qap = q4[:, :, qb * 128:(qb + 1) * 128]
                    attn_unit(qap, 128, tiles, True, lambda acc, qb=qb: acc[:, :, qb * 128:(qb + 1) * 128])
            elif g == 1:
                nbi, half = b // 2, b % 2
                for c in range(4):
                    tiles = []
                    if nbi > 0:
                        i = key_tile(rows(512 * (nbi - 1) + c, 4, 128), 128, g, kh)
                        tiles.append((i, e1_off(half, 1, kh), 0 if 512 * (nbi - 1) < 1024 else None))
                    cnt = 64 * (half + 1)
                    i = key_tile(rows(512 * nbi + c, 4, cnt), cnt, g, kh)
                    tiles.append((i, e1_off(half, 0, kh), None))
                    qap = q4[:, :, :].rearrange("p s (j c) -> p s c j", c=4)[:, :, c, :]
                    attn_unit(qap, 64, tiles, False, lambda acc, c=c: acc[:, :, :].rearrange("p s (j c) -> p s c j", c=4)[:, :, c, :])
            else:
                for c in range(16):
                    cnt = 16 * (b + 1)
                    i = key_tile(rows(c, 16, cnt), cnt, g, kh)
                    qap = q4[:, :, :].rearrange("p s (j c) -> p s c j", c=16)[:, :, c, :]
                    attn_unit(qap, 16, [(i, e2_off(b, kh), 1)], False, lambda acc, c=c: acc[:, :, :].rearrange("p s (j c) -> p s c j", c=16)[:, :, c, :])

        def attention_sample(g, kh):
            r = DIL[g]
            lb = 128 * r
            src = skv[g].rearrange("(j c) (v x) -> j c v x", c=r, v=2)[:, 0, :, kh * 128:(kh + 1) * 128]
            i = uctr[0] % 2; uctr[0] += 1
            P.dma("sp", kvinsem[i], lambda e: e.dma_start(out=kvin[i][:, :, :], in_=src), writes=[Bkvin[i]])
            tb_ = 4 + i
            P.op("pe", lambda e: e.transpose(out=ps[:, tb_, 0:128], in_=kvin[i][:, 0, :], identity=ident[:, :]), reads=[Bkvin[i], Bid], writes=[Bps[tb_]])
            P.op("act", lambda e: e.activation(out=kTb[i][:, :], in_=ps[:, tb_, 0:128], func=AF.Copy), reads=[Bps[tb_]], writes=[BkT[i]])
            P.op("pool", lambda e: e.tensor_copy(out=vbb[i][:, :], in_=kvin[i][:, 1, :]), reads=[Bkvin[i]], writes=[Bvb[i]])
            qap = q4[:, :, 0:1]
            P.op("pe", lambda e: e.matmul(ps[:, tb_, 0:4].rearrange("p (s q) -> p s q", s=4), lhsT=kTb[i][:, :], rhs=qap, start=True, stop=True),
                 reads=[BkT[i], Bq4], writes=[Bps[tb_]])
            P.op("act", lambda e: e.activation(out=pT[i][:, 0:4], in_=ps[:, tb_, 0:4], func=AF.Exp, scale=SCALE), reads=[Bps[tb_]], writes=[BpT[i]])
            eo = es_off(g, kh)
            P.op("dve", lambda e: e.tensor_tensor(out=pT[i][:, 0:4], in0=pT[i][:, 0:4], in1=Eall[:, eo:eo + 4], op=ALU.mult), reads=[BpT[i], BE], writes=[BpT[i]])
            P.op("pe", lambda e: e.matmul(ps[:, 6, 0:4], lhsT=vbb[i][:, :], rhs=pT[i][:, 0:4], start=True, stop=False), reads=[Bvb[i], BpT[i]], writes=[Bps[6]])
            P.op("pe", lambda e: e.matmul(ps[:, 7, 0:4], lhsT=ones[:, :], rhs=pT[i][:, 0:4], start=True, stop=False), reads=[Bones, BpT[i]], writes=[Bps[7]])
            kc = g * 1024 + kh * 128
            P.op("pe", lambda e: e.transpose(out=ps[:, tb_, 8:9], in_=kvt[0][0:1, kc:kc + 128], identity=ident[0:1, 0:1]), reads=[Bkvt[0], Bid], writes=[Bps[tb_]])
            P.op("act", lambda e: e.activation(out=knew[:, 0:1], in_=ps[:, tb_, 8:9], func=AF.Copy), reads=[Bps[tb_]], writes=[Bknew])
            P.op("pool", lambda e: e.tensor_copy(out=vnew[0:1, 0:128], in_=kvt[0][0:1, kc + 512:kc + 640]), reads=[Bkvt[0]], writes=[Bvnew])
            P.op("pe", lambda e: e.matmul(ps[0:1, tb_, 16:20].rearrange("p (s q) -> p s q", s=4), lhsT=knew[:, 0:1], rhs=qap, start=True, stop=True),
                 reads=[Bknew, Bq4], writes=[Bps[tb_]])
            P.op("act", lambda e: e.activation(out=misc[0:1, 0:4], in_=ps[0:1, tb_, 16:20], func=AF.Exp, scale=SCALE), reads=[Bps[tb_]], writes=[Bmisc])
            en = (g * 4 + kh) * 4
            P.op("dve", lambda e: e.tensor_tensor(out=pnew[0:1, 0:4], in0=misc[0:1, 0:4], in1=enew[0:1, en:en + 4], op=ALU.mult), reads=[Bmisc, Benew], writes=[Bpnew])
            P.op("pe", lambda e: e.matmul(ps[:, 6, 0:4], lhsT=vnew[0:1, 0:128], rhs=pnew[0:1, 0:4], start=False, stop=True), reads=[Bvnew, Bpnew], writes=[Bps[6]])
            P.op("pe", lambda e: e.matmul(ps[:, 7, 0:4], lhsT=ones[0:1, :], rhs=pnew[0:1, 0:4], start=False, stop=True), reads=[Bones, Bpnew], writes=[Bps[7]])
            for (acc, Bacc, bank) in ((num, Bnum, 6), (den, Bden, 7)):
                src2 = ps[:, bank, 0:4].rearrange("p (s q) -> p s q", s=4)
                if g == 0:
                    P.op("act", lambda e, acc=acc, src2=src2: e.activation(out=acc[:, :, 0:1], in_=src2, func=AF.Copy), reads=[Bps[bank]], writes=[Bacc])
                else:
                    P.op("dve", lambda e, acc=acc, src2=src2: e.tensor_tensor(out=acc[:, :, 0:1], in0=acc[:, :, 0:1], in1=src2, op=ALU.add),
                         reads=[Bps[bank], Bacc], writes=[Bacc])

        def b_layer(l, Tn, b, sample):
            norm_pre(V_BPRE + 32 * l, Tn)
            cols = []
            for kh in range(4):
                for g in range(3):
                    cols += [g * 2048 + (4 * kh + s) * 128 for s in range(4)]
                cols += [6144 + (4 * kh + s) * 128 for s in range(4)]

            def cons(idx, pb):
                kh, j = idx // 16, idx % 16
                if j < 12:
                    g, s = j // 4, j % 4
                    P.op("act", lambda e: e.activation(out=q4[:, s, 0:Tn], in_=ps[:, pb, 0:Tn], func=AF.Copy), reads=[Bps[pb]], writes=[Bq4])
                    if s == 3:
                        if sample:
                            attention_sample(g, kh)
                        else:
                            attention_prompt(b, g, kh)
                else:
                    s = j - 12
                    if s == 0:
                        P.op("dve", lambda e: e.reciprocal(out=den[:, :, 0:Tn], in_=den[:, :, 0:Tn]), reads=[Bden], writes=[Bden])
                        P.op("dve", lambda e: e.tensor_tensor(out=num[:, :, 0:Tn], in0=num[:, :, 0:Tn], in1=den[:, :, 0:Tn], op=ALU.mult),
                             reads=[Bnum, Bden], writes=[Bnum])
                    P.op("act", lambda e: e.activation(out=sg[:, 0:Tn], in_=ps[:, pb, 0:Tn], func=AF.Silu), reads=[Bps[pb]], writes=[Bsg])
                    P.op("dve", lambda e: e.tensor_tensor(out=hg[:, 4 * kh + s, 0:Tn], in0=num[:, s, 0:Tn], in1=sg[:, 0:Tn], op=ALU.mult),
                         reads=[Bnum, Bsg], writes=[Bh[4 * kh + s]])

            lin(b_w_in[l], 32, cols, ub, Bu, Tn, cons, ("bin", l))
            lin(b_w_out[l], 16, [m * 128 for m in range(32)], hg, Bh, Tn, out_consumer(Tn), ("bout", l))
            post(V_BPOST + 32 * l, Tn)

        def load_x(src_rows, M, col0):
            P.dma("sp", xiosem, lambda e: e.dma_start(out=xio[0:M, :], in_=src_rows), writes=[Bxio])
            for k in range(32):
                tb_ = 4 + k % 2
                P.op("pe", lambda e, k=k, tb_=tb_: e.transpose(out=ps[:, tb_, 0:M], in_=xio[0:M, k * 128:(k + 1) * 128], identity=ident[0:M, 0:M]),
                     reads=[Bxio, Bid], writes=[Bps[tb_]])
                P.op("act", lambda e, k=k, tb_=tb_: e.activation(out=xT[:, k, col0:col0 + M], in_=ps[:, tb_, 0:M], func=AF.Copy), reads=[Bps[tb_]], writes=[Bx[k]])

        def store_x(dst_rows, M, col0):
            for k in range(32):
                tb_ = 4 + k % 2
                P.op("pe", lambda e, k=k, tb_=tb_: e.transpose(out=ps[0:M, tb_, 0:128], in_=xT[:, k, col0:col0 + M], identity=ident[:, :]),
                     reads=[Bx[k], Bid], writes=[Bps[tb_]])
                P.op("act", lambda e, k=k, tb_=tb_: e.activation(out=xio[0:M, k * 128:(k + 1) * 128], in_=ps[0:M, tb_, 0:128], func=AF.Copy),
                     reads=[Bps[tb_]], writes=[Bxio])
            P.dma("sp", xiosem, lambda e: e.dma_start(out=dst_rows, in_=xio[0:M, :]), reads=[Bxio], writes=[Bout])

        def store_states(conv_o, h_o):
            for l in range(2):
                P.op("pe", lambda e, l=l: e.transpose(out=ps[:, 4, 0:128], in_=stt[l][:, :, :].rearrange("p j c -> p (j c)"), identity=ident[:, :]),
                     reads=[Bst[l], Bid], writes=[Bps[4]])
                P.op("act", lambda e: e.activation(out=stT[:, :], in_=ps[:, 4, 0:128], func=AF.Copy), reads=[Bps[4]], writes=[BstT])
                for j in range(3):
                    P.dma("sp", outsem, lambda e, l=l, j=j: e.dma_start(out=conv_o[l, j].rearrange("(c p) -> c p", p=128), in_=stT[j * 32:(j + 1) * 32, :]),
                          reads=[BstT], writes=[Bout])
                P.dma("sp", outsem, lambda e, l=l: e.dma_start(out=h_o[l].rearrange("(c p) -> c p", p=128), in_=stT[96:128, :]), reads=[BstT], writes=[Bout])

        def load_states():
            for l in range(2):
                for j in range(3):
                    P.dma("sp", setsem[4], lambda e, l=l, j=j: e.dma_start(out=stT[j * 32:(j + 1) * 32, :], in_=sconv[l, j].rearrange("(c p) -> c p", p=128)), writes=[BstT])
                P.dma("sp", setsem[4], lambda e, l=l: e.dma_start(out=stT[96:128, :], in_=sh[l].rearrange("(c p) -> c p", p=128)), writes=[BstT])
                P.op("pe", lambda e, l=l: e.transpose(out=ps[:, 4, 0:128], in_=stT[:, :], identity=ident[:, :]), reads=[BstT, Bid], writes=[Bps[4]])
                P.op("act", lambda e, l=l: e.activation(out=stt[l][:, :, :].rearrange("p j c -> p (j c)"), in_=ps[:, 4, 0:128], func=AF.Copy),
                     reads=[Bps[4]], writes=[Bst[l]])

        for b in range(NB):
            for tc in range(T // 128):
                load_x(xp[b * T + tc * 128:b * T + (tc + 1) * 128, :], 128, tc * 128)
            a_layer(0, T)
            a_layer(1, T)
            kv_phase(b * T, T, False)
            if b == NB // 2 - 1:
                for l in range(2):
                    P.op("dve", lambda e, l=l: e.tensor_scalar(out=stt[l][:, :, :].rearrange("p j c -> p (j c)"), in0=stt[l][:, :, :].rearrange("p j c -> p (j c)"),
                                                               scalar1=flg[:, 0:1], scalar2=None, op0=ALU.mult), reads=[Bst[l], Bflg], writes=[Bst[l]])
            if b >= NB // 2:
                b_layer(0, T, b, False)
                b_layer(1, T, b, False)
                bo = b - NB // 2
                for tc in range(T // 128):
                    store_x(yp[bo * T + tc * 128:bo * T + (tc + 1) * 128, :], 128, tc * 128)
        store_states(pconv, ph)
        load_states()
        load_x(xs[0:1, :], 1, 0)
        a_layer(0, 1)
        a_layer(1, 1)
        kv_phase(0, 1, True)
        for g in range(3):
            lbg = 128 * DIL[g]
            P.dma("sp", d2dsem, lambda e, g=g, lbg=lbg: e.dma_start(out=skv_o[g][0:lbg - 1, :], in_=skv[g][1:lbg, :]), writes=[Bout])
            P.dma("sp", d2dsem, lambda e, g=g, lbg=lbg: e.dma_start(out=skv_o[g][lbg - 1:lbg, :], in_=kvt[0][0:1, g * 1024:(g + 1) * 1024]),
                  reads=[Bkvt[0]], writes=[Bout])
        b_layer(0, 1, 0, True)
        b_layer(1, 1, 0, True)
        store_x(ys[0:1, :], 1, 0)
        store_states(sconv_o, sh_o)
        fin = [(sm, P.dcnt[id(sm)]) for sm in (kvosem, outsem, xiosem, d2dsem) if P.dcnt[id(sm)] > 0]
        P.q["sp"].append((None, fin, None))
        with nc.Block() as block:
            P.replay(block)
    return nc


def _rel_bucket(dist):
    dist = np.asarray(dist)
    d = np.maximum(dist, 1).astype(np.float32)
    large = 16 + (np.log(d / 16) / np.log(2048 / 16) * (32 - 16)).astype(np.int32)
    large = np.minimum(large, 31)
    return np.where(dist < 16, dist, large).astype(np.int32)


def _etiles(rel_bias):
    ein = np.zeros((128, NE), np.float32)
    em = np.zeros((128, NE), np.float32)
    enew = np.zeros((1, 48), np.float32)
    p = np.arange(128)[:, None]

    def fill(off, g, kh, diff, valid):
        nq = diff.shape[1]
        bk = _rel_bucket(np.clip(diff, 0, 128) * DIL[g])
        for s in range(4):
            col = g * 16 + 4 * kh + s
            ein[:, off + s * nq:off + (s + 1) * nq] = rel_bias[bk, col]
            em[:, off + s * nq:off + (s + 1) * nq] = valid.astype(np.float32)

    for kh in range(4):
        q = np.arange(128)[None, :]
        fill(e0_off(0, kh), 0, kh, q - p, (q - p) >= 0)
        fill(e0_off(1, kh), 0, kh, 128 + q - p, (128 + q - p) <= 128)
        q = np.arange(64)[None, :]
        for half in range(2):
            d = 64 * half + q - p
            fill(e1_off(half, 0, kh), 1, kh, d, d >= 0)
            d = 128 + 64 * half + q - p
            fill(e1_off(half, 1, kh), 1, kh, d, d <= 128)
        q = np.arange(16)[None, :]
        for b in range(NB):
            d = 16 * b + q - p
            fill(e2_off(b, kh), 2, kh, d, d >= 0)
        for g in range(3):
            d = 128 - p + np.zeros((1, 1), np.int64)
            fill(es_off(g, kh), g, kh, d, d >= 0)
            for s in range(4):
                enew[0, (g * 4 + kh) * 4 + s] = rel_bias[0, g * 16 + 4 * kh + s]
    return ein, em, enew


def _colvec(v):
    v = np.asarray(v, np.float32).reshape(-1, 32, 128)
    return np.ascontiguousarray(v.transpose(2, 0, 1).reshape(128, -1))


_NC_CACHE = {}


def kernel(x_prompt, x_sample, state_conv, state_h, state_kv_w128, state_kv_w512, state_kv_w2048,
           a_pre_g, a_w_in, a_conv_w, a_conv_b, a_w_gate_a, a_b_gate_a, a_w_gate_x, a_b_gate_x,
           a_lambda, a_w_out, a_post_g, kv_norm_g, w_kv, rel_bias, b_pre_g, b_w_in, b_w_out, b_post_g):
    f = lambda a: np.ascontiguousarray(np.asarray(a, np.float32))
    vecs = np.concatenate([
        _colvec(a_pre_g), _colvec(np.asarray(a_conv_w)), _colvec(a_conv_b),
        _colvec(np.asarray(a_b_gate_a).reshape(2, 4096)), _colvec(np.asarray(a_b_gate_x).reshape(2, 4096)),
        _colvec(a_lambda), _colvec(a_post_g), _colvec(kv_norm_g), _colvec(b_pre_g), _colvec(b_post_g)], axis=1)
    assert vecs.shape == (128, NV)
    ein, em, enew = _etiles(np.asarray(rel_bias, np.float32))
    def tile_w(w):
        w = np.asarray(w, np.float32)
        lead = w.shape[:-2]
        K, M = w.shape[-2:]
        w = w.reshape(lead + (K // 128, 128, M // 128, 128))
        nl = len(lead)
        perm = tuple(range(nl)) + (nl + 2, nl + 1, nl + 0, nl + 3)
        return np.ascontiguousarray(w.transpose(perm)).reshape(lead + (M // 128, 128, K))

    shared = dict(a_w_in=tile_w(a_w_in), a_w_out=tile_w(a_w_out), a_wga=f(a_w_gate_a), a_wgx=f(a_w_gate_x), w_kv=tile_w(w_kv),
                  b_w_in=tile_w(b_w_in), b_w_out=tile_w(b_w_out), vecs=f(vecs), ein=ein, em=em, enew=enew,
                  ident=np.eye(128, dtype=np.float32))
    xp = f(x_prompt); xs = f(x_sample); sc = f(state_conv); shh = f(state_h)
    k0 = f(state_kv_w128).reshape(8, 128, 1024); k1 = f(state_kv_w512).reshape(8, 512, 1024); k2 = f(state_kv_w2048).reshape(8, 2048, 1024)
    in_maps = []
    for c in range(8):
        m = dict(shared)
        sq_, hf = c // 2, c % 2
        xpc = xp[sq_] if hf == 1 else np.concatenate([np.zeros((1024, 4096), np.float32), xp[sq_][:1024]], axis=0)
        fl = np.zeros((128, 2), np.float32); fl[:, 0] = hf; fl[:, 1] = 1.0; fl[:64, 1] = hf
        m.update(xp=xpc, flg=fl, xs=xs[c], sconv=np.ascontiguousarray(sc[:, c]), sh=np.ascontiguousarray(shh[:, c]),
                 skv0=k0[c], skv1=k1[c], skv2=k2[c])
        in_maps.append(m)
    if "nc" not in _NC_CACHE:
        _NC_CACHE["nc"] = build()
    res = run_bass_kernel_spmd(_NC_CACHE["nc"], in_maps, core_ids=list(range(8)))
    R = res.results
    y_prompt = np.stack([np.concatenate([R[2 * q]["yp"], R[2 * q + 1]["yp"]], axis=0) for q in range(4)])
    y_sample = np.stack([R[c]["ys"] for c in range(8)])
    p_conv = np.stack([R[2 * q + 1]["pconv"] for q in range(4)], axis=1)
    p_h = np.stack([R[2 * q + 1]["ph"] for q in range(4)], axis=1)
    pkv = np.stack([R[2 * q + 1]["pkv"] for q in range(4)]).reshape(4, 2048, 3, 2, 4, 128)
    p_kv128 = np.ascontiguousarray(pkv[:, 2048 - 128:, 0])
    p_kv512 = np.ascontiguousarray(pkv[:, 2048 - 512:, 1])
    p_kv2048 = np.ascontiguousarray(pkv[:, :, 2])
    s_conv = np.stack([R[c]["sconv_o"] for c in range(8)], axis=1)
    s_h = np.stack([R[c]["sh_o"] for c in range(8)], axis=1)
    s_kv128 = np.stack([R[c]["skv_o0"] for c in range(8)]).reshape(8, 128, 2, 4, 128)
    s_kv512 = np.stack([R[c]["skv_o1"] for c in range(8)]).reshape(8, 512, 2, 4, 128)
    s_kv2048 = np.stack([R[c]["skv_o2"] for c in range(8)]).reshape(8, 2048, 2, 4, 128)
    f32 = lambda a: np.asarray(a, np.float32)
    return tuple(f32(a) for a in (y_prompt, y_sample, p_conv, p_h, p_kv128, p_kv512, p_kv2048, s_conv, s_h, s_kv128, s_kv512, s_kv2048))
```

```python
import contextlib
import numpy as np
import concourse.bass as bass
import concourse.mybir as mybir
from concourse.bass_utils import run_bass_kernel_spmd

F32 = mybir.dt.float32
BF16 = mybir.dt.bfloat16
AF = mybir.ActivationFunctionType
ALU = mybir.AluOpType

SAME_ENGINE_SYNC = True
T = 256
NB = 2048 // T
EPS = 1e-6
SCALE = 128 ** -0.5
DIL = (1, 4, 16)


class Buf:
    __slots__ = ("name", "w", "r")

    def __init__(self, name=""):
        self.name = name
        self.w = None
        self.r = {}


class Prog:
    ENGS = ("pe", "act", "dve", "pool", "sp")

    def __init__(self, nc, stack):
        self.nc = nc
        self.stack = stack
        self.q = {e: [] for e in self.ENGS}
        self.cnt = {e: 0 for e in self.ENGS}
        self.seen = {e: {} for e in self.ENGS}
        self.esem = {e: stack.enter_context(nc.semaphore("es_" + e)) for e in self.ENGS}
        self.dcnt = {}

    def dma_sem(self, name):
        s = self.stack.enter_context(self.nc.semaphore(name))
        self.dcnt[id(s)] = 0
        return s

    def sb(self, name, shape, dt):
        return self.stack.enter_context(self.nc.sbuf_tensor("sb_" + name, list(shape), dt))

    def ps(self, name, shape, dt=F32):
        return self.stack.enter_context(self.nc.psum_tensor("psum_" + name, list(shape), dt))

    def _deps(self, eng, reads, writes):
        deps = []
        for b in reads:
            if b.w is not None:
                deps.append(b.w)
        for b in writes:
            if b.w is not None:
                deps.append(b.w)
            deps.extend(b.r.values())
        waits = []
        seen = self.seen[eng]
        for (sem, val, src) in deps:
            if src == eng and (eng == "pe" or not SAME_ENGINE_SYNC):
                continue
            k = id(sem)
            if seen.get(k, 0) >= val:
                continue
            seen[k] = val
            waits.append((sem, val))
        return waits

    def _commit(self, tok, reads, writes):
        for b in writes:
            b.w = tok
            b.r = {}
        for b in reads:
            if b not in writes:
                b.r[id(tok[0])] = tok

    def op(self, eng, fn, reads=(), writes=(), inc=True):
        waits = self._deps(eng, reads, writes)
        if inc:
            self.cnt[eng] += 1
            tok = (self.esem[eng], self.cnt[eng], eng)
        else:
            tok = (self.esem[eng], self.cnt[eng] + 1, eng)
        self.q[eng].append((fn, waits, (self.esem[eng], 1) if inc else None))
        self._commit(tok, reads, writes)
        return tok

    def dma(self, eng, sem, fn, reads=(), writes=()):
        waits = self._deps(eng, reads, writes)
        self.dcnt[id(sem)] += 16
        tok = (sem, self.dcnt[id(sem)], "dma")
        self.q[eng].append((fn, waits, (sem, 16)))
        self._commit(tok, reads, writes)
        return tok

    def wait_all(self, eng, bufs):
        waits = self._deps(eng, (), bufs)
        self.q[eng].append((None, waits, None))

    def replay(self, block):
        names = {"pe": "tensor", "act": "scalar", "dve": "vector", "pool": "gpsimd", "sp": "sync"}

        def run(e, engobj):
            for fn, waits, inc in self.q[e]:
                for sem, val in waits:
                    engobj.wait_ge(sem, val)
                if fn is None:
                    continue
                ins = fn(engobj)
                if inc is not None:
                    ins.then_inc(inc[0], inc[1])

        for e in self.ENGS:
            if not self.q[e]:
                continue

            def mk(e):
                def body(engobj):
                    run(e, engobj)
                return body
            getattr(block, names[e])(mk(e))


V_APRE, V_CW, V_CB, V_BGA, V_BGX, V_LAM, V_APOST, V_KVG, V_BPRE, V_BPOST = 0, 64, 320, 384, 448, 512, 576, 640, 672, 736
NV = 800
E0 = 0
E1 = 4096
E2 = 8192
ES = 10240
NE = 10240 + 48


def e0_off(pc, kh): return E0 + (pc * 4 + kh) * 512
def e1_off(half, pc, kh): return E1 + ((half * 2 + pc) * 4 + kh) * 256
def e2_off(b, kh): return E2 + (b * 4 + kh) * 64
def es_off(g, kh): return ES + (g * 4 + kh) * 4


def build():
    nc = bass.Bass("TRN2", target_bir_lowering=False)

    def D(name, shape, dt=F32, kind="ExternalInput"):
        return nc.dram_tensor(name, list(shape), dt, kind=kind).ap()

    xp = D("xp", [2048, 4096]); xs = D("xs", [1, 4096])
    sconv = D("sconv", [2, 3, 4096]); sh = D("sh", [2, 4096])
    skv = [D("skv0", [128, 1024]), D("skv1", [512, 1024]), D("skv2", [2048, 1024])]
    a_w_in = D("a_w_in", [2, 64, 128, 4096]); a_w_out = D("a_w_out", [2, 32, 128, 4096])
    a_wga = D("a_wga", [2, 16, 256, 256]); a_wgx = D("a_wgx", [2, 16, 256, 256])
    w_kv = D("w_kv", [24, 128, 4096])
    b_w_in = D("b_w_in", [2, 64, 128, 4096]); b_w_out = D("b_w_out", [2, 32, 128, 2048])
    vecs_d = D("vecs", [128, NV]); ein_d = D("ein", [128, NE]); em_d = D("em", [128, NE])
    enew_d = D("enew", [1, 48]); ident_d = D("ident", [128, 128]); flg_d = D("flg", [128, 2])
    O = lambda n, s: D(n, s, kind="ExternalOutput")
    yp = O("yp", [1024, 4096]); ys = O("ys", [1, 4096])
    pconv = O("pconv", [2, 3, 4096]); ph = O("ph", [2, 4096]); pkv = O("pkv", [2048, 3072])
    sconv_o = O("sconv_o", [2, 3, 4096]); sh_o = O("sh_o", [2, 4096])
    skv_o = [O("skv_o0", [128, 1024]), O("skv_o1", [512, 1024]), O("skv_o2", [2048, 1024])]
    kvs = nc.dram_tensor("kvs", [2048 + 16, 3072], F32).ap()
    wsc = {}
    for l in range(2):
        wsc[("ain", l)] = nc.dram_tensor("wsc_ain%d" % l, [64, 128, 4096], BF16).ap()
        wsc[("aout", l)] = nc.dram_tensor("wsc_aout%d" % l, [32, 128, 4096], BF16).ap()
        wsc[("bin", l)] = nc.dram_tensor("wsc_bin%d" % l, [64, 128, 4096], BF16).ap()
        wsc[("bout", l)] = nc.dram_tensor("wsc_bout%d" % l, [32, 128, 2048], BF16).ap()
    wsc[("kv", 0)] = nc.dram_tensor("wsc_kv", [24, 128, 4096], BF16).ap()
    wdone = {}

    with contextlib.ExitStack() as st:
        P = Prog(nc, st)
        xT = P.sb("xT", [128, 32, T], F32); Bx = [Buf() for _ in range(32)]
        ub = P.sb("ub", [128, 32, T], BF16); Bu = [Buf() for _ in range(32)]
        hg = P.sb("hg", [128, 32, T], BF16); Bh = [Buf() for _ in range(32)]
        NW = 3
        wbf = [P.sb("wbf%d" % i, [128, 32, 128], BF16) for i in range(NW)]; Bw = [Buf() for _ in range(NW)]
        wsem = [P.dma_sem("wsem%d" % i) for i in range(NW)]
        ssem = [P.dma_sem("ssem%d" % i) for i in range(NW)]
        wg = P.sb("wg", [128, 2, 2, 256], BF16); Bwg = Buf(); wgsem = P.dma_sem("wgsem")
        vecs = P.sb("vecs", [128, NV], F32); Bv = Buf()
        c1 = P.sb("c1", [128, 64], F32); Bc1 = Buf()
        Eall = P.sb("Eall", [128, NE], BF16); BE = Buf()
        enew = P.sb("enew", [1, 48], F32); Benew = Buf()
        ident = P.sb("ident", [128, 128], F32); Bid = Buf()
        flg = P.sb("flg", [128, 2], F32); Bflg = Buf(); flgsem = P.dma_sem("flgsem")
        ones = P.sb("ones", [128, 128], BF16); Bones = Buf()
        rs = P.sb("rs", [128, T], F32); Brs = Buf()
        sqb = [P.sb("sqb%d" % i, [128, T], BF16) for i in range(2)]; Bsq = [Buf(), Buf()]
        stt = [P.sb("st%d" % l, [128, 4, 32], F32) for l in range(2)]; Bst = [Buf(), Buf()]
        stT = P.sb("stT", [128, 128], F32); BstT = Buf()
        xc = [P.sb("xc%d" % i, [128, 3 + T], F32) for i in range(2)]; Bxc = [Buf(), Buf()]
        xv = [P.sb("xv%d" % i, [128, T], F32) for i in range(2)]; Bxv = [Buf(), Buf()]
        xvb = [P.sb("xvb%d" % i, [128, T], BF16) for i in range(2)]; Bxvb = [Buf(), Buf()]
        rg = [[P.sb("rg%d%d" % (a, j), [128, T], F32) for j in range(2)] for a in range(2)]
        Brg = [[Buf(), Buf()], [Buf(), Buf()]]
        ta = P.sb("ta", [128, T], F32); Bta = Buf()
        tb = P.sb("tb", [128, T], F32); Btb = Buf()
        tcc = P.sb("tcc", [128, T], F32); Btc = Buf()
        hh = [P.sb("hh%d" % i, [128, T], F32) for i in range(2)]; Bhh = [Buf(), Buf()]
        sg = P.sb("sg", [128, T], F32); Bsg = Buf()
        sg2 = [P.sb("sg2_%d" % i, [128, T], F32) for i in range(2)]; Bsg2 = [Buf(), Buf()]
        tmpx = P.sb("tmpx", [128, T], F32); Btmpx = Buf()
        xio = P.sb("xio", [128, 4096], F32); Bxio = Buf(); xiosem = P.dma_sem("xiosem")
        kvt = [P.sb("kvt%d" % i, [128, 3072], F32) for i in range(2)]; Bkvt = [Buf(), Buf()]
        kvosem = P.dma_sem("kvosem")
        q4 = P.sb("q4", [128, 4, T], BF16); Bq4 = Buf()
        num = P.sb("num", [128, 4, T], F32); Bnum = Buf()
        den = P.sb("den", [128, 4, T], F32); Bden = Buf()
        kvin = [P.sb("kvin%d" % i, [128, 2, 128], F32) for i in range(2)]; Bkvin = [Buf(), Buf()]
        kvinsem = [P.dma_sem("kvinsem%d" % i) for i in range(2)]
        kTb = [P.sb("kTb%d" % i, [128, 128], BF16) for i in range(2)]; BkT = [Buf(), Buf()]
        vbb = [P.sb("vbb%d" % i, [128, 128], BF16) for i in range(2)]; Bvb = [Buf(), Buf()]
        pT = [P.sb("pT%d" % i, [128, 512], BF16) for i in range(2)]; BpT = [Buf(), Buf()]
        esb = P.sb("esb", [128, 1024], F32); Besb = Buf()
        emb = P.sb("emb", [128, 1024], F32); Bemb = Buf()
        misc = P.sb("misc", [128, 64], F32); Bmisc = Buf()
        knew = P.sb("knew", [128, 16], BF16); Bknew = Buf()
        vnew = P.sb("vnew", [1, 128], BF16); Bvnew = Buf()
        pnew = P.sb("pnew", [1, 16], BF16); Bpnew = Buf()
        ps = P.ps("ps", [128, 8, 512], F32); Bps = [Buf() for _ in range(8)]
        setsem = [P.dma_sem("setsem%d" % i) for i in range(6)]
        outsem = P.dma_sem("outsem"); Bout = Buf()
        d2dsem = P.dma_sem("d2dsem"); Bd2d = Buf()

        P.dma("sp", setsem[0], lambda e: e.dma_start(out=vecs[:, :], in_=vecs_d), writes=[Bv])
        P.dma("sp", setsem[1], lambda e: e.dma_start(out=enew[:, :], in_=enew_d), writes=[Benew])
        P.dma("sp", flgsem, lambda e: e.dma_start(out=flg[:, :], in_=flg_d), writes=[Bflg])
        P.dma("sp", setsem[5], lambda e: e.dma_start(out=ident[:, :], in_=ident_d), writes=[Bid])
        P.op("pool", lambda e: e.memset(ones[:, :], 1.0), writes=[Bones])
        for i in range(2):
            P.op("pool", lambda e, i=i: e.memset(kvin[i][:, :, :], 0.0), writes=[Bkvin[i]])
            P.op("pool", lambda e, i=i: e.memset(kvt[i][:, :], 0.0), writes=[Bkvt[i]])
        for l in range(2):
            P.op("pool", lambda e, l=l: e.memset(stt[l][:, :, :], 0.0), writes=[Bst[l]])
        P.op("act", lambda e: e.activation(out=c1[:, :], in_=vecs[:, V_LAM:V_LAM + 64], func=AF.Exp, scale=-1.0), reads=[Bv], writes=[Bc1])
        P.op("act", lambda e: e.activation(out=c1[:, :], in_=c1[:, :], func=AF.Ln, bias=1.0, scale=1.0), reads=[Bc1], writes=[Bc1])
        P.op("dve", lambda e: e.tensor_scalar(out=c1[:, :], in0=c1[:, :], scalar1=-8.0, scalar2=None, op0=ALU.mult), reads=[Bc1], writes=[Bc1])
        for c0 in range(0, NE, 1024):
            cw = min(1024, NE - c0)
            P.dma("sp", setsem[2], lambda e, c0=c0, cw=cw: e.dma_start(out=esb[:, 0:cw], in_=ein_d[:, c0:c0 + cw]), writes=[Besb])
            P.dma("sp", setsem[3], lambda e, c0=c0, cw=cw: e.dma_start(out=emb[:, 0:cw], in_=em_d[:, c0:c0 + cw]), writes=[Bemb])
            P.op("act", lambda e, cw=cw: e.activation(out=esb[:, 0:cw], in_=esb[:, 0:cw], func=AF.Exp), reads=[Besb], writes=[Besb])
            P.op("dve", lambda e, c0=c0, cw=cw: e.tensor_tensor(out=Eall[:, c0:c0 + cw], in0=esb[:, 0:cw], in1=emb[:, 0:cw], op=ALU.mult),
                 reads=[Besb, Bemb], writes=[BE])
        P.op("act", lambda e: e.activation(out=enew[:, :], in_=enew[:, :], func=AF.Exp), reads=[Benew], writes=[Benew])

        wctr = [0]
        pctr = [0]

        def wload(Wsl, nk, col0, key):
            s_ = wctr[0] % NW; wctr[0] += 1
            strip = col0 // 128
            dst = wbf[s_][:, 0:nk, :].rearrange("p k m -> p (k m)")
            kk = (key, strip)
            if kk not in wdone:
                P.dma("pool", wsem[s_], lambda e: e.dma_start(out=dst, in_=Wsl[strip], max_dma_last_dim=8192), writes=[Bw[s_]])
                bb = Buf(); wdone[kk] = bb
                P.dma("sp", ssem[s_], lambda e: e.dma_start(out=wsc[key][strip], in_=dst), reads=[Bw[s_]], writes=[bb])
            else:
                P.dma("pool", wsem[s_], lambda e: e.dma_start(out=dst, in_=wsc[key][strip]), reads=[wdone[kk]], writes=[Bw[s_]])
            return s_

        def lin(Wsl, nk, cols, src, Bsrc, Tn, consumer, key):
            slots = {}
            for i0 in range(min(2, len(cols))):
                slots[i0] = wload(Wsl, nk, cols[i0], key)
            for idx, col0 in enumerate(cols):
                s = slots.pop(idx)
                pb = pctr[0] % 3; pctr[0] += 1
                for k in range(nk):
                    P.op("pe", lambda e, s=s, k=k, pb=pb: e.matmul(ps[:, pb, 0:Tn], lhsT=wbf[s][:, k, :], rhs=src[:, k, 0:Tn],
                                                                  start=(k == 0), stop=(k == nk - 1)),
                         reads=[Bw[s], Bsrc[k]], writes=[Bps[pb]], inc=(k == nk - 1))
                if idx + 2 < len(cols):
                    slots[idx + 2] = wload(Wsl, nk, cols[idx + 2], key)
                consumer(idx, pb)

        def rstd_from_ps3(Tn):
            P.op("act", lambda e: e.activation(out=rs[:, 0:Tn], in_=ps[:, 3, 0:Tn], func=AF.Sqrt, bias=EPS, scale=1.0 / 4096), reads=[Bps[3]], writes=[Brs])
            P.op("dve", lambda e: e.reciprocal(out=rs[:, 0:Tn], in_=rs[:, 0:Tn]), reads=[Brs], writes=[Brs])

        def norm_pre(gcol, Tn, ssq=True, rstd=True):
            for k in (range(32) if ssq else ()):
                i = k % 2
                P.op("dve", lambda e, k=k, i=i: e.tensor_tensor(out=sqb[i][:, 0:Tn], in0=xT[:, k, 0:Tn], in1=xT[:, k, 0:Tn], op=ALU.mult),
                     reads=[Bx[k]], writes=[Bsq[i]])
                P.op("pe", lambda e, k=k, i=i: e.matmul(ps[:, 3, 0:Tn], lhsT=ones[:, :], rhs=sqb[i][:, 0:Tn], start=(k == 0), stop=(k == 31)),
                     reads=[Bsq[i], Bones], writes=[Bps[3]], inc=True)
            if rstd:
                rstd_from_ps3(Tn)
            for k in range(32):
                P.op("dve", lambda e, k=k: e.scalar_tensor_tensor(out=ub[:, k, 0:Tn], in0=xT[:, k, 0:Tn], scalar=vecs[:, gcol + k:gcol + k + 1],
                                                                  in1=rs[:, 0:Tn], op0=ALU.mult, op1=ALU.mult),
                     reads=[Bx[k], Brs, Bv], writes=[Bu[k]])

        def out_consumer(Tn):
            def cons(m, pb):
                i = m % 2
                P.op("act", lambda e, m=m, pb=pb: e.activation(out=ub[:, m, 0:Tn], in_=ps[:, pb, 0:Tn], func=AF.Copy), reads=[Bps[pb]], writes=[Bu[m]])
                P.op("act", lambda e, i=i, pb=pb: e.activation(out=sqb[i][:, 0:Tn], in_=ps[:, pb, 0:Tn], func=AF.Square), reads=[Bps[pb]], writes=[Bsq[i]])
                P.op("pe", lambda e, m=m, i=i: e.matmul(ps[:, 3, 0:Tn], lhsT=ones[:, :], rhs=sqb[i][:, 0:Tn], start=(m == 0), stop=(m == 31)),
                     reads=[Bsq[i], Bones], writes=[Bps[3]], inc=True)
            return cons

        def post(gcol, Tn):
            rstd_from_ps3(Tn)
            for k in range(32):
                P.op("dve", lambda e, k=k: e.scalar_tensor_tensor(out=tmpx[:, 0:Tn], in0=ub[:, k, 0:Tn], scalar=vecs[:, gcol + k:gcol + k + 1],
                                                                  in1=rs[:, 0:Tn], op0=ALU.mult, op1=ALU.mult),
                     reads=[Bu[k], Brs, Bv], writes=[Btmpx])
                P.op("dve", lambda e, k=k: e.tensor_tensor(out=xT[:, k, 0:Tn], in0=xT[:, k, 0:Tn], in1=tmpx[:, 0:Tn], op=ALU.add),
                     reads=[Btmpx, Bx[k]], writes=[Bx[k]])
                i = k % 2
                P.op("act", lambda e, k=k, i=i: e.activation(out=sqb[i][:, 0:Tn], in_=xT[:, k, 0:Tn], func=AF.Square), reads=[Bx[k]], writes=[Bsq[i]])
                P.op("pe", lambda e, k=k, i=i: e.matmul(ps[:, 3, 0:Tn], lhsT=ones[:, :], rhs=sqb[i][:, 0:Tn], start=(k == 0), stop=(k == 31)),
                     reads=[Bsq[i], Bones], writes=[Bps[3]], inc=True)

        def a_layer(l, Tn):
            norm_pre(V_APRE + 32 * l, Tn, ssq=(l == 0))
            S = stt[l]; BS = Bst[l]
            cols = []
            for h in range(16):
                cols += [(2 * h) * 128, (2 * h + 1) * 128, 4096 + (2 * h) * 128, 4096 + (2 * h + 1) * 128]

            def cons(idx, pb):
                h, j = idx // 4, idx % 4
                if j < 2:
                    c = 2 * h + j
                    P.op("dve", lambda e: e.tensor_copy(out=xc[j][:, 0:3], in_=S[:, 0:3, c]), reads=[BS], writes=[Bxc[j]])
                    P.op("act", lambda e: e.activation(out=xc[j][:, 3:3 + Tn], in_=ps[:, pb, 0:Tn], func=AF.Copy), reads=[Bps[pb]], writes=[Bxc[j]])
                    P.op("dve", lambda e: e.tensor_copy(out=S[:, 0:3, c], in_=xc[j][:, Tn:Tn + 3]), reads=[Bxc[j]], writes=[BS])
                    cwc = V_CW + l * 128
                    P.op("dve", lambda e: e.tensor_scalar(out=xv[j][:, 0:Tn], in0=xc[j][:, 3:3 + Tn], scalar1=vecs[:, cwc + 96 + c:cwc + 97 + c],
                                                          scalar2=vecs[:, V_CB + 32 * l + c:V_CB + 32 * l + c + 1], op0=ALU.mult, op1=ALU.add),
                         reads=[Bxc[j], Bv], writes=[Bxv[j]])
                    for kk in range(3):
                        P.op("dve", lambda e, kk=kk: e.scalar_tensor_tensor(out=xv[j][:, 0:Tn], in0=xc[j][:, kk:kk + Tn],
                                                                            scalar=vecs[:, cwc + 32 * kk + c:cwc + 32 * kk + c + 1],
                                                                            in1=xv[j][:, 0:Tn], op0=ALU.mult, op1=ALU.add),
                             reads=[Bxc[j], Bv, Bxv[j]], writes=[Bxv[j]])
                    P.op("act", lambda e: e.activation(out=xvb[j][:, 0:Tn], in_=xv[j][:, 0:Tn], func=AF.Copy), reads=[Bxv[j]], writes=[Bxvb[j]])
                else:
                    jo = j - 2
                    P.op("act", lambda e: e.activation(out=sg2[jo][:, 0:Tn], in_=ps[:, pb, 0:Tn], func=AF.Silu), reads=[Bps[pb]], writes=[Bsg2[jo]])
                    if jo == 1:
                        P.dma("pool", wgsem, lambda e: e.dma_start(out=wg[:, 0, :, :], in_=a_wga[l, h].rearrange("(ic p) o -> p ic o", p=128)), writes=[Bwg])
                        P.dma("pool", wgsem, lambda e: e.dma_start(out=wg[:, 1, :, :], in_=a_wgx[l, h].rearrange("(ic p) o -> p ic o", p=128)), writes=[Bwg])
                        for a in range(2):
                            bcol = (V_BGA if a == 0 else V_BGX) + 32 * l
                            for jo in range(2):
                                gb = 4 + (a * 2 + jo) % 2
                                for ic in range(2):
                                    P.op("pe", lambda e, a=a, jo=jo, ic=ic, gb=gb: e.matmul(ps[:, gb, 0:Tn], lhsT=wg[:, a, ic, jo * 128:(jo + 1) * 128],
                                                                                            rhs=xvb[ic][:, 0:Tn], start=(ic == 0), stop=(ic == 1)),
                                         reads=[Bwg, Bxvb[ic]], writes=[Bps[gb]], inc=(ic == 1))
                                cc = 2 * h + jo
                                P.op("act", lambda e, a=a, jo=jo, gb=gb, cc=cc, bcol=bcol: e.activation(
                                    out=rg[a][jo][:, 0:Tn], in_=ps[:, gb, 0:Tn], func=AF.Sigmoid, bias=vecs[:, bcol + cc:bcol + cc + 1], scale=1.0),
                                    reads=[Bps[gb], Bv], writes=[Brg[a][jo]])
                        for jo in range(2):
                            cc = 2 * h + jo
                            P.op("act", lambda e, jo=jo, cc=cc: e.activation(out=ta[:, 0:Tn], in_=rg[0][jo][:, 0:Tn], func=AF.Exp,
                                                                             scale=c1[:, 32 * l + cc:32 * l + cc + 1]),
                                 reads=[Brg[0][jo], Bc1], writes=[Bta])
                            P.op("dve", lambda e: e.tensor_tensor(out=tb[:, 0:Tn], in0=ta[:, 0:Tn], in1=ta[:, 0:Tn], op=ALU.mult), reads=[Bta], writes=[Btb])
                            P.op("act", lambda e: e.activation(out=tb[:, 0:Tn], in_=tb[:, 0:Tn], func=AF.Sqrt, bias=1.0, scale=-1.0), reads=[Btb], writes=[Btb])
                            P.op("dve", lambda e, jo=jo: e.tensor_tensor(out=tcc[:, 0:Tn], in0=rg[1][jo][:, 0:Tn], in1=xv[jo][:, 0:Tn], op=ALU.mult),
                                 reads=[Brg[1][jo], Bxv[jo]], writes=[Btc])
                            P.op("dve", lambda e: e.tensor_tensor(out=tcc[:, 0:Tn], in0=tcc[:, 0:Tn], in1=tb[:, 0:Tn], op=ALU.mult), reads=[Btc, Btb], writes=[Btc])
                            P.op("dve", lambda e, jo=jo, cc=cc: e.tensor_tensor_scan(out=hh[jo][:, 0:Tn], data0=ta[:, 0:Tn], data1=tcc[:, 0:Tn],
                                                                                     initial=S[:, 3, cc:cc + 1], op0=ALU.mult, op1=ALU.add),
                                 reads=[Bta, Btc, BS], writes=[Bhh[jo]])
                            P.op("dve", lambda e, jo=jo, cc=cc: e.tensor_copy(out=S[:, 3, cc:cc + 1], in_=hh[jo][:, Tn - 1:Tn]), reads=[Bhh[jo]], writes=[BS])
                        for j2 in range(2):
                            c2 = 2 * h + j2
                            P.op("dve", lambda e, j2=j2, c2=c2: e.tensor_tensor(out=hg[:, c2, 0:Tn], in0=hh[j2][:, 0:Tn], in1=sg2[j2][:, 0:Tn], op=ALU.mult),
                                 reads=[Bhh[j2], Bsg2[j2]], writes=[Bh[c2]])
                    jo = j - 2

            lin(a_w_in[l], 32, cols, ub, Bu, Tn, cons, ("ain", l))
            lin(a_w_out[l], 32, [m * 128 for m in range(32)], hg, Bh, Tn, out_consumer(Tn), ("aout", l))
            post(V_APOST + 32 * l, Tn)

        def kv_phase(row0, Tn, sample):
            norm_pre(V_KVG, Tn, ssq=False)
            ntc = max(1, Tn // 128)
            M = min(128, Tn)
            slots = {0: wload(w_kv, 32, 0, ("kv", 0)), 1: wload(w_kv, 32, 128, ("kv", 0))}
            for n in range(24):
                s = slots.pop(n)
                if n + 2 < 24 and n >= 1:
                    pass
                for tc in range(ntc):
                    pb = pctr[0] % 3; pctr[0] += 1
                    for k in range(32):
                        P.op("pe", lambda e, s=s, k=k, pb=pb, tc=tc: e.matmul(ps[0:M, pb, 0:128], lhsT=ub[:, k, tc * 128:tc * 128 + M], rhs=wbf[s][:, k, :],
                                                                              start=(k == 0), stop=(k == 31)),
                             reads=[Bw[s], Bu[k]], writes=[Bps[pb]], inc=(k == 31))
                    P.op("act", lambda e, pb=pb, tc=tc, n=n: e.activation(out=kvt[tc][0:M, n * 128:(n + 1) * 128], in_=ps[0:M, pb, 0:128], func=AF.Copy),
                         reads=[Bps[pb]], writes=[Bkvt[tc]])
                if n + 2 < 24:
                    slots[n + 2] = wload(w_kv, 32, (n + 2) * 128, ("kv", 0))
            if not sample:
                for tc in range(ntc):
                    r0 = row0 + tc * 128
                    P.dma("sp", kvosem, lambda e, tc=tc, r0=r0: e.dma_start(out=pkv[r0:r0 + 128, :], in_=kvt[tc][:, :]), reads=[Bkvt[tc]], writes=[Bout])
                    P.dma("sp", kvosem, lambda e, tc=tc, r0=r0: e.dma_start(out=kvs[r0:r0 + 128, :], in_=kvt[tc][:, :]), reads=[Bkvt[tc]], writes=[Bd2d])

        uctr = [0]

        def key_tile(src_rows_ap, nrows, g, kh):
            i = uctr[0] % 2; uctr[0] += 1
            P.dma("sp", kvinsem[i], lambda e: e.dma_start(out=kvin[i][0:nrows, :, :], in_=src_rows_ap), reads=[Bd2d], writes=[Bkvin[i]])
            tb_ = 4 + i
            P.op("pe", lambda e: e.transpose(out=ps[:, tb_, 0:128], in_=kvin[i][:, 0, :], identity=ident[:, :]), reads=[Bkvin[i], Bid], writes=[Bps[tb_]])
            P.op("act", lambda e: e.activation(out=kTb[i][:, :], in_=ps[:, tb_, 0:128], func=AF.Copy), reads=[Bps[tb_]], writes=[BkT[i]])
            P.op("dve", lambda e: e.tensor_copy(out=vbb[i][:, :], in_=kvin[i][:, 1, :]), reads=[Bkvin[i]], writes=[Bvb[i]])
            return i

        def attn_unit(qap, nq, tiles, acc_first, accap_fn):
            W = 4 * nq
            for ti, (i, eoff, fcol) in enumerate(tiles):
                sb_ = 4 + i
                P.op("pe", lambda e, i=i, sb_=sb_: e.matmul(ps[:, sb_, 0:W].rearrange("p (s q) -> p s q", s=4), lhsT=kTb[i][:, :], rhs=qap, start=True, stop=True),
                     reads=[BkT[i], Bq4], writes=[Bps[sb_]])
                P.op("act", lambda e, i=i, sb_=sb_: e.activation(out=pT[i][:, 0:W], in_=ps[:, sb_, 0:W], func=AF.Exp, scale=SCALE), reads=[Bps[sb_]], writes=[BpT[i]])
                if fcol is None:
                    P.op("dve", lambda e, i=i, eoff=eoff: e.tensor_tensor(out=pT[i][:, 0:W], in0=pT[i][:, 0:W], in1=Eall[:, eoff:eoff + W], op=ALU.mult),
                         reads=[BpT[i], BE], writes=[BpT[i]])
                else:
                    P.op("dve", lambda e, i=i, eoff=eoff, fcol=fcol: e.scalar_tensor_tensor(
                        out=pT[i][:, 0:W], in0=pT[i][:, 0:W], scalar=flg[:, fcol:fcol + 1], in1=Eall[:, eoff:eoff + W], op0=ALU.mult, op1=ALU.mult),
                         reads=[BpT[i], BE, Bflg], writes=[BpT[i]])
                first, last = ti == 0, ti == len(tiles) - 1
                P.op("pe", lambda e, i=i, first=first, last=last: e.matmul(ps[:, 6, 0:W], lhsT=vbb[i][:, :], rhs=pT[i][:, 0:W], start=first, stop=last),
                     reads=[Bvb[i], BpT[i]], writes=[Bps[6]])
                P.op("pe", lambda e, i=i, first=first, last=last: e.matmul(ps[:, 7, 0:W], lhsT=ones[:, :], rhs=pT[i][:, 0:W], start=first, stop=last),
                     reads=[Bones, BpT[i]], writes=[Bps[7]])
            for (acc, Bacc, bank) in ((num, Bnum, 6), (den, Bden, 7)):
                src = ps[:, bank, 0:W].rearrange("p (s q) -> p s q", s=4)
                if acc_first:
                    P.op("act", lambda e, acc=acc, src=src: e.activation(out=accap_fn(acc), in_=src, func=AF.Copy), reads=[Bps[bank]], writes=[Bacc])
                else:
                    P.op("dve", lambda e, acc=acc, src=src: e.tensor_tensor(out=accap_fn(acc), in0=accap_fn(acc), in1=src, op=ALU.add),
                         reads=[Bps[bank], Bacc], writes=[Bacc])

        def attention_prompt(b, g, kh):
            t0 = b * T
            kc = g * 1024 + kh * 128

            def rows(start, step, count):
                v = kvs[start:start + step * count, :].rearrange("(j c) n -> j c n", c=step)[:, 0, :]
                return v[:, g * 1024:(g + 1) * 1024].rearrange("j (v k x) -> j v k x", v=2, k=4)[:, :, kh, :]

            if g == 0:
                for qb in range(T // 128):
                    tiles = []
                    if t0 + qb * 128 - 128 >= 0:
                        i = key_tile(rows(t0 + qb * 128 - 128, 1, 128), 128, g, kh)
                        tiles.append((i, e0_off(1, kh), 0 if (t0 + qb * 128 - 128) < 1024 else None))
                    i = key_tile(rows(t0 + qb * 128, 1, 128), 128, g, kh)
                    tiles.append((i, e0_off(0, kh), None))
                    qap = q4[:, :, qb * 128:(qb + 1) * 128]
                    attn_unit(qap, 128, tiles, True, lambda acc, qb=qb: acc[:, :, qb * 128:(qb + 1) * 128])
            elif g == 1:
                nbi, half = b // 2, b % 2
                for c in range(4):
                    tiles = []
                    if nbi > 0:
                        i = key_tile(rows(512 * (nbi - 1) + c, 4, 128), 128, g, kh)
                        tiles.append((i, e1_off(half, 1, kh), 0 if 512 * (nbi - 1) < 1024 else None))
                    cnt = 64 * (half + 1)
                    i = key_tile(rows(512 * nbi + c, 4, cnt), cnt, g, kh)
                    tiles.append((i, e1_off(half, 0, kh), None))
                    qap = q4[:, :, :].rearrange("p s (j c) -> p s c j", c=4)[:, :, c, :]
                    attn_unit(qap, 64, tiles, False, lambda acc, c=c: acc[:, :, :].rearrange("p s (j c) -> p s c j", c=4)[:, :, c, :])
            else:
                for c in range(16):
                    cnt = 16 * (b + 1)
                    i = key_tile(rows(c, 16, cnt), cnt, g, kh)
                    qap = q4[:, :, :].rearrange("p s (j c) -> p s c j", c=16)[:, :, c, :]
                    attn_unit(qap, 16, [(i, e2_off(b, kh), 1)], False, lambda acc, c=c: acc[:, :, :].rearrange("p s (j c) -> p s c j", c=16)[:, :, c, :])

        def attention_sample(g, kh):
            r = DIL[g]
            lb = 128 * r
            src = skv[g].rearrange("(j c) (v x) -> j c v x", c=r, v=2)[:, 0, :, kh * 128:(kh + 1) * 128]
            i = uctr[0] % 2; uctr[0] += 1
            P.dma("sp", kvinsem[i], lambda e: e.dma_start(out=kvin[i][:, :, :], in_=src), writes=[Bkvin[i]])
            tb_ = 4 + i
            P.op("pe", lambda e: e.transpose(out=ps[:, tb_, 0:128], in_=kvin[i][:, 0, :], identity=ident[:, :]), reads=[Bkvin[i], Bid], writes=[Bps[tb_]])
            P.op("act", lambda e: e.activation(out=kTb[i][:, :], in_=ps[:, tb_, 0:128], func=AF.Copy), reads=[Bps[tb_]], writes=[BkT[i]])
            P.op("dve", lambda e: e.tensor_copy(out=vbb[i][:, :], in_=kvin[i][:, 1, :]), reads=[Bkvin[i]], writes=[Bvb[i]])
            qap = q4[:, :, 0:1]
            P.op("pe", lambda e: e.matmul(ps[:, tb_, 0:4].rearrange("p (s q) -> p s q", s=4), lhsT=kTb[i][:, :], rhs=qap, start=True, stop=True),
                 reads=[BkT[i], Bq4], writes=[Bps[tb_]])
            P.op("act", lambda e: e.activation(out=pT[i][:, 0:4], in_=ps[:, tb_, 0:4], func=AF.Exp, scale=SCALE), reads=[Bps[tb_]], writes=[BpT[i]])
            eo = es_off(g, kh)
            P.op("dve", lambda e: e.tensor_tensor(out=pT[i][:, 0:4], in0=pT[i][:, 0:4], in1=Eall[:, eo:eo + 4], op=ALU.mult), reads=[BpT[i], BE], writes=[BpT[i]])
            P.op("pe", lambda e: e.matmul(ps[:, 6, 0:4], lhsT=vbb[i][:, :], rhs=pT[i][:, 0:4], start=True, stop=False), reads=[Bvb[i], BpT[i]], writes=[Bps[6]])
            P.op("pe", lambda e: e.matmul(ps[:, 7, 0:4], lhsT=ones[:, :], rhs=pT[i][:, 0:4], start=True, stop=False), reads=[Bones, BpT[i]], writes=[Bps[7]])
            kc = g * 1024 + kh * 128
            P.op("pe", lambda e: e.transpose(out=ps[:, tb_, 8:9], in_=kvt[0][0:1, kc:kc + 128], identity=ident[0:1, 0:1]), reads=[Bkvt[0], Bid], writes=[Bps[tb_]])
            P.op("act", lambda e: e.activation(out=knew[:, 0:1], in_=ps[:, tb_, 8:9], func=AF.Copy), reads=[Bps[tb_]], writes=[Bknew])
            P.op("dve", lambda e: e.tensor_copy(out=vnew[0:1, 0:128], in_=kvt[0][0:1, kc + 512:kc + 640]), reads=[Bkvt[0]], writes=[Bvnew])
            P.op("pe", lambda e: e.matmul(ps[0:1, tb_, 16:20].rearrange("p (s q) -> p s q", s=4), lhsT=knew[:, 0:1], rhs=qap, start=True, stop=True),
                 reads=[Bknew, Bq4], writes=[Bps[tb_]])
            P.op("act", lambda e: e.activation(out=misc[0:1, 0:4], in_=ps[0:1, tb_, 16:20], func=AF.Exp, scale=SCALE), reads=[Bps[tb_]], writes=[Bmisc])
            en = (g * 4 + kh) * 4
            P.op("dve", lambda e: e.tensor_tensor(out=pnew[0:1, 0:4], in0=misc[0:1, 0:4], in1=enew[0:1, en:en + 4], op=ALU.mult), reads=[Bmisc, Benew], writes=[Bpnew])
            P.op("pe", lambda e: e.matmul(ps[:, 6, 0:4], lhsT=vnew[0:1, 0:128], rhs=pnew[0:1, 0:4], start=False, stop=True), reads=[Bvnew, Bpnew], writes=[Bps[6]])
            P.op("pe", lambda e: e.matmul(ps[:, 7, 0:4], lhsT=ones[0:1, :], rhs=pnew[0:1, 0:4], start=False, stop=True), reads=[Bones, Bpnew], writes=[Bps[7]])
            for (acc, Bacc, bank) in ((num, Bnum, 6), (den, Bden, 7)):
                src2 = ps[:, bank, 0:4].rearrange("p (s q) -> p s q", s=4)
                if g == 0:
                    P.op("act", lambda e, acc=acc, src2=src2: e.activation(out=acc[:, :, 0:1], in_=src2, func=AF.Copy), reads=[Bps[bank]], writes=[Bacc])
                else:
                    P.op("dve", lambda e, acc=acc, src2=src2: e.tensor_tensor(out=acc[:, :, 0:1], in0=acc[:, :, 0:1], in1=src2, op=ALU.add),
                         reads=[Bps[bank], Bacc], writes=[Bacc])

        def b_layer(l, Tn, b, sample):
            norm_pre(V_BPRE + 32 * l, Tn, ssq=False, rstd=(l == 1))
            cols = []
            for kh in range(4):
                for g in range(3):
                    cols += [g * 2048 + (4 * kh + s) * 128 for s in range(4)]
                cols += [6144 + (4 * kh + s) * 128 for s in range(4)]

            def cons(idx, pb):
                kh, j = idx // 16, idx % 16
                if j < 12:
                    g, s = j // 4, j % 4
                    P.op("act", lambda e: e.activation(out=q4[:, s, 0:Tn], in_=ps[:, pb, 0:Tn], func=AF.Copy), reads=[Bps[pb]], writes=[Bq4])
                    if s == 3:
                        if sample:
                            attention_sample(g, kh)
                        else:
                            attention_prompt(b, g, kh)
                else:
                    s = j - 12
                    if s == 0:
                        P.op("dve", lambda e: e.reciprocal(out=den[:, :, 0:Tn], in_=den[:, :, 0:Tn]), reads=[Bden], writes=[Bden])
                        P.op("dve", lambda e: e.tensor_tensor(out=num[:, :, 0:Tn], in0=num[:, :, 0:Tn], in1=den[:, :, 0:Tn], op=ALU.mult),
                             reads=[Bnum, Bden], writes=[Bnum])
                    P.op("act", lambda e: e.activation(out=sg[:, 0:Tn], in_=ps[:, pb, 0:Tn], func=AF.Silu), reads=[Bps[pb]], writes=[Bsg])
                    P.op("dve", lambda e: e.tensor_tensor(out=hg[:, 4 * kh + s, 0:Tn], in0=num[:, s, 0:Tn], in1=sg[:, 0:Tn], op=ALU.mult),
                         reads=[Bnum, Bsg], writes=[Bh[4 * kh + s]])

            lin(b_w_in[l], 32, cols, ub, Bu, Tn, cons, ("bin", l))
            lin(b_w_out[l], 16, [m * 128 for m in range(32)], hg, Bh, Tn, out_consumer(Tn), ("bout", l))
            post(V_BPOST + 32 * l, Tn)

        def load_x(src_rows, M, col0):
            P.dma("sp", xiosem, lambda e: e.dma_start(out=xio[0:M, :], in_=src_rows), writes=[Bxio])
            for k in range(32):
                tb_ = 4 + k % 2
                P.op("pe", lambda e, k=k, tb_=tb_: e.transpose(out=ps[:, tb_, 0:M], in_=xio[0:M, k * 128:(k + 1) * 128], identity=ident[0:M, 0:M]),
                     reads=[Bxio, Bid], writes=[Bps[tb_]])
                P.op("act", lambda e, k=k, tb_=tb_: e.activation(out=xT[:, k, col0:col0 + M], in_=ps[:, tb_, 0:M], func=AF.Copy), reads=[Bps[tb_]], writes=[Bx[k]])

        def store_x(dst_rows, M, col0):
            for k in range(32):
                tb_ = 4 + k % 2
                P.op("pe", lambda e, k=k, tb_=tb_: e.transpose(out=ps[0:M, tb_, 0:128], in_=xT[:, k, col0:col0 + M], identity=ident[:, :]),
                     reads=[Bx[k], Bid], writes=[Bps[tb_]])
                P.op("act", lambda e, k=k, tb_=tb_: e.activation(out=xio[0:M, k * 128:(k + 1) * 128], in_=ps[0:M, tb_, 0:128], func=AF.Copy),
                     reads=[Bps[tb_]], writes=[Bxio])
            P.dma("sp", xiosem, lambda e: e.dma_start(out=dst_rows, in_=xio[0:M, :]), reads=[Bxio], writes=[Bout])

        def store_states(conv_o, h_o):
            for l in range(2):
                P.op("pe", lambda e, l=l: e.transpose(out=ps[:, 4, 0:128], in_=stt[l][:, :, :].rearrange("p j c -> p (j c)"), identity=ident[:, :]),
                     reads=[Bst[l], Bid], writes=[Bps[4]])
                P.op("act", lambda e: e.activation(out=stT[:, :], in_=ps[:, 4, 0:128], func=AF.Copy), reads=[Bps[4]], writes=[BstT])
                for j in range(3):
                    P.dma("sp", outsem, lambda e, l=l, j=j: e.dma_start(out=conv_o[l, j].rearrange("(c p) -> c p", p=128), in_=stT[j * 32:(j + 1) * 32, :]),
                          reads=[BstT], writes=[Bout])
                P.dma("sp", outsem, lambda e, l=l: e.dma_start(out=h_o[l].rearrange("(c p) -> c p", p=128), in_=stT[96:128, :]), reads=[BstT], writes=[Bout])

        def load_states():
            for l in range(2):
                for j in range(3):
                    P.dma("sp", setsem[4], lambda e, l=l, j=j: e.dma_start(out=stT[j * 32:(j + 1) * 32, :], in_=sconv[l, j].rearrange("(c p) -> c p", p=128)), writes=[BstT])
                P.dma("sp", setsem[4], lambda e, l=l: e.dma_start(out=stT[96:128, :], in_=sh[l].rearrange("(c p) -> c p", p=128)), writes=[BstT])
                P.op("pe", lambda e, l=l: e.transpose(out=ps[:, 4, 0:128], in_=stT[:, :], identity=ident[:, :]), reads=[BstT, Bid], writes=[Bps[4]])
                P.op("act", lambda e, l=l: e.activation(out=stt[l][:, :, :].rearrange("p j c -> p (j c)"), in_=ps[:, 4, 0:128], func=AF.Copy),
                     reads=[Bps[4]], writes=[Bst[l]])

        for b in range(NB):
            for tc in range(T // 128):
                load_x(xp[b * T + tc * 128:b * T + (tc + 1) * 128, :], 128, tc * 128)
            a_layer(0, T)
            a_layer(1, T)
            kv_phase(b * T, T, False)
            if b == NB // 2 - 1:
                for l in range(2):
                    P.op("dve", lambda e, l=l: e.tensor_scalar(out=stt[l][:, :, :].rearrange("p j c -> p (j c)"), in0=stt[l][:, :, :].rearrange("p j c -> p (j c)"),
                                                               scalar1=flg[:, 0:1], scalar2=None, op0=ALU.mult), reads=[Bst[l], Bflg], writes=[Bst[l]])
            if b >= NB // 2:
                b_layer(0, T, b, False)
                b_layer(1, T, b, False)
                bo = b - NB // 2
                for tc in range(T // 128):
                    store_x(yp[bo * T + tc * 128:bo * T + (tc + 1) * 128, :], 128, tc * 128)
        store_states(pconv, ph)
        load_states()
        load_x(xs[0:1, :], 1, 0)
        a_layer(0, 1)
        a_layer(1, 1)
        kv_phase(0, 1, True)
        for g in range(3):
            lbg = 128 * DIL[g]
            P.dma("sp", d2dsem, lambda e, g=g, lbg=lbg: e.dma_start(out=skv_o[g][0:lbg - 1, :], in_=skv[g][1:lbg, :]), writes=[Bout])
            P.dma("sp", d2dsem, lambda e, g=g, lbg=lbg: e.dma_start(out=skv_o[g][lbg - 1:lbg, :], in_=kvt[0][0:1, g * 1024:(g + 1) * 1024]),
                  reads=[Bkvt[0]], writes=[Bout])
        b_layer(0, 1, 0, True)
        b_layer(1, 1, 0, True)
        store_x(ys[0:1, :], 1, 0)
        store_states(sconv_o, sh_o)
        fin = [(sm, P.dcnt[id(sm)]) for sm in (kvosem, outsem, xiosem, d2dsem) if P.dcnt[id(sm)] > 0]
        P.q["sp"].append((None, fin, None))
        with nc.Block() as block:
            P.replay(block)
    return nc


def _rel_bucket(dist):
    dist = np.asarray(dist)
    d = np.maximum(dist, 1).astype(np.float32)
    large = 16 + (np.log(d / 16) / np.log(2048 / 16) * (32 - 16)).astype(np.int32)
    large = np.minimum(large, 31)
    return np.where(dist < 16, dist, large).astype(np.int32)


def _etiles(rel_bias):
    ein = np.zeros((128, NE), np.float32)
    em = np.zeros((128, NE), np.float32)
    enew = np.zeros((1, 48), np.float32)
    p = np.arange(128)[:, None]

    def fill(off, g, kh, diff, valid):
        nq = diff.shape[1]
        bk = _rel_bucket(np.clip(diff, 0, 128) * DIL[g])
        for s in range(4):
            col = g * 16 + 4 * kh + s
            ein[:, off + s * nq:off + (s + 1) * nq] = rel_bias[bk, col]
            em[:, off + s * nq:off + (s + 1) * nq] = valid.astype(np.float32)

    for kh in range(4):
        q = np.arange(128)[None, :]
        fill(e0_off(0, kh), 0, kh, q - p, (q - p) >= 0)
        fill(e0_off(1, kh), 0, kh, 128 + q - p, (128 + q - p) <= 128)
        q = np.arange(64)[None, :]
        for half in range(2):
            d = 64 * half + q - p
            fill(e1_off(half, 0, kh), 1, kh, d, d >= 0)
            d = 128 + 64 * half + q - p
            fill(e1_off(half, 1, kh), 1, kh, d, d <= 128)
        q = np.arange(16)[None, :]
        for b in range(NB):
            d = 16 * b + q - p
            fill(e2_off(b, kh), 2, kh, d, d >= 0)
        for g in range(3):
            d = 128 - p + np.zeros((1, 1), np.int64)
            fill(es_off(g, kh), g, kh, d, d >= 0)
            for s in range(4):
                enew[0, (g * 4 + kh) * 4 + s] = rel_bias[0, g * 16 + 4 * kh + s]
    return ein, em, enew


def _colvec(v):
    v = np.asarray(v, np.float32).reshape(-1, 32, 128)
    return np.ascontiguousarray(v.transpose(2, 0, 1).reshape(128, -1))


_NC_CACHE = {}


def kernel(x_prompt, x_sample, state_conv, state_h, state_kv_w128, state_kv_w512, state_kv_w2048,
           a_pre_g, a_w_in, a_conv_w, a_conv_b, a_w_gate_a, a_b_gate_a, a_w_gate_x, a_b_gate_x,
           a_lambda, a_w_out, a_post_g, kv_norm_g, w_kv, rel_bias, b_pre_g, b_w_in, b_w_out, b_post_g):
    f = lambda a: np.ascontiguousarray(np.asarray(a, np.float32))
    vecs = np.concatenate([
        _colvec(a_pre_g), _colvec(np.asarray(a_conv_w)), _colvec(a_conv_b),
        _colvec(np.asarray(a_b_gate_a).reshape(2, 4096)), _colvec(np.asarray(a_b_gate_x).reshape(2, 4096)),
        _colvec(a_lambda), _colvec(a_post_g), _colvec(kv_norm_g), _colvec(b_pre_g), _colvec(b_post_g)], axis=1)
    assert vecs.shape == (128, NV)
    ein, em, enew = _etiles(np.asarray(rel_bias, np.float32))
    def tile_w(w):
        w = np.asarray(w, np.float32)
        lead = w.shape[:-2]
        K, M = w.shape[-2:]
        w = w.reshape(lead + (K // 128, 128, M // 128, 128))
        nl = len(lead)
        perm = tuple(range(nl)) + (nl + 2, nl + 1, nl + 0, nl + 3)
        return np.ascontiguousarray(w.transpose(perm)).reshape(lead + (M // 128, 128, K))

    shared = dict(a_w_in=tile_w(a_w_in), a_w_out=tile_w(a_w_out), a_wga=f(a_w_gate_a), a_wgx=f(a_w_gate_x), w_kv=tile_w(w_kv),
                  b_w_in=tile_w(b_w_in), b_w_out=tile_w(b_w_out), vecs=f(vecs), ein=ein, em=em, enew=enew,
                  ident=np.eye(128, dtype=np.float32))
    xp = f(x_prompt); xs = f(x_sample); sc = f(state_conv); shh = f(state_h)
    k0 = f(state_kv_w128).reshape(8, 128, 1024); k1 = f(state_kv_w512).reshape(8, 512, 1024); k2 = f(state_kv_w2048).reshape(8, 2048, 1024)
    in_maps = []
    for c in range(8):
        m = dict(shared)
        sq_, hf = c // 2, c % 2
        xpc = xp[sq_] if hf == 1 else np.concatenate([np.zeros((1024, 4096), np.float32), xp[sq_][:1024]], axis=0)
        fl = np.zeros((128, 2), np.float32); fl[:, 0] = hf; fl[:, 1] = 1.0; fl[:64, 1] = hf
        m.update(xp=xpc, flg=fl, xs=xs[c], sconv=np.ascontiguousarray(sc[:, c]), sh=np.ascontiguousarray(shh[:, c]),
                 skv0=k0[c], skv1=k1[c], skv2=k2[c])
        in_maps.append(m)
    if "nc" not in _NC_CACHE:
        _NC_CACHE["nc"] = build()
    res = run_bass_kernel_spmd(_NC_CACHE["nc"], in_maps, core_ids=list(range(8)))
    R = res.results
    y_prompt = np.stack([np.concatenate([R[2 * q]["yp"], R[2 * q + 1]["yp"]], axis=0) for q in range(4)])
    y_sample = np.stack([R[c]["ys"] for c in range(8)])
    p_conv = np.stack([R[2 * q + 1]["pconv"] for q in range(4)], axis=1)
    p_h = np.stack([R[2 * q + 1]["ph"] for q in range(4)], axis=1)
    pkv = np.stack([R[2 * q + 1]["pkv"] for q in range(4)]).reshape(4, 2048, 3, 2, 4, 128)
    p_kv128 = np.ascontiguousarray(pkv[:, 2048 - 128:, 0])
    p_kv512 = np.ascontiguousarray(pkv[:, 2048 - 512:, 1])
    p_kv2048 = np.ascontiguousarray(pkv[:, :, 2])
    s_conv = np.stack([R[c]["sconv_o"] for c in range(8)], axis=1)
    s_h = np.stack([R[c]["sh_o"] for c in range(8)], axis=1)
    s_kv128 = np.stack([R[c]["skv_o0"] for c in range(8)]).reshape(8, 128, 2, 4, 128)
    s_kv512 = np.stack([R[c]["skv_o1"] for c in range(8)]).reshape(8, 512, 2, 4, 128)
    s_kv2048 = np.stack([R[c]["skv_o2"] for c in range(8)]).reshape(8, 2048, 2, 4, 128)
    f32 = lambda a: np.asarray(a, np.float32)
    return tuple(f32(a) for a in (y_prompt, y_sample, p_conv, p_h, p_kv128, p_kv512, p_kv2048, s_conv, s_h, s_kv128, s_kv512, s_kv2048))
```

```python
import contextlib
import numpy as np
import concourse.bass as bass
import concourse.mybir as mybir
from concourse.bass_utils import run_bass_kernel_spmd

F32 = mybir.dt.float32
BF16 = mybir.dt.bfloat16
AF = mybir.ActivationFunctionType
ALU = mybir.AluOpType

SAME_ENGINE_SYNC = True
T = 256
NB = 2048 // T
EPS = 1e-6
SCALE = 128 ** -0.5
DIL = (1, 4, 16)


class Buf:
    __slots__ = ("name", "w", "r")

    def __init__(self, name=""):
        self.name = name
        self.w = None
        self.r = {}


class Prog:
    ENGS = ("pe", "act", "dve", "pool", "sp")

    def __init__(self, nc, stack):
        self.nc = nc
        self.stack = stack
        self.q = {e: [] for e in self.ENGS}
        self.cnt = {e: 0 for e in self.ENGS}
        self.seen = {e: {} for e in self.ENGS}
        self.esem = {e: stack.enter_context(nc.semaphore("es_" + e)) for e in self.ENGS}
        self.dcnt = {}

    def dma_sem(self, name):
        s = self.stack.enter_context(self.nc.semaphore(name))
        self.dcnt[id(s)] = 0
        return s

    def sb(self, name, shape, dt):
        return self.stack.enter_context(self.nc.sbuf_tensor("sb_" + name, list(shape), dt))

    def ps(self, name, shape, dt=F32):
        return self.stack.enter_context(self.nc.psum_tensor("psum_" + name, list(shape), dt))

    def _deps(self, eng, reads, writes):
        deps = []
        for b in reads:
            if b.w is not None:
                deps.append(b.w)
        for b in writes:
            if b.w is not None:
                deps.append(b.w)
            deps.extend(b.r.values())
        waits = []
        seen = self.seen[eng]
        for (sem, val, src) in deps:
            if src == eng and (eng == "pe" or not SAME_ENGINE_SYNC):
                continue
            k = id(sem)
            if seen.get(k, 0) >= val:
                continue
            seen[k] = val
            waits.append((sem, val))
        return waits

    def _commit(self, tok, reads, writes):
        for b in writes:
            b.w = tok
            b.r = {}
        for b in reads:
            if b not in writes:
                b.r[id(tok[0])] = tok

    def op(self, eng, fn, reads=(), writes=(), inc=True):
        waits = self._deps(eng, reads, writes)
        if inc:
            self.cnt[eng] += 1
            tok = (self.esem[eng], self.cnt[eng], eng)
        else:
            tok = (self.esem[eng], self.cnt[eng] + 1, eng)
        self.q[eng].append((fn, waits, (self.esem[eng], 1) if inc else None))
        self._commit(tok, reads, writes)
        return tok

    def dma(self, eng, sem, fn, reads=(), writes=()):
        waits = self._deps(eng, reads, writes)
        self.dcnt[id(sem)] += 16
        tok = (sem, self.dcnt[id(sem)], "dma")
        self.q[eng].append((fn, waits, (sem, 16)))
        self._commit(tok, reads, writes)
        return tok

    def wait_all(self, eng, bufs):
        waits = self._deps(eng, (), bufs)
        self.q[eng].append((None, waits, None))

    def replay(self, block):
        names = {"pe": "tensor", "act": "scalar", "dve": "vector", "pool": "gpsimd", "sp": "sync"}

        def run(e, engobj):
            for fn, waits, inc in self.q[e]:
                for sem, val in waits:
                    engobj.wait_ge(sem, val)
                if fn is None:
                    continue
                ins = fn(engobj)
                if inc is not None:
                    ins.then_inc(inc[0], inc[1])

        for e in self.ENGS:
            if not self.q[e]:
                continue

            def mk(e):
                def body(engobj):
                    run(e, engobj)
                return body
            getattr(block, names[e])(mk(e))


V_APRE, V_CW, V_CB, V_BGA, V_BGX, V_LAM, V_APOST, V_KVG, V_BPRE, V_BPOST = 0, 64, 320, 384, 448, 512, 576, 640, 672, 736
NV = 800
E0 = 0
E1 = 4096
E2 = 8192
ES = 10240
NE = 10240 + 48


def e0_off(pc, kh): return E0 + (pc * 4 + kh) * 512
def e1_off(half, pc, kh): return E1 + ((half * 2 + pc) * 4 + kh) * 256
def e2_off(b, kh): return E2 + (b * 4 + kh) * 64
def es_off(g, kh): return ES + (g * 4 + kh) * 4


def build():
    nc = bass.Bass("TRN2", target_bir_lowering=False)

    def D(name, shape, dt=F32, kind="ExternalInput"):
        return nc.dram_tensor(name, list(shape), dt, kind=kind).ap()

    xp = D("xp", [2048, 4096]); xs = D("xs", [1, 4096])
    sconv = D("sconv", [2, 3, 4096]); sh = D("sh", [2, 4096])
    skv = [D("skv0", [128, 1024]), D("skv1", [512, 1024]), D("skv2", [2048, 1024])]
    a_w_in = D("a_w_in", [2, 64, 128, 4096]); a_w_out = D("a_w_out", [2, 32, 128, 4096])
    a_wga = D("a_wga", [2, 16, 256, 256]); a_wgx = D("a_wgx", [2, 16, 256, 256])
    w_kv = D("w_kv", [24, 128, 4096])
    b_w_in = D("b_w_in", [2, 64, 128, 4096]); b_w_out = D("b_w_out", [2, 32, 128, 2048])
    vecs_d = D("vecs", [128, NV]); ein_d = D("ein", [128, NE]); em_d = D("em", [128, NE])
    enew_d = D("enew", [1, 48]); ident_d = D("ident", [128, 128]); flg_d = D("flg", [128, 2])
    O = lambda n, s: D(n, s, kind="ExternalOutput")
    yp = O("yp", [1024, 4096]); ys = O("ys", [1, 4096])
    pconv = O("pconv", [2, 3, 4096]); ph = O("ph", [2, 4096]); pkv = O("pkv", [2048, 3072])
    sconv_o = O("sconv_o", [2, 3, 4096]); sh_o = O("sh_o", [2, 4096])
    skv_o = [O("skv_o0", [128, 1024]), O("skv_o1", [512, 1024]), O("skv_o2", [2048, 1024])]
    kvs = nc.dram_tensor("kvs", [2048 + 16, 3072], F32).ap()
    wsc = {}
    for l in range(2):
        wsc[("ain", l)] = nc.dram_tensor("wsc_ain%d" % l, [64, 128, 4096], BF16).ap()
        wsc[("aout", l)] = nc.dram_tensor("wsc_aout%d" % l, [32, 128, 4096], BF16).ap()
        wsc[("bin", l)] = nc.dram_tensor("wsc_bin%d" % l, [64, 128, 4096], BF16).ap()
        wsc[("bout", l)] = nc.dram_tensor("wsc_bout%d" % l, [32, 128, 2048], BF16).ap()
    wsc[("kv", 0)] = nc.dram_tensor("wsc_kv", [24, 128, 4096], BF16).ap()
    wdone = {}

    with contextlib.ExitStack() as st:
        P = Prog(nc, st)
        xT = P.sb("xT", [128, 32, T], F32); Bx = [Buf() for _ in range(32)]
        ub = P.sb("ub", [128, 32, T], BF16); Bu = [Buf() for _ in range(32)]
        hg = P.sb("hg", [128, 32, T], BF16); Bh = [Buf() for _ in range(32)]
        NW = 3
        wbf = [P.sb("wbf%d" % i, [128, 32, 128], BF16) for i in range(NW)]; Bw = [Buf() for _ in range(NW)]
        wsem = [P.dma_sem("wsem%d" % i) for i in range(NW)]
        ssem = [P.dma_sem("ssem%d" % i) for i in range(NW)]
        wg = P.sb("wg", [128, 2, 2, 256], BF16); Bwg = Buf(); wgsem = P.dma_sem("wgsem")
        vecs = P.sb("vecs", [128, NV], F32); Bv = Buf()
        c1 = P.sb("c1", [128, 64], F32); Bc1 = Buf()
        Eall = P.sb("Eall", [128, NE], BF16); BE = Buf()
        enew = P.sb("enew", [1, 48], F32); Benew = Buf()
        ident = P.sb("ident", [128, 128], F32); Bid = Buf()
        flg = P.sb("flg", [128, 2], F32); Bflg = Buf(); flgsem = P.dma_sem("flgsem")
        ones = P.sb("ones", [128, 128], BF16); Bones = Buf()
        rs = P.sb("rs", [128, T], F32); Brs = Buf()
        sqb = [P.sb("sqb%d" % i, [128, T], BF16) for i in range(2)]; Bsq = [Buf(), Buf()]
        stt = [P.sb("st%d" % l, [128, 4, 32], F32) for l in range(2)]; Bst = [Buf(), Buf()]
        stT = P.sb("stT", [128, 128], F32); BstT = Buf()
        xc = [P.sb("xc%d" % i, [128, 3 + T], F32) for i in range(2)]; Bxc = [Buf(), Buf()]
        xv = [[P.sb("xv%d%d" % (hp, i), [128, T], F32) for i in range(2)] for hp in range(2)]; Bxv = [[Buf(), Buf()], [Buf(), Buf()]]
        xvb = [[P.sb("xvb%d%d" % (hp, i), [128, T], BF16) for i in range(2)] for hp in range(2)]; Bxvb = [[Buf(), Buf()], [Buf(), Buf()]]
        rg = [[P.sb("rg%d%d" % (a, j), [128, T], F32) for j in range(2)] for a in range(2)]
        Brg = [[Buf(), Buf()], [Buf(), Buf()]]
        ta = P.sb("ta", [128, T], F32); Bta = Buf()
        tb = P.sb("tb", [128, T], F32); Btb = Buf()
        tcc = P.sb("tcc", [128, T], F32); Btc = Buf()
        hh = [P.sb("hh%d" % i, [128, T], F32) for i in range(2)]; Bhh = [Buf(), Buf()]
        sg = P.sb("sg", [128, T], F32); Bsg = Buf()
        sg2 = [[P.sb("sg2_%d%d" % (hp, i), [128, T], F32) for i in range(2)] for hp in range(2)]; Bsg2 = [[Buf(), Buf()], [Buf(), Buf()]]
        tmpx = P.sb("tmpx", [128, T], F32); Btmpx = Buf()
        xio = P.sb("xio", [128, 4096], F32); Bxio = Buf(); xiosem = P.dma_sem("xiosem")
        kvt = [P.sb("kvt%d" % i, [128, 3072], F32) for i in range(2)]; Bkvt = [Buf(), Buf()]
        kvosem = P.dma_sem("kvosem")
        q4 = P.sb("q4", [128, 4, T], BF16); Bq4 = Buf()
        num = P.sb("num", [128, 4, T], F32); Bnum = Buf()
        den = P.sb("den", [128, 4, T], F32); Bden = Buf()
        kvin = [P.sb("kvin%d" % i, [128, 2, 128], F32) for i in range(2)]; Bkvin = [Buf(), Buf()]
        kvinsem = [P.dma_sem("kvinsem%d" % i) for i in range(2)]
        kTb = [P.sb("kTb%d" % i, [128, 128], BF16) for i in range(2)]; BkT = [Buf(), Buf()]
        vbb = [P.sb("vbb%d" % i, [128, 128], BF16) for i in range(2)]; Bvb = [Buf(), Buf()]
        pT = [P.sb("pT%d" % i, [128, 512], BF16) for i in range(2)]; BpT = [Buf(), Buf()]
        esb = P.sb("esb", [128, 512], F32); Besb = Buf()
        emb = P.sb("emb", [128, 512], F32); Bemb = Buf()
        misc = P.sb("misc", [128, 64], F32); Bmisc = Buf()
        knew = P.sb("knew", [128, 16], BF16); Bknew = Buf()
        vnew = P.sb("vnew", [1, 128], BF16); Bvnew = Buf()
        pnew = P.sb("pnew", [1, 16], BF16); Bpnew = Buf()
        ps = P.ps("ps", [128, 8, 512], F32); Bps = [Buf() for _ in range(8)]
        setsem = [P.dma_sem("setsem%d" % i) for i in range(6)]
        outsem = P.dma_sem("outsem"); Bout = Buf()
        d2dsem = P.dma_sem("d2dsem"); Bd2d = Buf()

        P.dma("sp", setsem[0], lambda e: e.dma_start(out=vecs[:, :], in_=vecs_d), writes=[Bv])
        P.dma("sp", setsem[1], lambda e: e.dma_start(out=enew[:, :], in_=enew_d), writes=[Benew])
        P.dma("sp", flgsem, lambda e: e.dma_start(out=flg[:, :], in_=flg_d), writes=[Bflg])
        P.dma("sp", setsem[5], lambda e: e.dma_start(out=ident[:, :], in_=ident_d), writes=[Bid])
        P.op("pool", lambda e: e.memset(ones[:, :], 1.0), writes=[Bones])
        for i in range(2):
            P.op("pool", lambda e, i=i: e.memset(kvin[i][:, :, :], 0.0), writes=[Bkvin[i]])
            P.op("pool", lambda e, i=i: e.memset(kvt[i][:, :], 0.0), writes=[Bkvt[i]])
        for l in range(2):
            P.op("pool", lambda e, l=l: e.memset(stt[l][:, :, :], 0.0), writes=[Bst[l]])
        P.op("act", lambda e: e.activation(out=c1[:, :], in_=vecs[:, V_LAM:V_LAM + 64], func=AF.Exp, scale=-1.0), reads=[Bv], writes=[Bc1])
        P.op("act", lambda e: e.activation(out=c1[:, :], in_=c1[:, :], func=AF.Ln, bias=1.0, scale=1.0), reads=[Bc1], writes=[Bc1])
        P.op("dve", lambda e: e.tensor_scalar(out=c1[:, :], in0=c1[:, :], scalar1=-8.0, scalar2=None, op0=ALU.mult), reads=[Bc1], writes=[Bc1])
        for c0 in range(0, NE, 512):
            cw = min(512, NE - c0)
            P.dma("sp", setsem[2], lambda e, c0=c0, cw=cw: e.dma_start(out=esb[:, 0:cw], in_=ein_d[:, c0:c0 + cw]), writes=[Besb])
            P.dma("sp", setsem[3], lambda e, c0=c0, cw=cw: e.dma_start(out=emb[:, 0:cw], in_=em_d[:, c0:c0 + cw]), writes=[Bemb])
            P.op("act", lambda e, cw=cw: e.activation(out=esb[:, 0:cw], in_=esb[:, 0:cw], func=AF.Exp), reads=[Besb], writes=[Besb])
            P.op("dve", lambda e, c0=c0, cw=cw: e.tensor_tensor(out=Eall[:, c0:c0 + cw], in0=esb[:, 0:cw], in1=emb[:, 0:cw], op=ALU.mult),
                 reads=[Besb, Bemb], writes=[BE])
        P.op("act", lambda e: e.activation(out=enew[:, :], in_=enew[:, :], func=AF.Exp), reads=[Benew], writes=[Benew])

        wctr = [0]
        pctr = [0]

        def wload(Wsl, nk, col0, key):
            s_ = wctr[0] % NW; wctr[0] += 1
            strip = col0 // 128
            dst = wbf[s_][:, 0:nk, :].rearrange("p k m -> p (k m)")
            kk = (key, strip)
            if kk not in wdone:
                P.dma("pool", wsem[s_], lambda e: e.dma_start(out=dst, in_=Wsl[strip], max_dma_last_dim=8192), writes=[Bw[s_]])
                bb = Buf(); wdone[kk] = bb
                P.dma("sp", ssem[s_], lambda e: e.dma_start(out=wsc[key][strip], in_=dst), reads=[Bw[s_]], writes=[bb])
            else:
                P.dma("pool", wsem[s_], lambda e: e.dma_start(out=dst, in_=wsc[key][strip]), reads=[wdone[kk]], writes=[Bw[s_]])
            return s_

        def lin(Wsl, nk, cols, src, Bsrc, Tn, consumer, key):
            slots = {}
            for i0 in range(min(2, len(cols))):
                slots[i0] = wload(Wsl, nk, cols[i0], key)
            for idx, col0 in enumerate(cols):
                s = slots.pop(idx)
                pb = pctr[0] % 3; pctr[0] += 1
                for k in range(nk):
                    P.op("pe", lambda e, s=s, k=k, pb=pb: e.matmul(ps[:, pb, 0:Tn], lhsT=wbf[s][:, k, :], rhs=src[:, k, 0:Tn],
                                                                  start=(k == 0), stop=(k == nk - 1)),
                         reads=[Bw[s], Bsrc[k]], writes=[Bps[pb]], inc=(k == nk - 1))
                if idx + 2 < len(cols):
                    slots[idx + 2] = wload(Wsl, nk, cols[idx + 2], key)
                consumer(idx, pb)

        def rstd_from_ps3(Tn):
            P.op("act", lambda e: e.activation(out=rs[:, 0:Tn], in_=ps[:, 3, 0:Tn], func=AF.Sqrt, bias=EPS, scale=1.0 / 4096), reads=[Bps[3]], writes=[Brs])
            P.op("dve", lambda e: e.reciprocal(out=rs[:, 0:Tn], in_=rs[:, 0:Tn]), reads=[Brs], writes=[Brs])

        def norm_pre(gcol, Tn, ssq=True, rstd=True):
            for k in (range(32) if ssq else ()):
                i = k % 2
                P.op("dve", lambda e, k=k, i=i: e.tensor_tensor(out=sqb[i][:, 0:Tn], in0=xT[:, k, 0:Tn], in1=xT[:, k, 0:Tn], op=ALU.mult),
                     reads=[Bx[k]], writes=[Bsq[i]])
                P.op("pe", lambda e, k=k, i=i: e.matmul(ps[:, 3, 0:Tn], lhsT=ones[:, :], rhs=sqb[i][:, 0:Tn], start=(k == 0), stop=(k == 31)),
                     reads=[Bsq[i], Bones], writes=[Bps[3]], inc=True)
            if rstd:
                rstd_from_ps3(Tn)
            for k in range(32):
                P.op("dve", lambda e, k=k: e.scalar_tensor_tensor(out=ub[:, k, 0:Tn], in0=xT[:, k, 0:Tn], scalar=vecs[:, gcol + k:gcol + k + 1],
                                                                  in1=rs[:, 0:Tn], op0=ALU.mult, op1=ALU.mult),
                     reads=[Bx[k], Brs, Bv], writes=[Bu[k]])

        def out_consumer(Tn):
            pend = []

            def flush():
                for (m, i) in pend:
                    P.op("pe", lambda e, m=m, i=i: e.matmul(ps[:, 3, 0:Tn], lhsT=ones[:, :], rhs=sqb[i][:, 0:Tn], start=(m == 0), stop=(m == 31)),
                         reads=[Bsq[i], Bones], writes=[Bps[3]], inc=True)
                del pend[:]

            def cons(m, pb):
                i = m % 2
                P.op("act", lambda e, m=m, pb=pb: e.activation(out=ub[:, m, 0:Tn], in_=ps[:, pb, 0:Tn], func=AF.Copy), reads=[Bps[pb]], writes=[Bu[m]])
                flush()
                P.op("act", lambda e, i=i, pb=pb: e.activation(out=sqb[i][:, 0:Tn], in_=ps[:, pb, 0:Tn], func=AF.Square), reads=[Bps[pb]], writes=[Bsq[i]])
                pend.append((m, i))
                if m == 31:
                    flush()
            return cons

        def post(gcol, Tn):
            rstd_from_ps3(Tn)
            for k in range(32):
                P.op("dve", lambda e, k=k: e.scalar_tensor_tensor(out=tmpx[:, 0:Tn], in0=ub[:, k, 0:Tn], scalar=vecs[:, gcol + k:gcol + k + 1],
                                                                  in1=rs[:, 0:Tn], op0=ALU.mult, op1=ALU.mult),
                     reads=[Bu[k], Brs, Bv], writes=[Btmpx])
                P.op("dve", lambda e, k=k: e.tensor_tensor(out=xT[:, k, 0:Tn], in0=xT[:, k, 0:Tn], in1=tmpx[:, 0:Tn], op=ALU.add),
                     reads=[Btmpx, Bx[k]], writes=[Bx[k]])
                i = k % 2
                P.op("act", lambda e, k=k, i=i: e.activation(out=sqb[i][:, 0:Tn], in_=xT[:, k, 0:Tn], func=AF.Square), reads=[Bx[k]], writes=[Bsq[i]])
                P.op("pe", lambda e, k=k, i=i: e.matmul(ps[:, 3, 0:Tn], lhsT=ones[:, :], rhs=sqb[i][:, 0:Tn], start=(k == 0), stop=(k == 31)),
                     reads=[Bsq[i], Bones], writes=[Bps[3]], inc=True)

        def a_layer(l, Tn):
            norm_pre(V_APRE + 32 * l, Tn, ssq=(l == 0))
            S = stt[l]; BS = Bst[l]
            cols = []
            for h in range(16):
                cols += [(2 * h) * 128, (2 * h + 1) * 128, 4096 + (2 * h) * 128, 4096 + (2 * h + 1) * 128]

            def chain(h):
                hp = h % 2
                P.dma("pool", wgsem, lambda e: e.dma_start(out=wg[:, 0, :, :], in_=a_wga[l, h].rearrange("(ic p) o -> p ic o", p=128)), writes=[Bwg])
                P.dma("pool", wgsem, lambda e: e.dma_start(out=wg[:, 1, :, :], in_=a_wgx[l, h].rearrange("(ic p) o -> p ic o", p=128)), writes=[Bwg])
                for a in range(2):
                    bcol = (V_BGA if a == 0 else V_BGX) + 32 * l
                    for jo in range(2):
                        gb = 4 + (a * 2 + jo) % 2
                        for ic in range(2):
                            P.op("pe", lambda e, a=a, jo=jo, ic=ic, gb=gb: e.matmul(ps[:, gb, 0:Tn], lhsT=wg[:, a, ic, jo * 128:(jo + 1) * 128],
                                                                                    rhs=xvb[hp][ic][:, 0:Tn], start=(ic == 0), stop=(ic == 1)),
                                 reads=[Bwg, Bxvb[hp][ic]], writes=[Bps[gb]], inc=(ic == 1))
                        cc = 2 * h + jo
                        P.op("act", lambda e, a=a, jo=jo, gb=gb, cc=cc, bcol=bcol: e.activation(
                            out=rg[a][jo][:, 0:Tn], in_=ps[:, gb, 0:Tn], func=AF.Sigmoid, bias=vecs[:, bcol + cc:bcol + cc + 1], scale=1.0),
                            reads=[Bps[gb], Bv], writes=[Brg[a][jo]])
                for jo in range(2):
                    cc = 2 * h + jo
                    P.op("act", lambda e, jo=jo, cc=cc: e.activation(out=ta[:, 0:Tn], in_=rg[0][jo][:, 0:Tn], func=AF.Exp,
                                                                     scale=c1[:, 32 * l + cc:32 * l + cc + 1]),
                         reads=[Brg[0][jo], Bc1], writes=[Bta])
                    P.op("dve", lambda e: e.tensor_tensor(out=tb[:, 0:Tn], in0=ta[:, 0:Tn], in1=ta[:, 0:Tn], op=ALU.mult), reads=[Bta], writes=[Btb])
                    P.op("act", lambda e: e.activation(out=tb[:, 0:Tn], in_=tb[:, 0:Tn], func=AF.Sqrt, bias=1.0, scale=-1.0), reads=[Btb], writes=[Btb])
                    P.op("dve", lambda e, jo=jo: e.tensor_tensor(out=tcc[:, 0:Tn], in0=rg[1][jo][:, 0:Tn], in1=xv[hp][jo][:, 0:Tn], op=ALU.mult),
                         reads=[Brg[1][jo], Bxv[hp][jo]], writes=[Btc])
                    P.op("dve", lambda e: e.tensor_tensor(out=tcc[:, 0:Tn], in0=tcc[:, 0:Tn], in1=tb[:, 0:Tn], op=ALU.mult), reads=[Btc, Btb], writes=[Btc])
                    P.op("dve", lambda e, jo=jo, cc=cc: e.tensor_tensor_scan(out=hh[jo][:, 0:Tn], data0=ta[:, 0:Tn], data1=tcc[:, 0:Tn],
                                                                             initial=S[:, 3, cc:cc + 1], op0=ALU.mult, op1=ALU.add),
                         reads=[Bta, Btc, BS], writes=[Bhh[jo]])
                    P.op("dve", lambda e, jo=jo, cc=cc: e.tensor_copy(out=S[:, 3, cc:cc + 1], in_=hh[jo][:, Tn - 1:Tn]), reads=[Bhh[jo]], writes=[BS])
                for j2 in range(2):
                    c2 = 2 * h + j2
                    P.op("dve", lambda e, j2=j2, c2=c2: e.tensor_tensor(out=hg[:, c2, 0:Tn], in0=hh[j2][:, 0:Tn], in1=sg2[hp][j2][:, 0:Tn], op=ALU.mult),
                         reads=[Bhh[j2], Bsg2[hp][j2]], writes=[Bh[c2]])

            def cons(idx, pb):
                h, j = idx // 4, idx % 4
                hp = h % 2
                if j < 2:
                    c = 2 * h + j
                    P.op("dve", lambda e: e.tensor_copy(out=xc[j][:, 0:3], in_=S[:, 0:3, c]), reads=[BS], writes=[Bxc[j]])
                    P.op("act", lambda e: e.activation(out=xc[j][:, 3:3 + Tn], in_=ps[:, pb, 0:Tn], func=AF.Copy), reads=[Bps[pb]], writes=[Bxc[j]])
                    P.op("dve", lambda e: e.tensor_copy(out=S[:, 0:3, c], in_=xc[j][:, Tn:Tn + 3]), reads=[Bxc[j]], writes=[BS])
                    cwc = V_CW + l * 128
                    P.op("dve", lambda e: e.tensor_scalar(out=xv[hp][j][:, 0:Tn], in0=xc[j][:, 3:3 + Tn], scalar1=vecs[:, cwc + 96 + c:cwc + 97 + c],
                                                          scalar2=vecs[:, V_CB + 32 * l + c:V_CB + 32 * l + c + 1], op0=ALU.mult, op1=ALU.add),
                         reads=[Bxc[j], Bv], writes=[Bxv[hp][j]])
                    for kk in range(3):
                        P.op("dve", lambda e, kk=kk: e.scalar_tensor_tensor(out=xv[hp][j][:, 0:Tn], in0=xc[j][:, kk:kk + Tn],
                                                                            scalar=vecs[:, cwc + 32 * kk + c:cwc + 32 * kk + c + 1],
                                                                            in1=xv[hp][j][:, 0:Tn], op0=ALU.mult, op1=ALU.add),
                             reads=[Bxc[j], Bv, Bxv[hp][j]], writes=[Bxv[hp][j]])
                    P.op("act", lambda e: e.activation(out=xvb[hp][j][:, 0:Tn], in_=xv[hp][j][:, 0:Tn], func=AF.Copy), reads=[Bxv[hp][j]], writes=[Bxvb[hp][j]])
                else:
                    jo = j - 2
                    P.op("act", lambda e: e.activation(out=sg2[hp][jo][:, 0:Tn], in_=ps[:, pb, 0:Tn], func=AF.Silu), reads=[Bps[pb]], writes=[Bsg2[hp][jo]])
                    if jo == 1:
                        if h >= 1:
                            chain(h - 1)
                        if h == 15:
                            chain(15)

            lin(a_w_in[l], 32, cols, ub, Bu, Tn, cons, ("ain", l))
            lin(a_w_out[l], 32, [m * 128 for m in range(32)], hg, Bh, Tn, out_consumer(Tn), ("aout", l))
            post(V_APOST + 32 * l, Tn)

        def kv_phase(row0, Tn, sample):
            norm_pre(V_KVG, Tn, ssq=False)
            ntc = max(1, Tn // 128)
            M = min(128, Tn)
            slots = {0: wload(w_kv, 32, 0, ("kv", 0)), 1: wload(w_kv, 32, 128, ("kv", 0))}
            for n in range(24):
                s = slots.pop(n)
                if n + 2 < 24 and n >= 1:
                    pass
                for tc in range(ntc):
                    pb = pctr[0] % 3; pctr[0] += 1
                    for k in range(32):
                        P.op("pe", lambda e, s=s, k=k, pb=pb, tc=tc: e.matmul(ps[0:M, pb, 0:128], lhsT=ub[:, k, tc * 128:tc * 128 + M], rhs=wbf[s][:, k, :],
                                                                              start=(k == 0), stop=(k == 31)),
                             reads=[Bw[s], Bu[k]], writes=[Bps[pb]], inc=(k == 31))
                    P.op("act", lambda e, pb=pb, tc=tc, n=n: e.activation(out=kvt[tc][0:M, n * 128:(n + 1) * 128], in_=ps[0:M, pb, 0:128], func=AF.Copy),
                         reads=[Bps[pb]], writes=[Bkvt[tc]])
                if n + 2 < 24:
                    slots[n + 2] = wload(w_kv, 32, (n + 2) * 128, ("kv", 0))
            if not sample:
                for tc in range(ntc):
                    r0 = row0 + tc * 128
                    P.dma("sp", kvosem, lambda e, tc=tc, r0=r0: e.dma_start(out=pkv[r0:r0 + 128, :], in_=kvt[tc][:, :]), reads=[Bkvt[tc]], writes=[Bout])
                    P.dma("sp", kvosem, lambda e, tc=tc, r0=r0: e.dma_start(out=kvs[r0:r0 + 128, :], in_=kvt[tc][:, :]), reads=[Bkvt[tc]], writes=[Bd2d])

        uctr = [0]

        def key_tile(src_rows_ap, nrows, g, kh):
            i = uctr[0] % 2; uctr[0] += 1
            P.dma("sp", kvinsem[i], lambda e: e.dma_start(out=kvin[i][0:nrows, :, :], in_=src_rows_ap), reads=[Bd2d], writes=[Bkvin[i]])
            tb_ = 4 + i
            P.op("pe", lambda e: e.transpose(out=ps[:, tb_, 0:128], in_=kvin[i][:, 0, :], identity=ident[:, :]), reads=[Bkvin[i], Bid], writes=[Bps[tb_]])
            P.op("act", lambda e: e.activation(out=kTb[i][:, :], in_=ps[:, tb_, 0:128], func=AF.Copy), reads=[Bps[tb_]], writes=[BkT[i]])
            P.op("dve", lambda e: e.tensor_copy(out=vbb[i][:, :], in_=kvin[i][:, 1, :]), reads=[Bkvin[i]], writes=[Bvb[i]])
            return i

        def attn_unit(qap, nq, tiles, acc_first, accap_fn):
            W = 4 * nq
            for ti, (i, eoff, fcol) in enumerate(tiles):
                sb_ = 4 + i
                P.op("pe", lambda e, i=i, sb_=sb_: e.matmul(ps[:, sb_, 0:W].rearrange("p (s q) -> p s q", s=4), lhsT=kTb[i][:, :], rhs=qap, start=True, stop=True),
                     reads=[BkT[i], Bq4], writes=[Bps[sb_]])
                P.op("act", lambda e, i=i, sb_=sb_: e.activation(out=pT[i][:, 0:W], in_=ps[:, sb_, 0:W], func=AF.Exp, scale=SCALE), reads=[Bps[sb_]], writes=[BpT[i]])
                if fcol is None:
                    P.op("dve", lambda e, i=i, eoff=eoff: e.tensor_tensor(out=pT[i][:, 0:W], in0=pT[i][:, 0:W], in1=Eall[:, eoff:eoff + W], op=ALU.mult),
                         reads=[BpT[i], BE], writes=[BpT[i]])
                else:
                    P.op("dve", lambda e, i=i, eoff=eoff, fcol=fcol: e.scalar_tensor_tensor(
                        out=pT[i][:, 0:W], in0=pT[i][:, 0:W], scalar=flg[:, fcol:fcol + 1], in1=Eall[:, eoff:eoff + W], op0=ALU.mult, op1=ALU.mult),
                         reads=[BpT[i], BE, Bflg], writes=[BpT[i]])
                first, last = ti == 0, ti == len(tiles) - 1
                P.op("pe", lambda e, i=i, first=first, last=last: e.matmul(ps[:, 6, 0:W], lhsT=vbb[i][:, :], rhs=pT[i][:, 0:W], start=first, stop=last),
                     reads=[Bvb[i], BpT[i]], writes=[Bps[6]])
                P.op("pe", lambda e, i=i, first=first, last=last: e.matmul(ps[:, 7, 0:W], lhsT=ones[:, :], rhs=pT[i][:, 0:W], start=first, stop=last),
                     reads=[Bones, BpT[i]], writes=[Bps[7]])
            for (acc, Bacc, bank) in ((num, Bnum, 6), (den, Bden, 7)):
                src = ps[:, bank, 0:W].rearrange("p (s q) -> p s q", s=4)
                if acc_first:
                    P.op("act", lambda e, acc=acc, src=src: e.activation(out=accap_fn(acc), in_=src, func=AF.Copy), reads=[Bps[bank]], writes=[Bacc])
                else:
                    P.op("dve", lambda e, acc=acc, src=src: e.tensor_tensor(out=accap_fn(acc), in0=accap_fn(acc), in1=src, op=ALU.add),
                         reads=[Bps[bank], Bacc], writes=[Bacc])

        def attention_prompt(b, g, kh):
            t0 = b * T
            kc = g * 1024 + kh * 128

            def rows(start, step, count):
                v = kvs[start:start + step * count, :].rearrange("(j c) n -> j c n", c=step)[:, 0, :]
                return v[:, g * 1024:(g + 1) * 1024].rearrange("j (v k x) -> j v k x", v=2, k=4)[:, :, kh, :]

            if g == 0:
                for qb in range(T // 128):
                    tiles = []
                    if t0 + qb * 128 - 128 >= 0:
                        i = key_tile(rows(t0 + qb * 128 - 128, 1, 128), 128, g, kh)
                        tiles.append((i, e0_off(1, kh), 0 if (t0 + qb * 128 - 128) < 1024 else None))
                    i = key_tile(rows(t0 + qb * 128, 1, 128), 128, g, kh)
                    tiles.append((i, e0_off(0, kh), None))
                    qap = q4[:, :, qb * 128:(qb + 1) * 128]
                    attn_unit(qap, 128, tiles, True, lambda acc, qb=qb: acc[:, :, qb * 128:(qb + 1) * 128])
            elif g == 1:
                nbi, half = b // 2, b % 2
                for c in range(4):
                    tiles = []
                    if nbi > 0:
                        i = key_tile(rows(512 * (nbi - 1) + c, 4, 128), 128, g, kh)
                        tiles.append((i, e1_off(half, 1, kh), 0 if 512 * (nbi - 1) < 1024 else None))
                    cnt = 64 * (half + 1)
                    i = key_tile(rows(512 * nbi + c, 4, cnt), cnt, g, kh)
                    tiles.append((i, e1_off(half, 0, kh), None))
                    qap = q4[:, :, :].rearrange("p s (j c) -> p s c j", c=4)[:, :, c, :]
                    attn_unit(qap, 64, tiles, False, lambda acc, c=c: acc[:, :, :].rearrange("p s (j c) -> p s c j", c=4)[:, :, c, :])
            else:
                for c in range(16):
                    cnt = 16 * (b + 1)
                    i = key_tile(rows(c, 16, cnt), cnt, g, kh)
                    qap = q4[:, :, :].rearrange("p s (j c) -> p s c j", c=16)[:, :, c, :]
                    attn_unit(qap, 16, [(i, e2_off(b, kh), 1)], False, lambda acc, c=c: acc[:, :, :].rearrange("p s (j c) -> p s c j", c=16)[:, :, c, :])

        def attention_sample(g, kh):
            r = DIL[g]
            lb = 128 * r
            src = skv[g].rearrange("(j c) (v x) -> j c v x", c=r, v=2)[:, 0, :, kh * 128:(kh + 1) * 128]
            i = uctr[0] % 2; uctr[0] += 1
            P.dma("sp", kvinsem[i], lambda e: e.dma_start(out=kvin[i][:, :, :], in_=src), writes=[Bkvin[i]])
            tb_ = 4 + i
            P.op("pe", lambda e: e.transpose(out=ps[:, tb_, 0:128], in_=kvin[i][:, 0, :], identity=ident[:, :]), reads=[Bkvin[i], Bid], writes=[Bps[tb_]])
            P.op("act", lambda e: e.activation(out=kTb[i][:, :], in_=ps[:, tb_, 0:128], func=AF.Copy), reads=[Bps[tb_]], writes=[BkT[i]])
            P.op("dve", lambda e: e.tensor_copy(out=vbb[i][:, :], in_=kvin[i][:, 1, :]), reads=[Bkvin[i]], writes=[Bvb[i]])
            qap = q4[:, :, 0:1]
            P.op("pe", lambda e: e.matmul(ps[:, tb_, 0:4].rearrange("p (s q) -> p s q", s=4), lhsT=kTb[i][:, :], rhs=qap, start=True, stop=True),
                 reads=[BkT[i], Bq4], writes=[Bps[tb_]])
            P.op("act", lambda e: e.activation(out=pT[i][:, 0:4], in_=ps[:, tb_, 0:4], func=AF.Exp, scale=SCALE), reads=[Bps[tb_]], writes=[BpT[i]])
            eo = es_off(g, kh)
            P.op("dve", lambda e: e.tensor_tensor(out=pT[i][:, 0:4], in0=pT[i][:, 0:4], in1=Eall[:, eo:eo + 4], op=ALU.mult), reads=[BpT[i], BE], writes=[BpT[i]])
            P.op("pe", lambda e: e.matmul(ps[:, 6, 0:4], lhsT=vbb[i][:, :], rhs=pT[i][:, 0:4], start=True, stop=False), reads=[Bvb[i], BpT[i]], writes=[Bps[6]])
            P.op("pe", lambda e: e.matmul(ps[:, 7, 0:4], lhsT=ones[:, :], rhs=pT[i][:, 0:4], start=True, stop=False), reads=[Bones, BpT[i]], writes=[Bps[7]])
            kc = g * 1024 + kh * 128
            P.op("pe", lambda e: e.transpose(out=ps[:, tb_, 8:9], in_=kvt[0][0:1, kc:kc + 128], identity=ident[0:1, 0:1]), reads=[Bkvt[0], Bid], writes=[Bps[tb_]])
            P.op("act", lambda e: e.activation(out=knew[:, 0:1], in_=ps[:, tb_, 8:9], func=AF.Copy), reads=[Bps[tb_]], writes=[Bknew])
            P.op("dve", lambda e: e.tensor_copy(out=vnew[0:1, 0:128], in_=kvt[0][0:1, kc + 512:kc + 640]), reads=[Bkvt[0]], writes=[Bvnew])
            P.op("pe", lambda e: e.matmul(ps[0:1, tb_, 16:20].rearrange("p (s q) -> p s q", s=4), lhsT=knew[:, 0:1], rhs=qap, start=True, stop=True),
                 reads=[Bknew, Bq4], writes=[Bps[tb_]])
            P.op("act", lambda e: e.activation(out=misc[0:1, 0:4], in_=ps[0:1, tb_, 16:20], func=AF.Exp, scale=SCALE), reads=[Bps[tb_]], writes=[Bmisc])
            en = (g * 4 + kh) * 4
            P.op("dve", lambda e: e.tensor_tensor(out=pnew[0:1, 0:4], in0=misc[0:1, 0:4], in1=enew[0:1, en:en + 4], op=ALU.mult), reads=[Bmisc, Benew], writes=[Bpnew])
            P.op("pe", lambda e: e.matmul(ps[:, 6, 0:4], lhsT=vnew[0:1, 0:128], rhs=pnew[0:1, 0:4], start=False, stop=True), reads=[Bvnew, Bpnew], writes=[Bps[6]])
            P.op("pe", lambda e: e.matmul(ps[:, 7, 0:4], lhsT=ones[0:1, :], rhs=pnew[0:1, 0:4], start=False, stop=True), reads=[Bones, Bpnew], writes=[Bps[7]])
            for (acc, Bacc, bank) in ((num, Bnum, 6), (den, Bden, 7)):
                src2 = ps[:, bank, 0:4].rearrange("p (s q) -> p s q", s=4)
                if g == 0:
                    P.op("act", lambda e, acc=acc, src2=src2: e.activation(out=acc[:, :, 0:1], in_=src2, func=AF.Copy), reads=[Bps[bank]], writes=[Bacc])
                else:
                    P.op("dve", lambda e, acc=acc, src2=src2: e.tensor_tensor(out=acc[:, :, 0:1], in0=acc[:, :, 0:1], in1=src2, op=ALU.add),
                         reads=[Bps[bank], Bacc], writes=[Bacc])

        def b_layer(l, Tn, b, sample):
            norm_pre(V_BPRE + 32 * l, Tn, ssq=False, rstd=(l == 1))
            cols = []
            for kh in range(4):
                for g in range(3):
                    cols += [g * 2048 + (4 * kh + s) * 128 for s in range(4)]
                cols += [6144 + (4 * kh + s) * 128 for s in range(4)]

            def cons(idx, pb):
                kh, j = idx // 16, idx % 16
                if j < 12:
                    g, s = j // 4, j % 4
                    P.op("act", lambda e: e.activation(out=q4[:, s, 0:Tn], in_=ps[:, pb, 0:Tn], func=AF.Copy), reads=[Bps[pb]], writes=[Bq4])
                    if s == 3:
                        if sample:
                            attention_sample(g, kh)
                        else:
                            attention_prompt(b, g, kh)
                else:
                    s = j - 12
                    if s == 0:
                        P.op("dve", lambda e: e.reciprocal(out=den[:, :, 0:Tn], in_=den[:, :, 0:Tn]), reads=[Bden], writes=[Bden])
                        P.op("dve", lambda e: e.tensor_tensor(out=num[:, :, 0:Tn], in0=num[:, :, 0:Tn], in1=den[:, :, 0:Tn], op=ALU.mult),
                             reads=[Bnum, Bden], writes=[Bnum])
                    P.op("act", lambda e: e.activation(out=sg[:, 0:Tn], in_=ps[:, pb, 0:Tn], func=AF.Silu), reads=[Bps[pb]], writes=[Bsg])
                    P.op("dve", lambda e: e.tensor_tensor(out=hg[:, 4 * kh + s, 0:Tn], in0=num[:, s, 0:Tn], in1=sg[:, 0:Tn], op=ALU.mult),
                         reads=[Bnum, Bsg], writes=[Bh[4 * kh + s]])

            lin(b_w_in[l], 32, cols, ub, Bu, Tn, cons, ("bin", l))
            lin(b_w_out[l], 16, [m * 128 for m in range(32)], hg, Bh, Tn, out_consumer(Tn), ("bout", l))
            post(V_BPOST + 32 * l, Tn)

        def load_x(src_rows, M, col0):
            P.dma("sp", xiosem, lambda e: e.dma_start(out=xio[0:M, :], in_=src_rows), writes=[Bxio])
            for k in range(32):
                tb_ = 4 + k % 2
                P.op("pe", lambda e, k=k, tb_=tb_: e.transpose(out=ps[:, tb_, 0:M], in_=xio[0:M, k * 128:(k + 1) * 128], identity=ident[0:M, 0:M]),
                     reads=[Bxio, Bid], writes=[Bps[tb_]])
                P.op("act", lambda e, k=k, tb_=tb_: e.activation(out=xT[:, k, col0:col0 + M], in_=ps[:, tb_, 0:M], func=AF.Copy), reads=[Bps[tb_]], writes=[Bx[k]])

        def store_x(dst_rows, M, col0):
            for k in range(32):
                tb_ = 4 + k % 2
                P.op("pe", lambda e, k=k, tb_=tb_: e.transpose(out=ps[0:M, tb_, 0:128], in_=xT[:, k, col0:col0 + M], identity=ident[:, :]),
                     reads=[Bx[k], Bid], writes=[Bps[tb_]])
                P.op("act", lambda e, k=k, tb_=tb_: e.activation(out=xio[0:M, k * 128:(k + 1) * 128], in_=ps[0:M, tb_, 0:128], func=AF.Copy),
                     reads=[Bps[tb_]], writes=[Bxio])
            P.dma("sp", xiosem, lambda e: e.dma_start(out=dst_rows, in_=xio[0:M, :]), reads=[Bxio], writes=[Bout])

        def store_states(conv_o, h_o):
            for l in range(2):
                P.op("pe", lambda e, l=l: e.transpose(out=ps[:, 4, 0:128], in_=stt[l][:, :, :].rearrange("p j c -> p (j c)"), identity=ident[:, :]),
                     reads=[Bst[l], Bid], writes=[Bps[4]])
                P.op("act", lambda e: e.activation(out=stT[:, :], in_=ps[:, 4, 0:128], func=AF.Copy), reads=[Bps[4]], writes=[BstT])
                for j in range(3):
                    P.dma("sp", outsem, lambda e, l=l, j=j: e.dma_start(out=conv_o[l, j].rearrange("(c p) -> c p", p=128), in_=stT[j * 32:(j + 1) * 32, :]),
                          reads=[BstT], writes=[Bout])
                P.dma("sp", outsem, lambda e, l=l: e.dma_start(out=h_o[l].rearrange("(c p) -> c p", p=128), in_=stT[96:128, :]), reads=[BstT], writes=[Bout])

        def load_states():
            for l in range(2):
                for j in range(3):
                    P.dma("sp", setsem[4], lambda e, l=l, j=j: e.dma_start(out=stT[j * 32:(j + 1) * 32, :], in_=sconv[l, j].rearrange("(c p) -> c p", p=128)), writes=[BstT])
                P.dma("sp", setsem[4], lambda e, l=l: e.dma_start(out=stT[96:128, :], in_=sh[l].rearrange("(c p) -> c p", p=128)), writes=[BstT])
                P.op("pe", lambda e, l=l: e.transpose(out=ps[:, 4, 0:128], in_=stT[:, :], identity=ident[:, :]), reads=[BstT, Bid], writes=[Bps[4]])
                P.op("act", lambda e, l=l: e.activation(out=stt[l][:, :, :].rearrange("p j c -> p (j c)"), in_=ps[:, 4, 0:128], func=AF.Copy),
                     reads=[Bps[4]], writes=[Bst[l]])

        for b in range(NB):
            for tc in range(T // 128):
                load_x(xp[b * T + tc * 128:b * T + (tc + 1) * 128, :], 128, tc * 128)
            a_layer(0, T)
            a_layer(1, T)
            kv_phase(b * T, T, False)
            if b == NB // 2 - 1:
                for l in range(2):
                    P.op("dve", lambda e, l=l: e.tensor_scalar(out=stt[l][:, :, :].rearrange("p j c -> p (j c)"), in0=stt[l][:, :, :].rearrange("p j c -> p (j c)"),
                                                               scalar1=flg[:, 0:1], scalar2=None, op0=ALU.mult), reads=[Bst[l], Bflg], writes=[Bst[l]])
            if b >= NB // 2:
                b_layer(0, T, b, False)
                b_layer(1, T, b, False)
                bo = b - NB // 2
                for tc in range(T // 128):
                    store_x(yp[bo * T + tc * 128:bo * T + (tc + 1) * 128, :], 128, tc * 128)
        store_states(pconv, ph)
        load_states()
        load_x(xs[0:1, :], 1, 0)
        a_layer(0, 1)
        a_layer(1, 1)
        kv_phase(0, 1, True)
        for g in range(3):
            lbg = 128 * DIL[g]
            P.dma("sp", d2dsem, lambda e, g=g, lbg=lbg: e.dma_start(out=skv_o[g][0:lbg - 1, :], in_=skv[g][1:lbg, :]), writes=[Bout])
            P.dma("sp", d2dsem, lambda e, g=g, lbg=lbg: e.dma_start(out=skv_o[g][lbg - 1:lbg, :], in_=kvt[0][0:1, g * 1024:(g + 1) * 1024]),
                  reads=[Bkvt[0]], writes=[Bout])
        b_layer(0, 1, 0, True)
        b_layer(1, 1, 0, True)
        store_x(ys[0:1, :], 1, 0)
        store_states(sconv_o, sh_o)
        fin = [(sm, P.dcnt[id(sm)]) for sm in (kvosem, outsem, xiosem, d2dsem) if P.dcnt[id(sm)] > 0]
        P.q["sp"].append((None, fin, None))
        import os
        if os.environ.get("KDEBUG"):
            print("SBUF remaining bytes/partition:", nc.sbuf_bytes_remaining)
        with nc.Block() as block:
            P.replay(block)
    return nc


def _rel_bucket(dist):
    dist = np.asarray(dist)
    d = np.maximum(dist, 1).astype(np.float32)
    large = 16 + (np.log(d / 16) / np.log(2048 / 16) * (32 - 16)).astype(np.int32)
    large = np.minimum(large, 31)
    return np.where(dist < 16, dist, large).astype(np.int32)


def _etiles(rel_bias):
    ein = np.zeros((128, NE), np.float32)
    em = np.zeros((128, NE), np.float32)
    enew = np.zeros((1, 48), np.float32)
    p = np.arange(128)[:, None]

    def fill(off, g, kh, diff, valid):
        nq = diff.shape[1]
        bk = _rel_bucket(np.clip(diff, 0, 128) * DIL[g])
        for s in range(4):
            col = g * 16 + 4 * kh + s
            ein[:, off + s * nq:off + (s + 1) * nq] = rel_bias[bk, col]
            em[:, off + s * nq:off + (s + 1) * nq] = valid.astype(np.float32)

    for kh in range(4):
        q = np.arange(128)[None, :]
        fill(e0_off(0, kh), 0, kh, q - p, (q - p) >= 0)
        fill(e0_off(1, kh), 0, kh, 128 + q - p, (128 + q - p) <= 128)
        q = np.arange(64)[None, :]
        for half in range(2):
            d = 64 * half + q - p
            fill(e1_off(half, 0, kh), 1, kh, d, d >= 0)
            d = 128 + 64 * half + q - p
            fill(e1_off(half, 1, kh), 1, kh, d, d <= 128)
        q = np.arange(16)[None, :]
        for b in range(NB):
            d = 16 * b + q - p
            fill(e2_off(b, kh), 2, kh, d, d >= 0)
        for g in range(3):
            d = 128 - p + np.zeros((1, 1), np.int64)
            fill(es_off(g, kh), g, kh, d, d >= 0)
            for s in range(4):
                enew[0, (g * 4 + kh) * 4 + s] = rel_bias[0, g * 16 + 4 * kh + s]
    return ein, em, enew


def _colvec(v):
    v = np.asarray(v, np.float32).reshape(-1, 32, 128)
    return np.ascontiguousarray(v.transpose(2, 0, 1).reshape(128, -1))


_NC_CACHE = {}


def kernel(x_prompt, x_sample, state_conv, state_h, state_kv_w128, state_kv_w512, state_kv_w2048,
           a_pre_g, a_w_in, a_conv_w, a_conv_b, a_w_gate_a, a_b_gate_a, a_w_gate_x, a_b_gate_x,
           a_lambda, a_w_out, a_post_g, kv_norm_g, w_kv, rel_bias, b_pre_g, b_w_in, b_w_out, b_post_g):
    f = lambda a: np.ascontiguousarray(np.asarray(a, np.float32))
    vecs = np.concatenate([
        _colvec(a_pre_g), _colvec(np.asarray(a_conv_w)), _colvec(a_conv_b),
        _colvec(np.asarray(a_b_gate_a).reshape(2, 4096)), _colvec(np.asarray(a_b_gate_x).reshape(2, 4096)),
        _colvec(a_lambda), _colvec(a_post_g), _colvec(kv_norm_g), _colvec(b_pre_g), _colvec(b_post_g)], axis=1)
    assert vecs.shape == (128, NV)
    ein, em, enew = _etiles(np.asarray(rel_bias, np.float32))
    def tile_w(w):
        w = np.asarray(w, np.float32)
        lead = w.shape[:-2]
        K, M = w.shape[-2:]
        w = w.reshape(lead + (K // 128, 128, M // 128, 128))
        nl = len(lead)
        perm = tuple(range(nl)) + (nl + 2, nl + 1, nl + 0, nl + 3)
        return np.ascontiguousarray(w.transpose(perm)).reshape(lead + (M // 128, 128, K))

    shared = dict(a_w_in=tile_w(a_w_in), a_w_out=tile_w(a_w_out), a_wga=f(a_w_gate_a), a_wgx=f(a_w_gate_x), w_kv=tile_w(w_kv),
                  b_w_in=tile_w(b_w_in), b_w_out=tile_w(b_w_out), vecs=f(vecs), ein=ein, em=em, enew=enew,
                  ident=np.eye(128, dtype=np.float32))
    xp = f(x_prompt); xs = f(x_sample); sc = f(state_conv); shh = f(state_h)
    k0 = f(state_kv_w128).reshape(8, 128, 1024); k1 = f(state_kv_w512).reshape(8, 512, 1024); k2 = f(state_kv_w2048).reshape(8, 2048, 1024)
    in_maps = []
    for c in range(8):
        m = dict(shared)
        sq_, hf = c // 2, c % 2
        xpc = xp[sq_] if hf == 1 else np.concatenate([np.zeros((1024, 4096), np.float32), xp[sq_][:1024]], axis=0)
        fl = np.zeros((128, 2), np.float32); fl[:, 0] = hf; fl[:, 1] = 1.0; fl[:64, 1] = hf
        m.update(xp=xpc, flg=fl, xs=xs[c], sconv=np.ascontiguousarray(sc[:, c]), sh=np.ascontiguousarray(shh[:, c]),
                 skv0=k0[c], skv1=k1[c], skv2=k2[c])
        in_maps.append(m)
    if "nc" not in _NC_CACHE:
        _NC_CACHE["nc"] = build()
    res = run_bass_kernel_spmd(_NC_CACHE["nc"], in_maps, core_ids=list(range(8)))
    R = res.results
    y_prompt = np.stack([np.concatenate([R[2 * q]["yp"], R[2 * q + 1]["yp"]], axis=0) for q in range(4)])
    y_sample = np.stack([R[c]["ys"] for c in range(8)])
    p_conv = np.stack([R[2 * q + 1]["pconv"] for q in range(4)], axis=1)
    p_h = np.stack([R[2 * q + 1]["ph"] for q in range(4)], axis=1)
    pkv = np.stack([R[2 * q + 1]["pkv"] for q in range(4)]).reshape(4, 2048, 3, 2, 4, 128)
    p_kv128 = np.ascontiguousarray(pkv[:, 2048 - 128:, 0])
    p_kv512 = np.ascontiguousarray(pkv[:, 2048 - 512:, 1])
    p_kv2048 = np.ascontiguousarray(pkv[:, :, 2])
    s_conv = np.stack([R[c]["sconv_o"] for c in range(8)], axis=1)
    s_h = np.stack([R[c]["sh_o"] for c in range(8)], axis=1)
    s_kv128 = np.stack([R[c]["skv_o0"] for c in range(8)]).reshape(8, 128, 2, 4, 128)
    s_kv512 = np.stack([R[c]["skv_o1"] for c in range(8)]).reshape(8, 512, 2, 4, 128)
    s_kv2048 = np.stack([R[c]["skv_o2"] for c in range(8)]).reshape(8, 2048, 2, 4, 128)
    f32 = lambda a: np.asarray(a, np.float32)
    return tuple(f32(a) for a in (y_prompt, y_sample, p_conv, p_h, p_kv128, p_kv512, p_kv2048, s_conv, s_h, s_kv128, s_kv512, s_kv2048))
```

```python
import contextlib
import numpy as np
import concourse.bass as bass
import concourse.mybir as mybir
from concourse.bass_utils import run_bass_kernel_spmd

F32 = mybir.dt.float32
BF16 = mybir.dt.bfloat16
AF = mybir.ActivationFunctionType
ALU = mybir.AluOpType

SAME_ENGINE_SYNC = True
T = 256
NB = 2048 // T
EPS = 1e-6
SCALE = 128 ** -0.5
DIL = (1, 4, 16)


class Buf:
    __slots__ = ("name", "w", "r")

    def __init__(self, name=""):
        self.name = name
        self.w = None
        self.r = {}


class Prog:
    ENGS = ("pe", "act", "dve", "pool", "sp")

    def __init__(self, nc, stack):
        self.nc = nc
        self.stack = stack
        self.q = {e: [] for e in self.ENGS}
        self.cnt = {e: 0 for e in self.ENGS}
        self.seen = {e: {} for e in self.ENGS}
        self.esem = {e: stack.enter_context(nc.semaphore("es_" + e)) for e in self.ENGS}
        self.dcnt = {}

    def dma_sem(self, name):
        s = self.stack.enter_context(self.nc.semaphore(name))
        self.dcnt[id(s)] = 0
        return s

    def sb(self, name, shape, dt):
        return self.stack.enter_context(self.nc.sbuf_tensor("sb_" + name, list(shape), dt))

    def ps(self, name, shape, dt=F32):
        return self.stack.enter_context(self.nc.psum_tensor("psum_" + name, list(shape), dt))

    def _deps(self, eng, reads, writes):
        deps = []
        for b in reads:
            if b.w is not None:
                deps.append(b.w)
        for b in writes:
            if b.w is not None:
                deps.append(b.w)
            deps.extend(b.r.values())
        waits = []
        seen = self.seen[eng]
        for (sem, val, src) in deps:
            if src == eng and (eng == "pe" or not SAME_ENGINE_SYNC):
                continue
            k = id(sem)
            if seen.get(k, 0) >= val:
                continue
            seen[k] = val
            waits.append((sem, val))
        return waits

    def _commit(self, tok, reads, writes):
        for b in writes:
            b.w = tok
            b.r = {}
        for b in reads:
            if b not in writes:
                b.r[id(tok[0])] = tok

    def op(self, eng, fn, reads=(), writes=(), inc=True):
        waits = self._deps(eng, reads, writes)
        if inc:
            self.cnt[eng] += 1
            tok = (self.esem[eng], self.cnt[eng], eng)
        else:
            tok = (self.esem[eng], self.cnt[eng] + 1, eng)
        self.q[eng].append((fn, waits, (self.esem[eng], 1) if inc else None))
        self._commit(tok, reads, writes)
        return tok

    def dma(self, eng, sem, fn, reads=(), writes=()):
        waits = self._deps(eng, reads, writes)
        self.dcnt[id(sem)] += 16
        tok = (sem, self.dcnt[id(sem)], "dma")
        self.q[eng].append((fn, waits, (sem, 16)))
        self._commit(tok, reads, writes)
        return tok

    def wait_all(self, eng, bufs):
        waits = self._deps(eng, (), bufs)
        self.q[eng].append((None, waits, None))

    def replay(self, block):
        names = {"pe": "tensor", "act": "scalar", "dve": "vector", "pool": "gpsimd", "sp": "sync"}

        def run(e, engobj):
            for fn, waits, inc in self.q[e]:
                for sem, val in waits:
                    engobj.wait_ge(sem, val)
                if fn is None:
                    continue
                ins = fn(engobj)
                if inc is not None:
                    ins.then_inc(inc[0], inc[1])

        for e in self.ENGS:
            if not self.q[e]:
                continue

            def mk(e):
                def body(engobj):
                    run(e, engobj)
                return body
            getattr(block, names[e])(mk(e))


V_APRE, V_CW, V_CB, V_BGA, V_BGX, V_LAM, V_APOST, V_KVG, V_BPRE, V_BPOST = 0, 64, 320, 384, 448, 512, 576, 640, 672, 736
NV = 800
E0 = 0
E1 = 4096
E2 = 8192
ES = 10240
NE = 10240 + 48


def e0_off(pc, kh): return E0 + (pc * 4 + kh) * 512
def e1_off(half, pc, kh): return E1 + ((half * 2 + pc) * 4 + kh) * 256
def e2_off(b, kh): return E2 + (b * 4 + kh) * 64
def es_off(g, kh): return ES + (g * 4 + kh) * 4


def build():
    nc = bass.Bass("TRN2", target_bir_lowering=False)

    def D(name, shape, dt=F32, kind="ExternalInput"):
        return nc.dram_tensor(name, list(shape), dt, kind=kind).ap()

    xp = D("xp", [2048, 4096]); xs = D("xs", [1, 4096])
    sconv = D("sconv", [2, 3, 4096]); sh = D("sh", [2, 4096])
    skv = [D("skv0", [128, 1024]), D("skv1", [512, 1024]), D("skv2", [2048, 1024])]
    a_w_in = D("a_w_in", [2, 64, 128, 4096]); a_w_out = D("a_w_out", [2, 32, 128, 4096])
    a_wga = D("a_wga", [2, 16, 256, 256]); a_wgx = D("a_wgx", [2, 16, 256, 256])
    w_kv = D("w_kv", [24, 128, 4096])
    b_w_in = D("b_w_in", [2, 64, 128, 4096]); b_w_out = D("b_w_out", [2, 32, 128, 2048])
    vecs_d = D("vecs", [128, NV]); ein_d = D("ein", [128, NE]); em_d = D("em", [128, NE])
    enew_d = D("enew", [1, 48]); ident_d = D("ident", [128, 128]); flg_d = D("flg", [128, 2])
    O = lambda n, s: D(n, s, kind="ExternalOutput")
    yp = O("yp", [1024, 4096]); ys = O("ys", [1, 4096])
    pconv = O("pconv", [2, 3, 4096]); ph = O("ph", [2, 4096]); pkv = O("pkv", [2048, 3072])
    sconv_o = O("sconv_o", [2, 3, 4096]); sh_o = O("sh_o", [2, 4096])
    skv_o = [O("skv_o0", [128, 1024]), O("skv_o1", [512, 1024]), O("skv_o2", [2048, 1024])]
    kvs = nc.dram_tensor("kvs", [2048 + 16, 3072], F32).ap()
    wsc = {}
    for l in range(2):
        wsc[("ain", l)] = nc.dram_tensor("wsc_ain%d" % l, [64, 128, 4096], BF16).ap()
        wsc[("aout", l)] = nc.dram_tensor("wsc_aout%d" % l, [32, 128, 4096], BF16).ap()
        wsc[("bin", l)] = nc.dram_tensor("wsc_bin%d" % l, [64, 128, 4096], BF16).ap()
        wsc[("bout", l)] = nc.dram_tensor("wsc_bout%d" % l, [32, 128, 2048], BF16).ap()
    wsc[("kv", 0)] = nc.dram_tensor("wsc_kv", [24, 128, 4096], BF16).ap()
    wdone = {}

    with contextlib.ExitStack() as st:
        P = Prog(nc, st)
        xT = P.sb("xT", [128, 32, T], F32); Bx = [Buf() for _ in range(32)]
        ub = P.sb("ub", [128, 32, T], BF16); Bu = [Buf() for _ in range(32)]
        hg = P.sb("hg", [128, 32, T], BF16); Bh = [Buf() for _ in range(32)]
        NW = 3
        wbf = [P.sb("wbf%d" % i, [128, 32, 128], BF16) for i in range(NW)]; Bw = [Buf() for _ in range(NW)]
        wsem = [P.dma_sem("wsem%d" % i) for i in range(NW)]
        ssem = [P.dma_sem("ssem%d" % i) for i in range(NW)]
        wg = [P.sb("wg%d" % i, [128, 2, 2, 256], BF16) for i in range(2)]; Bwg = [Buf(), Buf()]; wgsem = [P.dma_sem("wgsem%d" % i) for i in range(2)]
        vecs = P.sb("vecs", [128, NV], F32); Bv = Buf()
        c1 = P.sb("c1", [128, 64], F32); Bc1 = Buf()
        Eall = P.sb("Eall", [128, NE], BF16); BE = Buf()
        enew = P.sb("enew", [1, 48], F32); Benew = Buf()
        ident = P.sb("ident", [128, 128], F32); Bid = Buf()
        flg = P.sb("flg", [128, 2], F32); Bflg = Buf(); flgsem = P.dma_sem("flgsem")
        ones = P.sb("ones", [128, 128], BF16); Bones = Buf()
        rs = P.sb("rs", [128, T], F32); Brs = Buf()
        sqb = [P.sb("sqb%d" % i, [128, T], BF16) for i in range(2)]; Bsq = [Buf(), Buf()]
        stt = [P.sb("st%d" % l, [128, 4, 32], F32) for l in range(2)]; Bst = [Buf(), Buf()]
        stT = P.sb("stT", [128, 128], F32); BstT = Buf()
        xc = [P.sb("xc%d" % i, [128, 3 + T], F32) for i in range(2)]; Bxc = [Buf(), Buf()]
        xv = [[P.sb("xv%d%d" % (hp, i), [128, T], F32) for i in range(2)] for hp in range(2)]; Bxv = [[Buf(), Buf()], [Buf(), Buf()]]
        xvb = [[P.sb("xvb%d%d" % (hp, i), [128, T], BF16) for i in range(2)] for hp in range(2)]; Bxvb = [[Buf(), Buf()], [Buf(), Buf()]]
        rg = [[P.sb("rg%d%d" % (a, j), [128, T], F32) for j in range(2)] for a in range(2)]
        Brg = [[Buf(), Buf()], [Buf(), Buf()]]
        ta = P.sb("ta", [128, T], F32); Bta = Buf()
        tb = P.sb("tb", [128, T], F32); Btb = Buf()
        tcc = P.sb("tcc", [128, T], F32); Btc = Buf()
        hh = [P.sb("hh%d" % i, [128, T], F32) for i in range(2)]; Bhh = [Buf(), Buf()]
        sg2 = [[P.sb("sg2_%d%d" % (hp, i), [128, T], F32) for i in range(2)] for hp in range(2)]; Bsg2 = [[Buf(), Buf()], [Buf(), Buf()]]
        tmpx = P.sb("tmpx", [128, T], F32); Btmpx = Buf()
        xio = P.sb("xio", [128, 4096], F32); Bxio = Buf(); xiosem = P.dma_sem("xiosem")
        kvt = [P.sb("kvt%d" % i, [128, 3072], F32) for i in range(2)]; Bkvt = [Buf(), Buf()]
        kvosem = P.dma_sem("kvosem")
        q4 = P.sb("q4", [128, 4, T], BF16); Bq4 = Buf()
        num = P.sb("num", [128, 4, T], F32); Bnum = Buf()
        den = P.sb("den", [128, 4, T], F32); Bden = Buf()
        kvin = [P.sb("kvin%d" % i, [128, 2, 128], F32) for i in range(2)]; Bkvin = [Buf(), Buf()]
        kvinsem = [P.dma_sem("kvinsem%d" % i) for i in range(2)]
        kTb = [P.sb("kTb%d" % i, [128, 128], BF16) for i in range(2)]; BkT = [Buf(), Buf()]
        vbb = [P.sb("vbb%d" % i, [128, 128], BF16) for i in range(2)]; Bvb = [Buf(), Buf()]
        pT = [P.sb("pT%d" % i, [128, 512], BF16) for i in range(2)]; BpT = [Buf(), Buf()]
        esb = P.sb("esb", [128, 256], F32); Besb = Buf()
        emb = P.sb("emb", [128, 256], F32); Bemb = Buf()
        misc = P.sb("misc", [128, 64], F32); Bmisc = Buf()
        knew = P.sb("knew", [128, 16], BF16); Bknew = Buf()
        vnew = P.sb("vnew", [1, 128], BF16); Bvnew = Buf()
        pnew = P.sb("pnew", [1, 16], BF16); Bpnew = Buf()
        ps = P.ps("ps", [128, 8, 512], F32); Bps = [Buf() for _ in range(8)]
        setsem = [P.dma_sem("setsem%d" % i) for i in range(6)]
        outsem = P.dma_sem("outsem"); Bout = Buf()
        d2dsem = P.dma_sem("d2dsem"); Bd2d = Buf()

        P.dma("sp", setsem[0], lambda e: e.dma_start(out=vecs[:, :], in_=vecs_d), writes=[Bv])
        P.dma("sp", setsem[1], lambda e: e.dma_start(out=enew[:, :], in_=enew_d), writes=[Benew])
        P.dma("sp", flgsem, lambda e: e.dma_start(out=flg[:, :], in_=flg_d), writes=[Bflg])
        P.dma("sp", setsem[5], lambda e: e.dma_start(out=ident[:, :], in_=ident_d), writes=[Bid])
        P.op("pool", lambda e: e.memset(ones[:, :], 1.0), writes=[Bones])
        for i in range(2):
            P.op("pool", lambda e, i=i: e.memset(kvin[i][:, :, :], 0.0), writes=[Bkvin[i]])
            P.op("pool", lambda e, i=i: e.memset(kvt[i][:, :], 0.0), writes=[Bkvt[i]])
        for l in range(2):
            P.op("pool", lambda e, l=l: e.memset(stt[l][:, :, :], 0.0), writes=[Bst[l]])
        P.op("act", lambda e: e.activation(out=c1[:, :], in_=vecs[:, V_LAM:V_LAM + 64], func=AF.Exp, scale=-1.0), reads=[Bv], writes=[Bc1])
        P.op("act", lambda e: e.activation(out=c1[:, :], in_=c1[:, :], func=AF.Ln, bias=1.0, scale=1.0), reads=[Bc1], writes=[Bc1])
        P.op("dve", lambda e: e.tensor_scalar(out=c1[:, :], in0=c1[:, :], scalar1=-8.0, scalar2=None, op0=ALU.mult), reads=[Bc1], writes=[Bc1])
        for c0 in range(0, NE, 256):
            cw = min(256, NE - c0)
            P.dma("sp", setsem[2], lambda e, c0=c0, cw=cw: e.dma_start(out=esb[:, 0:cw], in_=ein_d[:, c0:c0 + cw]), writes=[Besb])
            P.dma("sp", setsem[3], lambda e, c0=c0, cw=cw: e.dma_start(out=emb[:, 0:cw], in_=em_d[:, c0:c0 + cw]), writes=[Bemb])
            P.op("act", lambda e, cw=cw: e.activation(out=esb[:, 0:cw], in_=esb[:, 0:cw], func=AF.Exp), reads=[Besb], writes=[Besb])
            P.op("dve", lambda e, c0=c0, cw=cw: e.tensor_tensor(out=Eall[:, c0:c0 + cw], in0=esb[:, 0:cw], in1=emb[:, 0:cw], op=ALU.mult),
                 reads=[Besb, Bemb], writes=[BE])
        P.op("act", lambda e: e.activation(out=enew[:, :], in_=enew[:, :], func=AF.Exp), reads=[Benew], writes=[Benew])

        wctr = [0]
        pctr = [0]

        def wload(Wsl, nk, col0, key):
            s_ = wctr[0] % NW; wctr[0] += 1
            strip = col0 // 128
            dst = wbf[s_][:, 0:nk, :].rearrange("p k m -> p (k m)")
            kk = (key, strip)
            if kk not in wdone:
                P.dma("pool", wsem[s_], lambda e: e.dma_start(out=dst, in_=Wsl[strip], max_dma_last_dim=8192), writes=[Bw[s_]])
                bb = Buf(); wdone[kk] = bb
                P.dma("sp", ssem[s_], lambda e: e.dma_start(out=wsc[key][strip], in_=dst), reads=[Bw[s_]], writes=[bb])
            else:
                P.dma("pool", wsem[s_], lambda e: e.dma_start(out=dst, in_=wsc[key][strip]), reads=[wdone[kk]], writes=[Bw[s_]])
            return s_

        def lin(Wsl, nk, cols, src, Bsrc, Tn, consumer, key):
            slots = {}
            for i0 in range(min(NW - 1, len(cols))):
                slots[i0] = wload(Wsl, nk, cols[i0], key)
            for idx, col0 in enumerate(cols):
                s = slots.pop(idx)
                pb = pctr[0] % 3; pctr[0] += 1
                for k in range(nk):
                    P.op("pe", lambda e, s=s, k=k, pb=pb: e.matmul(ps[:, pb, 0:Tn], lhsT=wbf[s][:, k, :], rhs=src[:, k, 0:Tn],
                                                                  start=(k == 0), stop=(k == nk - 1)),
                         reads=[Bw[s], Bsrc[k]], writes=[Bps[pb]], inc=(k == nk - 1))
                if idx + NW - 1 < len(cols):
                    slots[idx + NW - 1] = wload(Wsl, nk, cols[idx + NW - 1], key)
                consumer(idx, pb)

        def rstd_from_ps3(Tn):
            P.op("act", lambda e: e.activation(out=rs[:, 0:Tn], in_=ps[:, 3, 0:Tn], func=AF.Sqrt, bias=EPS, scale=1.0 / 4096), reads=[Bps[3]], writes=[Brs])
            P.op("dve", lambda e: e.reciprocal(out=rs[:, 0:Tn], in_=rs[:, 0:Tn]), reads=[Brs], writes=[Brs])

        def norm_pre(gcol, Tn, ssq=True, rstd=True):
            for k in (range(32) if ssq else ()):
                i = k % 2
                P.op("dve", lambda e, k=k, i=i: e.tensor_tensor(out=sqb[i][:, 0:Tn], in0=xT[:, k, 0:Tn], in1=xT[:, k, 0:Tn], op=ALU.mult),
                     reads=[Bx[k]], writes=[Bsq[i]])
                P.op("pe", lambda e, k=k, i=i: e.matmul(ps[:, 3, 0:Tn], lhsT=ones[:, :], rhs=sqb[i][:, 0:Tn], start=(k == 0), stop=(k == 31)),
                     reads=[Bsq[i], Bones], writes=[Bps[3]], inc=True)
            if rstd:
                rstd_from_ps3(Tn)
            for k in range(32):
                P.op("dve", lambda e, k=k: e.scalar_tensor_tensor(out=ub[:, k, 0:Tn], in0=xT[:, k, 0:Tn], scalar=vecs[:, gcol + k:gcol + k + 1],
                                                                  in1=rs[:, 0:Tn], op0=ALU.mult, op1=ALU.mult),
                     reads=[Bx[k], Brs, Bv], writes=[Bu[k]])

        def out_consumer(Tn):
            pend = []

            def flush():
                for (m, i) in pend:
                    P.op("pe", lambda e, m=m, i=i: e.matmul(ps[:, 3, 0:Tn], lhsT=ones[:, :], rhs=sqb[i][:, 0:Tn], start=(m == 0), stop=(m == 31)),
                         reads=[Bsq[i], Bones], writes=[Bps[3]], inc=True)
                del pend[:]

            def cons(m, pb):
                i = m % 2
                P.op("act", lambda e, m=m, pb=pb: e.activation(out=ub[:, m, 0:Tn], in_=ps[:, pb, 0:Tn], func=AF.Copy), reads=[Bps[pb]], writes=[Bu[m]])
                flush()
                P.op("act", lambda e, i=i, pb=pb: e.activation(out=sqb[i][:, 0:Tn], in_=ps[:, pb, 0:Tn], func=AF.Square), reads=[Bps[pb]], writes=[Bsq[i]])
                pend.append((m, i))
                if m == 31:
                    flush()
            return cons

        def post(gcol, Tn):
            rstd_from_ps3(Tn)
            for k in range(32):
                P.op("dve", lambda e, k=k: e.scalar_tensor_tensor(out=tmpx[:, 0:Tn], in0=ub[:, k, 0:Tn], scalar=vecs[:, gcol + k:gcol + k + 1],
                                                                  in1=rs[:, 0:Tn], op0=ALU.mult, op1=ALU.mult),
                     reads=[Bu[k], Brs, Bv], writes=[Btmpx])
                P.op("dve", lambda e, k=k: e.tensor_tensor(out=xT[:, k, 0:Tn], in0=xT[:, k, 0:Tn], in1=tmpx[:, 0:Tn], op=ALU.add),
                     reads=[Btmpx, Bx[k]], writes=[Bx[k]])
                i = k % 2
                P.op("act", lambda e, k=k, i=i: e.activation(out=sqb[i][:, 0:Tn], in_=xT[:, k, 0:Tn], func=AF.Square), reads=[Bx[k]], writes=[Bsq[i]])
                P.op("pe", lambda e, k=k, i=i: e.matmul(ps[:, 3, 0:Tn], lhsT=ones[:, :], rhs=sqb[i][:, 0:Tn], start=(k == 0), stop=(k == 31)),
                     reads=[Bsq[i], Bones], writes=[Bps[3]], inc=True)

        def a_layer(l, Tn):
            norm_pre(V_APRE + 32 * l, Tn, ssq=(l == 0))
            S = stt[l]; BS = Bst[l]
            cols = []
            for h in range(16):
                cols += [(2 * h) * 128, (2 * h + 1) * 128, 4096 + (2 * h) * 128, 4096 + (2 * h + 1) * 128]

            def chain(h):
                hp = h % 2
                for a in range(2):
                    bcol = (V_BGA if a == 0 else V_BGX) + 32 * l
                    for jo in range(2):
                        gb = 4 + (a * 2 + jo) % 2
                        for ic in range(2):
                            P.op("pe", lambda e, a=a, jo=jo, ic=ic, gb=gb: e.matmul(ps[:, gb, 0:Tn], lhsT=wg[hp][:, a, ic, jo * 128:(jo + 1) * 128],
                                                                                    rhs=xvb[hp][ic][:, 0:Tn], start=(ic == 0), stop=(ic == 1)),
                                 reads=[Bwg[hp], Bxvb[hp][ic]], writes=[Bps[gb]], inc=(ic == 1))
                        cc = 2 * h + jo
                        P.op("act", lambda e, a=a, jo=jo, gb=gb, cc=cc, bcol=bcol: e.activation(
                            out=rg[a][jo][:, 0:Tn], in_=ps[:, gb, 0:Tn], func=AF.Sigmoid, bias=vecs[:, bcol + cc:bcol + cc + 1], scale=1.0),
                            reads=[Bps[gb], Bv], writes=[Brg[a][jo]])
                for jo in range(2):
                    cc = 2 * h + jo
                    P.op("act", lambda e, jo=jo, cc=cc: e.activation(out=ta[:, 0:Tn], in_=rg[0][jo][:, 0:Tn], func=AF.Exp,
                                                                     scale=c1[:, 32 * l + cc:32 * l + cc + 1]),
                         reads=[Brg[0][jo], Bc1], writes=[Bta])
                    P.op("dve", lambda e: e.tensor_tensor(out=tb[:, 0:Tn], in0=ta[:, 0:Tn], in1=ta[:, 0:Tn], op=ALU.mult), reads=[Bta], writes=[Btb])
                    P.op("act", lambda e: e.activation(out=tb[:, 0:Tn], in_=tb[:, 0:Tn], func=AF.Sqrt, bias=1.0, scale=-1.0), reads=[Btb], writes=[Btb])
                    P.op("dve", lambda e, jo=jo: e.tensor_tensor(out=tcc[:, 0:Tn], in0=rg[1][jo][:, 0:Tn], in1=xv[hp][jo][:, 0:Tn], op=ALU.mult),
                         reads=[Brg[1][jo], Bxv[hp][jo]], writes=[Btc])
                    P.op("dve", lambda e: e.tensor_tensor(out=tcc[:, 0:Tn], in0=tcc[:, 0:Tn], in1=tb[:, 0:Tn], op=ALU.mult), reads=[Btc, Btb], writes=[Btc])
                    P.op("dve", lambda e, jo=jo, cc=cc: e.tensor_tensor_scan(out=hh[jo][:, 0:Tn], data0=ta[:, 0:Tn], data1=tcc[:, 0:Tn],
                                                                             initial=S[:, 3, cc:cc + 1], op0=ALU.mult, op1=ALU.add),
                         reads=[Bta, Btc, BS], writes=[Bhh[jo]])
                    P.op("dve", lambda e, jo=jo, cc=cc: e.tensor_copy(out=S[:, 3, cc:cc + 1], in_=hh[jo][:, Tn - 1:Tn]), reads=[Bhh[jo]], writes=[BS])
                for j2 in range(2):
                    c2 = 2 * h + j2
                    P.op("dve", lambda e, j2=j2, c2=c2: e.tensor_tensor(out=hg[:, c2, 0:Tn], in0=hh[j2][:, 0:Tn], in1=sg2[hp][j2][:, 0:Tn], op=ALU.mult),
                         reads=[Bhh[j2], Bsg2[hp][j2]], writes=[Bh[c2]])

            def cons(idx, pb):
                h, j = idx // 4, idx % 4
                hp = h % 2
                if j == 0:
                    P.dma("pool", wgsem[hp], lambda e: e.dma_start(out=wg[hp][:, 0, :, :], in_=a_wga[l, h].rearrange("(ic p) o -> p ic o", p=128)), writes=[Bwg[hp]])
                    P.dma("pool", wgsem[hp], lambda e: e.dma_start(out=wg[hp][:, 1, :, :], in_=a_wgx[l, h].rearrange("(ic p) o -> p ic o", p=128)), writes=[Bwg[hp]])
                if j < 2:
                    c = 2 * h + j
                    P.op("dve", lambda e: e.tensor_copy(out=xc[j][:, 0:3], in_=S[:, 0:3, c]), reads=[BS], writes=[Bxc[j]])
                    P.op("act", lambda e: e.activation(out=xc[j][:, 3:3 + Tn], in_=ps[:, pb, 0:Tn], func=AF.Copy), reads=[Bps[pb]], writes=[Bxc[j]])
                    P.op("dve", lambda e: e.tensor_copy(out=S[:, 0:3, c], in_=xc[j][:, Tn:Tn + 3]), reads=[Bxc[j]], writes=[BS])
                    cwc = V_CW + l * 128
                    P.op("dve", lambda e: e.tensor_scalar(out=xv[hp][j][:, 0:Tn], in0=xc[j][:, 3:3 + Tn], scalar1=vecs[:, cwc + 96 + c:cwc + 97 + c],
                                                          scalar2=vecs[:, V_CB + 32 * l + c:V_CB + 32 * l + c + 1], op0=ALU.mult, op1=ALU.add),
                         reads=[Bxc[j], Bv], writes=[Bxv[hp][j]])
                    for kk in range(3):
                        P.op("dve", lambda e, kk=kk: e.scalar_tensor_tensor(out=xv[hp][j][:, 0:Tn], in0=xc[j][:, kk:kk + Tn],
                                                                            scalar=vecs[:, cwc + 32 * kk + c:cwc + 32 * kk + c + 1],
                                                                            in1=xv[hp][j][:, 0:Tn], op0=ALU.mult, op1=ALU.add),
                             reads=[Bxc[j], Bv, Bxv[hp][j]], writes=[Bxv[hp][j]])
                    P.op("act", lambda e: e.activation(out=xvb[hp][j][:, 0:Tn], in_=xv[hp][j][:, 0:Tn], func=AF.Copy), reads=[Bxv[hp][j]], writes=[Bxvb[hp][j]])
                else:
                    jo = j - 2
                    P.op("act", lambda e: e.activation(out=sg2[hp][jo][:, 0:Tn], in_=ps[:, pb, 0:Tn], func=AF.Silu), reads=[Bps[pb]], writes=[Bsg2[hp][jo]])
                    if jo == 1:
                        if h >= 1:
                            chain(h - 1)
                        if h == 15:
                            chain(15)

            lin(a_w_in[l], 32, cols, ub, Bu, Tn, cons, ("ain", l))
            lin(a_w_out[l], 32, [m * 128 for m in range(32)], hg, Bh, Tn, out_consumer(Tn), ("aout", l))
            post(V_APOST + 32 * l, Tn)

        def kv_phase(row0, Tn, sample):
            norm_pre(V_KVG, Tn, ssq=False)
            ntc = max(1, Tn // 128)
            M = min(128, Tn)
            slots = {i0: wload(w_kv, 32, i0 * 128, ("kv", 0)) for i0 in range(NW - 1)}
            for n in range(24):
                s = slots.pop(n)
                if n + 2 < 24 and n >= 1:
                    pass
                for tc in range(ntc):
                    pb = pctr[0] % 3; pctr[0] += 1
                    for k in range(32):
                        P.op("pe", lambda e, s=s, k=k, pb=pb, tc=tc: e.matmul(ps[0:M, pb, 0:128], lhsT=ub[:, k, tc * 128:tc * 128 + M], rhs=wbf[s][:, k, :],
                                                                              start=(k == 0), stop=(k == 31)),
                             reads=[Bw[s], Bu[k]], writes=[Bps[pb]], inc=(k == 31))
                    P.op("act", lambda e, pb=pb, tc=tc, n=n: e.activation(out=kvt[tc][0:M, n * 128:(n + 1) * 128], in_=ps[0:M, pb, 0:128], func=AF.Copy),
                         reads=[Bps[pb]], writes=[Bkvt[tc]])
                if n + NW - 1 < 24:
                    slots[n + NW - 1] = wload(w_kv, 32, (n + NW - 1) * 128, ("kv", 0))
            if not sample:
                for tc in range(ntc):
                    r0 = row0 + tc * 128
                    P.dma("sp", kvosem, lambda e, tc=tc, r0=r0: e.dma_start(out=pkv[r0:r0 + 128, :], in_=kvt[tc][:, :]), reads=[Bkvt[tc]], writes=[Bout])
                    P.dma("sp", kvosem, lambda e, tc=tc, r0=r0: e.dma_start(out=kvs[r0:r0 + 128, :], in_=kvt[tc][:, :]), reads=[Bkvt[tc]], writes=[Bd2d])

        uctr = [0]

        def key_tile(src_rows_ap, nrows, g, kh):
            i = uctr[0] % 2; uctr[0] += 1
            P.dma("sp", kvinsem[i], lambda e: e.dma_start(out=kvin[i][0:nrows, :, :], in_=src_rows_ap), reads=[Bd2d], writes=[Bkvin[i]])
            tb_ = 4 + i
            P.op("pe", lambda e: e.transpose(out=ps[:, tb_, 0:128], in_=kvin[i][:, 0, :], identity=ident[:, :]), reads=[Bkvin[i], Bid], writes=[Bps[tb_]])
            P.op("act", lambda e: e.activation(out=kTb[i][:, :], in_=ps[:, tb_, 0:128], func=AF.Copy), reads=[Bps[tb_]], writes=[BkT[i]])
            P.op("dve", lambda e: e.tensor_copy(out=vbb[i][:, :], in_=kvin[i][:, 1, :]), reads=[Bkvin[i]], writes=[Bvb[i]])
            return i

        def attn_unit(qap, nq, tiles, acc_first, accap_fn):
            W = 4 * nq
            for ti, (i, eoff, fcol) in enumerate(tiles):
                sb_ = 4 + i
                P.op("pe", lambda e, i=i, sb_=sb_: e.matmul(ps[:, sb_, 0:W].rearrange("p (s q) -> p s q", s=4), lhsT=kTb[i][:, :], rhs=qap, start=True, stop=True),
                     reads=[BkT[i], Bq4], writes=[Bps[sb_]])
                P.op("act", lambda e, i=i, sb_=sb_: e.activation(out=pT[i][:, 0:W], in_=ps[:, sb_, 0:W], func=AF.Exp, scale=SCALE), reads=[Bps[sb_]], writes=[BpT[i]])
                if fcol is None:
                    P.op("dve", lambda e, i=i, eoff=eoff: e.tensor_tensor(out=pT[i][:, 0:W], in0=pT[i][:, 0:W], in1=Eall[:, eoff:eoff + W], op=ALU.mult),
                         reads=[BpT[i], BE], writes=[BpT[i]])
                else:
                    P.op("dve", lambda e, i=i, eoff=eoff, fcol=fcol: e.scalar_tensor_tensor(
                        out=pT[i][:, 0:W], in0=pT[i][:, 0:W], scalar=flg[:, fcol:fcol + 1], in1=Eall[:, eoff:eoff + W], op0=ALU.mult, op1=ALU.mult),
                         reads=[BpT[i], BE, Bflg], writes=[BpT[i]])
                first, last = ti == 0, ti == len(tiles) - 1
                P.op("pe", lambda e, i=i, first=first, last=last: e.matmul(ps[:, 6, 0:W], lhsT=vbb[i][:, :], rhs=pT[i][:, 0:W], start=first, stop=last),
                     reads=[Bvb[i], BpT[i]], writes=[Bps[6]])
                P.op("pe", lambda e, i=i, first=first, last=last: e.matmul(ps[:, 7, 0:W], lhsT=ones[:, :], rhs=pT[i][:, 0:W], start=first, stop=last),
                     reads=[Bones, BpT[i]], writes=[Bps[7]])
            for (acc, Bacc, bank) in ((num, Bnum, 6), (den, Bden, 7)):
                src = ps[:, bank, 0:W].rearrange("p (s q) -> p s q", s=4)
                if acc_first:
                    P.op("act", lambda e, acc=acc, src=src: e.activation(out=accap_fn(acc), in_=src, func=AF.Copy), reads=[Bps[bank]], writes=[Bacc])
                else:
                    P.op("dve", lambda e, acc=acc, src=src: e.tensor_tensor(out=accap_fn(acc), in0=accap_fn(acc), in1=src, op=ALU.add),
                         reads=[Bps[bank], Bacc], writes=[Bacc])

        def attention_prompt(b, g, kh):
            t0 = b * T
            kc = g * 1024 + kh * 128

            def rows(start, step, count):
                v = kvs[start:start + step * count, :].rearrange("(j c) n -> j c n", c=step)[:, 0, :]
                return v[:, g * 1024:(g + 1) * 1024].rearrange("j (v k x) -> j v k x", v=2, k=4)[:, :, kh, :]

            if g == 0:
                for qb in range(T // 128):
                    tiles = []
                    if t0 + qb * 128 - 128 >= 0:
                        i = key_tile(rows(t0 + qb * 128 - 128, 1, 128), 128, g, kh)
                        tiles.append((i, e0_off(1, kh), 0 if (t0 + qb * 128 - 128) < 1024 else None))
                    i = key_tile(rows(t0 + qb * 128, 1, 128), 128, g, kh)
                    tiles.append((i, e0_off(0, kh), None))
                    qap = q4[:, :, qb * 128:(qb + 1) * 128]
                    attn_unit(qap, 128, tiles, True, lambda acc, qb=qb: acc[:, :, qb * 128:(qb + 1) * 128])
            elif g == 1:
                nbi, half = b // 2, b % 2
                for c in range(4):
                    tiles = []
                    if nbi > 0:
                        i = key_tile(rows(512 * (nbi - 1) + c, 4, 128), 128, g, kh)
                        tiles.append((i, e1_off(half, 1, kh), 0 if 512 * (nbi - 1) < 1024 else None))
                    cnt = 64 * (half + 1)
                    i = key_tile(rows(512 * nbi + c, 4, cnt), cnt, g, kh)
                    tiles.append((i, e1_off(half, 0, kh), None))
                    qap = q4[:, :, :].rearrange("p s (j c) -> p s c j", c=4)[:, :, c, :]
                    attn_unit(qap, 64, tiles, False, lambda acc, c=c: acc[:, :, :].rearrange("p s (j c) -> p s c j", c=4)[:, :, c, :])
            else:
                for c in range(16):
                    cnt = 16 * (b + 1)
                    i = key_tile(rows(c, 16, cnt), cnt, g, kh)
                    qap = q4[:, :, :].rearrange("p s (j c) -> p s c j", c=16)[:, :, c, :]
                    attn_unit(qap, 16, [(i, e2_off(b, kh), 1)], False, lambda acc, c=c: acc[:, :, :].rearrange("p s (j c) -> p s c j", c=16)[:, :, c, :])

        def attention_sample(g, kh):
            r = DIL[g]
            lb = 128 * r
            src = skv[g].rearrange("(j c) (v x) -> j c v x", c=r, v=2)[:, 0, :, kh * 128:(kh + 1) * 128]
            i = uctr[0] % 2; uctr[0] += 1
            P.dma("sp", kvinsem[i], lambda e: e.dma_start(out=kvin[i][:, :, :], in_=src), writes=[Bkvin[i]])
            tb_ = 4 + i
            P.op("pe", lambda e: e.transpose(out=ps[:, tb_, 0:128], in_=kvin[i][:, 0, :], identity=ident[:, :]), reads=[Bkvin[i], Bid], writes=[Bps[tb_]])
            P.op("act", lambda e: e.activation(out=kTb[i][:, :], in_=ps[:, tb_, 0:128], func=AF.Copy), reads=[Bps[tb_]], writes=[BkT[i]])
            P.op("dve", lambda e: e.tensor_copy(out=vbb[i][:, :], in_=kvin[i][:, 1, :]), reads=[Bkvin[i]], writes=[Bvb[i]])
            qap = q4[:, :, 0:1]
            P.op("pe", lambda e: e.matmul(ps[:, tb_, 0:4].rearrange("p (s q) -> p s q", s=4), lhsT=kTb[i][:, :], rhs=qap, start=True, stop=True),
                 reads=[BkT[i], Bq4], writes=[Bps[tb_]])
            P.op("act", lambda e: e.activation(out=pT[i][:, 0:4], in_=ps[:, tb_, 0:4], func=AF.Exp, scale=SCALE), reads=[Bps[tb_]], writes=[BpT[i]])
            eo = es_off(g, kh)
            P.op("dve", lambda e: e.tensor_tensor(out=pT[i][:, 0:4], in0=pT[i][:, 0:4], in1=Eall[:, eo:eo + 4], op=ALU.mult), reads=[BpT[i], BE], writes=[BpT[i]])
            P.op("pe", lambda e: e.matmul(ps[:, 6, 0:4], lhsT=vbb[i][:, :], rhs=pT[i][:, 0:4], start=True, stop=False), reads=[Bvb[i], BpT[i]], writes=[Bps[6]])
            P.op("pe", lambda e: e.matmul(ps[:, 7, 0:4], lhsT=ones[:, :], rhs=pT[i][:, 0:4], start=True, stop=False), reads=[Bones, BpT[i]], writes=[Bps[7]])
            kc = g * 1024 + kh * 128
            P.op("pe", lambda e: e.transpose(out=ps[:, tb_, 8:9], in_=kvt[0][0:1, kc:kc + 128], identity=ident[0:1, 0:1]), reads=[Bkvt[0], Bid], writes=[Bps[tb_]])
            P.op("act", lambda e: e.activation(out=knew[:, 0:1], in_=ps[:, tb_, 8:9], func=AF.Copy), reads=[Bps[tb_]], writes=[Bknew])
            P.op("dve", lambda e: e.tensor_copy(out=vnew[0:1, 0:128], in_=kvt[0][0:1, kc + 512:kc + 640]), reads=[Bkvt[0]], writes=[Bvnew])
            P.op("pe", lambda e: e.matmul(ps[0:1, tb_, 16:20].rearrange("p (s q) -> p s q", s=4), lhsT=knew[:, 0:1], rhs=qap, start=True, stop=True),
                 reads=[Bknew, Bq4], writes=[Bps[tb_]])
            P.op("act", lambda e: e.activation(out=misc[0:1, 0:4], in_=ps[0:1, tb_, 16:20], func=AF.Exp, scale=SCALE), reads=[Bps[tb_]], writes=[Bmisc])
            en = (g * 4 + kh) * 4
            P.op("dve", lambda e: e.tensor_tensor(out=pnew[0:1, 0:4], in0=misc[0:1, 0:4], in1=enew[0:1, en:en + 4], op=ALU.mult), reads=[Bmisc, Benew], writes=[Bpnew])
            P.op("pe", lambda e: e.matmul(ps[:, 6, 0:4], lhsT=vnew[0:1, 0:128], rhs=pnew[0:1, 0:4], start=False, stop=True), reads=[Bvnew, Bpnew], writes=[Bps[6]])
            P.op("pe", lambda e: e.matmul(ps[:, 7, 0:4], lhsT=ones[0:1, :], rhs=pnew[0:1, 0:4], start=False, stop=True), reads=[Bones, Bpnew], writes=[Bps[7]])
            for (acc, Bacc, bank) in ((num, Bnum, 6), (den, Bden, 7)):
                src2 = ps[:, bank, 0:4].rearrange("p (s q) -> p s q", s=4)
                if g == 0:
                    P.op("act", lambda e, acc=acc, src2=src2: e.activation(out=acc[:, :, 0:1], in_=src2, func=AF.Copy), reads=[Bps[bank]], writes=[Bacc])
                else:
                    P.op("dve", lambda e, acc=acc, src2=src2: e.tensor_tensor(out=acc[:, :, 0:1], in0=acc[:, :, 0:1], in1=src2, op=ALU.add),
                         reads=[Bps[bank], Bacc], writes=[Bacc])

        def b_layer(l, Tn, b, sample):
            norm_pre(V_BPRE + 32 * l, Tn, ssq=False, rstd=(l == 1))
            cols = []
            for kh in range(4):
                for g in range(3):
                    cols += [g * 2048 + (4 * kh + s) * 128 for s in range(4)]
                cols += [6144 + (4 * kh + s) * 128 for s in range(4)]

            def cons(idx, pb):
                kh, j = idx // 16, idx % 16
                if j < 12:
                    g, s = j // 4, j % 4
                    P.op("act", lambda e: e.activation(out=q4[:, s, 0:Tn], in_=ps[:, pb, 0:Tn], func=AF.Copy), reads=[Bps[pb]], writes=[Bq4])
                    if s == 3:
                        if sample:
                            attention_sample(g, kh)
                        else:
                            attention_prompt(b, g, kh)
                else:
                    s = j - 12
                    if s == 0:
                        P.op("dve", lambda e: e.reciprocal(out=den[:, :, 0:Tn], in_=den[:, :, 0:Tn]), reads=[Bden], writes=[Bden])
                        P.op("dve", lambda e: e.tensor_tensor(out=num[:, :, 0:Tn], in0=num[:, :, 0:Tn], in1=den[:, :, 0:Tn], op=ALU.mult),
                             reads=[Bnum, Bden], writes=[Bnum])
                    P.op("act", lambda e: e.activation(out=sg2[0][0][:, 0:Tn], in_=ps[:, pb, 0:Tn], func=AF.Silu), reads=[Bps[pb]], writes=[Bsg2[0][0]])
                    P.op("dve", lambda e: e.tensor_tensor(out=hg[:, 4 * kh + s, 0:Tn], in0=num[:, s, 0:Tn], in1=sg2[0][0][:, 0:Tn], op=ALU.mult),
                         reads=[Bnum, Bsg2[0][0]], writes=[Bh[4 * kh + s]])

            lin(b_w_in[l], 32, cols, ub, Bu, Tn, cons, ("bin", l))
            lin(b_w_out[l], 16, [m * 128 for m in range(32)], hg, Bh, Tn, out_consumer(Tn), ("bout", l))
            post(V_BPOST + 32 * l, Tn)

        def load_x(src_rows, M, col0):
            P.dma("sp", xiosem, lambda e: e.dma_start(out=xio[0:M, :], in_=src_rows), writes=[Bxio])
            for k in range(32):
                tb_ = 4 + k % 2
                P.op("pe", lambda e, k=k, tb_=tb_: e.transpose(out=ps[:, tb_, 0:M], in_=xio[0:M, k * 128:(k + 1) * 128], identity=ident[0:M, 0:M]),
                     reads=[Bxio, Bid], writes=[Bps[tb_]])
                P.op("act", lambda e, k=k, tb_=tb_: e.activation(out=xT[:, k, col0:col0 + M], in_=ps[:, tb_, 0:M], func=AF.Copy), reads=[Bps[tb_]], writes=[Bx[k]])

        def store_x(dst_rows, M, col0):
            for k in range(32):
                tb_ = 4 + k % 2
                P.op("pe", lambda e, k=k, tb_=tb_: e.transpose(out=ps[0:M, tb_, 0:128], in_=xT[:, k, col0:col0 + M], identity=ident[:, :]),
                     reads=[Bx[k], Bid], writes=[Bps[tb_]])
                P.op("act", lambda e, k=k, tb_=tb_: e.activation(out=xio[0:M, k * 128:(k + 1) * 128], in_=ps[0:M, tb_, 0:128], func=AF.Copy),
                     reads=[Bps[tb_]], writes=[Bxio])
            P.dma("sp", xiosem, lambda e: e.dma_start(out=dst_rows, in_=xio[0:M, :]), reads=[Bxio], writes=[Bout])

        def store_states(conv_o, h_o):
            for l in range(2):
                P.op("pe", lambda e, l=l: e.transpose(out=ps[:, 4, 0:128], in_=stt[l][:, :, :].rearrange("p j c -> p (j c)"), identity=ident[:, :]),
                     reads=[Bst[l], Bid], writes=[Bps[4]])
                P.op("act", lambda e: e.activation(out=stT[:, :], in_=ps[:, 4, 0:128], func=AF.Copy), reads=[Bps[4]], writes=[BstT])
                for j in range(3):
                    P.dma("sp", outsem, lambda e, l=l, j=j: e.dma_start(out=conv_o[l, j].rearrange("(c p) -> c p", p=128), in_=stT[j * 32:(j + 1) * 32, :]),
                          reads=[BstT], writes=[Bout])
                P.dma("sp", outsem, lambda e, l=l: e.dma_start(out=h_o[l].rearrange("(c p) -> c p", p=128), in_=stT[96:128, :]), reads=[BstT], writes=[Bout])

        def load_states():
            for l in range(2):
                for j in range(3):
                    P.dma("sp", setsem[4], lambda e, l=l, j=j: e.dma_start(out=stT[j * 32:(j + 1) * 32, :], in_=sconv[l, j].rearrange("(c p) -> c p", p=128)), writes=[BstT])
                P.dma("sp", setsem[4], lambda e, l=l: e.dma_start(out=stT[96:128, :], in_=sh[l].rearrange("(c p) -> c p", p=128)), writes=[BstT])
                P.op("pe", lambda e, l=l: e.transpose(out=ps[:, 4, 0:128], in_=stT[:, :], identity=ident[:, :]), reads=[BstT, Bid], writes=[Bps[4]])
                P.op("act", lambda e, l=l: e.activation(out=stt[l][:, :, :].rearrange("p j c -> p (j c)"), in_=ps[:, 4, 0:128], func=AF.Copy),
                     reads=[Bps[4]], writes=[Bst[l]])

        for b in range(NB):
            for tc in range(T // 128):
                load_x(xp[b * T + tc * 128:b * T + (tc + 1) * 128, :], 128, tc * 128)
            a_layer(0, T)
            a_layer(1, T)
            kv_phase(b * T, T, False)
            if b == NB // 2 - 1:
                for l in range(2):
                    P.op("dve", lambda e, l=l: e.tensor_scalar(out=stt[l][:, :, :].rearrange("p j c -> p (j c)"), in0=stt[l][:, :, :].rearrange("p j c -> p (j c)"),
                                                               scalar1=flg[:, 0:1], scalar2=None, op0=ALU.mult), reads=[Bst[l], Bflg], writes=[Bst[l]])
            if b >= NB // 2:
                b_layer(0, T, b, False)
                b_layer(1, T, b, False)
                bo = b - NB // 2
                for tc in range(T // 128):
                    store_x(yp[bo * T + tc * 128:bo * T + (tc + 1) * 128, :], 128, tc * 128)
        store_states(pconv, ph)
        load_states()
        load_x(xs[0:1, :], 1, 0)
        a_layer(0, 1)
        a_layer(1, 1)
        kv_phase(0, 1, True)
        for g in range(3):
            lbg = 128 * DIL[g]
            P.dma("sp", d2dsem, lambda e, g=g, lbg=lbg: e.dma_start(out=skv_o[g][0:lbg - 1, :], in_=skv[g][1:lbg, :]), writes=[Bout])
            P.dma("sp", d2dsem, lambda e, g=g, lbg=lbg: e.dma_start(out=skv_o[g][lbg - 1:lbg, :], in_=kvt[0][0:1, g * 1024:(g + 1) * 1024]),
                  reads=[Bkvt[0]], writes=[Bout])
        b_layer(0, 1, 0, True)
        b_layer(1, 1, 0, True)
        store_x(ys[0:1, :], 1, 0)
        store_states(sconv_o, sh_o)
        fin = [(sm, P.dcnt[id(sm)]) for sm in (kvosem, outsem, xiosem, d2dsem) if P.dcnt[id(sm)] > 0]
        P.q["sp"].append((None, fin, None))
        import os
        if os.environ.get("KDEBUG"):
            print("SBUF remaining bytes/partition:", nc.sbuf_bytes_remaining)
        with nc.Block() as block:
            P.replay(block)
    return nc


def _rel_bucket(dist):
    dist = np.asarray(dist)
    d = np.maximum(dist, 1).astype(np.float32)
    large = 16 + (np.log(d / 16) / np.log(2048 / 16) * (32 - 16)).astype(np.int32)
    large = np.minimum(large, 31)
    return np.where(dist < 16, dist, large).astype(np.int32)


def _etiles(rel_bias):
    ein = np.zeros((128, NE), np.float32)
    em = np.zeros((128, NE), np.float32)
    enew = np.zeros((1, 48), np.float32)
    p = np.arange(128)[:, None]

    def fill(off, g, kh, diff, valid):
        nq = diff.shape[1]
        bk = _rel_bucket(np.clip(diff, 0, 128) * DIL[g])
        for s in range(4):
            col = g * 16 + 4 * kh + s
            ein[:, off + s * nq:off + (s + 1) * nq] = rel_bias[bk, col]
            em[:, off + s * nq:off + (s + 1) * nq] = valid.astype(np.float32)

    for kh in range(4):
        q = np.arange(128)[None, :]
        fill(e0_off(0, kh), 0, kh, q - p, (q - p) >= 0)
        fill(e0_off(1, kh), 0, kh, 128 + q - p, (128 + q - p) <= 128)
        q = np.arange(64)[None, :]
        for half in range(2):
            d = 64 * half + q - p
            fill(e1_off(half, 0, kh), 1, kh, d, d >= 0)
            d = 128 + 64 * half + q - p
            fill(e1_off(half, 1, kh), 1, kh, d, d <= 128)
        q = np.arange(16)[None, :]
        for b in range(NB):
            d = 16 * b + q - p
            fill(e2_off(b, kh), 2, kh, d, d >= 0)
        for g in range(3):
            d = 128 - p + np.zeros((1, 1), np.int64)
            fill(es_off(g, kh), g, kh, d, d >= 0)
            for s in range(4):
                enew[0, (g * 4 + kh) * 4 + s] = rel_bias[0, g * 16 + 4 * kh + s]
    return ein, em, enew


def _colvec(v):
    v = np.asarray(v, np.float32).reshape(-1, 32, 128)
    return np.ascontiguousarray(v.transpose(2, 0, 1).reshape(128, -1))


_NC_CACHE = {}


def kernel(x_prompt, x_sample, state_conv, state_h, state_kv_w128, state_kv_w512, state_kv_w2048,
           a_pre_g, a_w_in, a_conv_w, a_conv_b, a_w_gate_a, a_b_gate_a, a_w_gate_x, a_b_gate_x,
           a_lambda, a_w_out, a_post_g, kv_norm_g, w_kv, rel_bias, b_pre_g, b_w_in, b_w_out, b_post_g):
    f = lambda a: np.ascontiguousarray(np.asarray(a, np.float32))
    vecs = np.concatenate([
        _colvec(a_pre_g), _colvec(np.asarray(a_conv_w)), _colvec(a_conv_b),
        _colvec(np.asarray(a_b_gate_a).reshape(2, 4096)), _colvec(np.asarray(a_b_gate_x).reshape(2, 4096)),
        _colvec(a_lambda), _colvec(a_post_g), _colvec(kv_norm_g), _colvec(b_pre_g), _colvec(b_post_g)], axis=1)
    assert vecs.shape == (128, NV)
    ein, em, enew = _etiles(np.asarray(rel_bias, np.float32))
    def tile_w(w):
        w = np.asarray(w, np.float32)
        lead = w.shape[:-2]
        K, M = w.shape[-2:]
        w = w.reshape(lead + (K // 128, 128, M // 128, 128))
        nl = len(lead)
        perm = tuple(range(nl)) + (nl + 2, nl + 1, nl + 0, nl + 3)
        return np.ascontiguousarray(w.transpose(perm)).reshape(lead + (M // 128, 128, K))

    shared = dict(a_w_in=tile_w(a_w_in), a_w_out=tile_w(a_w_out), a_wga=f(a_w_gate_a), a_wgx=f(a_w_gate_x), w_kv=tile_w(w_kv),
                  b_w_in=tile_w(b_w_in), b_w_out=tile_w(b_w_out), vecs=f(vecs), ein=ein, em=em, enew=enew,
                  ident=np.eye(128, dtype=np.float32))
    xp = f(x_prompt); xs = f(x_sample); sc = f(state_conv); shh = f(state_h)
    k0 = f(state_kv_w128).reshape(8, 128, 1024); k1 = f(state_kv_w512).reshape(8, 512, 1024); k2 = f(state_kv_w2048).reshape(8, 2048, 1024)
    in_maps = []
    for c in range(8):
        m = dict(shared)
        sq_, hf = c // 2, c % 2
        xpc = xp[sq_] if hf == 1 else np.concatenate([np.zeros((1024, 4096), np.float32), xp[sq_][:1024]], axis=0)
        fl = np.zeros((128, 2), np.float32); fl[:, 0] = hf; fl[:, 1] = 1.0; fl[:64, 1] = hf
        m.update(xp=xpc, flg=fl, xs=xs[c], sconv=np.ascontiguousarray(sc[:, c]), sh=np.ascontiguousarray(shh[:, c]),
                 skv0=k0[c], skv1=k1[c], skv2=k2[c])
        in_maps.append(m)
    if "nc" not in _NC_CACHE:
        _NC_CACHE["nc"] = build()
    res = run_bass_kernel_spmd(_NC_CACHE["nc"], in_maps, core_ids=list(range(8)))
    R = res.results
    y_prompt = np.stack([np.concatenate([R[2 * q]["yp"], R[2 * q + 1]["yp"]], axis=0) for q in range(4)])
    y_sample = np.stack([R[c]["ys"] for c in range(8)])
    p_conv = np.stack([R[2 * q + 1]["pconv"] for q in range(4)], axis=1)
    p_h = np.stack([R[2 * q + 1]["ph"] for q in range(4)], axis=1)
    pkv = np.stack([R[2 * q + 1]["pkv"] for q in range(4)]).reshape(4, 2048, 3, 2, 4, 128)
    p_kv128 = np.ascontiguousarray(pkv[:, 2048 - 128:, 0])
    p_kv512 = np.ascontiguousarray(pkv[:, 2048 - 512:, 1])
    p_kv2048 = np.ascontiguousarray(pkv[:, :, 2])
    s_conv = np.stack([R[c]["sconv_o"] for c in range(8)], axis=1)
    s_h = np.stack([R[c]["sh_o"] for c in range(8)], axis=1)
    s_kv128 = np.stack([R[c]["skv_o0"] for c in range(8)]).reshape(8, 128, 2, 4, 128)
    s_kv512 = np.stack([R[c]["skv_o1"] for c in range(8)]).reshape(8, 512, 2, 4, 128)
    s_kv2048 = np.stack([R[c]["skv_o2"] for c in range(8)]).reshape(8, 2048, 2, 4, 128)
    f32 = lambda a: np.asarray(a, np.float32)
    return tuple(f32(a) for a in (y_prompt, y_sample, p_conv, p_h, p_kv128, p_kv512, p_kv2048, s_conv, s_h, s_kv128, s_kv512, s_kv2048))
```
